# Optimizing a Trainium2 kernel written in Bass

```python
import jax, jax.numpy as jnp
from jax import lax
import numpy as np

D_MODEL = 1024
BATCH = 1
SEQ = 16384
DEPTH = 1
DEC_BATCH = 128
DEC_SEQ = 8
PAST_LEN = 16384
PAGE_SIZE = 128

N_Q_HEADS = 8
N_KV_HEADS = 2
GROUP = N_Q_HEADS // N_KV_HEADS
HEAD_DIM = 64
ATTN_Q = N_Q_HEADS * HEAD_DIM
ATTN_KV = N_KV_HEADS * HEAD_DIM
WINDOW = 128
ROPE_THETA = 10000.0
CHUNK = 128
SG_GROUPS = 4
SG_GROUP_DIM = 128
SG_WIDTH = SG_GROUPS * SG_GROUP_DIM
N_MEM = 256
MEM_HEADS = 4
MEM_HEAD_DIM = 128
MEM_Q = MEM_HEADS * MEM_HEAD_DIM
N_BRANCHES = 3
BRANCH_WIDTH = 512
D_FF = 2816
CONV_WIDTH = 3
EPS = 1e-6
NEG_INF = -1e30
IN_SPLITS = (ATTN_Q, ATTN_KV, ATTN_KV, SG_WIDTH, SG_WIDTH, MEM_Q, N_BRANCHES * D_MODEL)
IN_WIDTH = sum(IN_SPLITS)
IN_OFFSETS = tuple(int(o) for o in np.cumsum(IN_SPLITS)[:-1])

kernel_name = 'gated_parallel_swa_sgmlp_memory_decoder_step'


def rms_norm(x, g):
    xf = x.astype(jnp.float32)
    y = xf * lax.rsqrt(jnp.mean(jnp.square(xf), axis=-1, keepdims=True) + EPS)
    return (y * g.astype(jnp.float32)).astype(x.dtype)


def layer_norm(x, g, b):
    xf = x.astype(jnp.float32)
    xc = xf - jnp.mean(xf, axis=-1, keepdims=True)
    var = jnp.mean(jnp.square(xc), axis=-1, keepdims=True)
    return (xc * lax.rsqrt(var + EPS) * g.astype(jnp.float32) + b.astype(jnp.float32)).astype(x.dtype)


def rope(x, pos):
    half = x.shape[-1] // 2
    inv_freq = ROPE_THETA ** (-jnp.arange(half, dtype=jnp.float32) / half)
    ang = pos.astype(jnp.float32)[:, None] * inv_freq[None, :]
    cos = jnp.cos(ang)[:, None, :]
    sin = jnp.sin(ang)[:, None, :]
    xf = x.astype(jnp.float32)
    x1, x2 = xf[..., :half], xf[..., half:]
    return jnp.concatenate([x1 * cos - x2 * sin, x2 * cos + x1 * sin], axis=-1).astype(x.dtype)


def sink_attention(q, k, v, mask, sinks):
    s = jnp.einsum('...qhgd,...khd->...hgqk', q, k, preferred_element_type=jnp.float32) * (HEAD_DIM ** -0.5)
    s = jnp.where(mask, s, NEG_INF)
    sink = jnp.broadcast_to(sinks.astype(jnp.float32).reshape(N_KV_HEADS, GROUP, 1, 1), s.shape[:-1] + (1,))
    p = jax.nn.softmax(jnp.concatenate([s, sink], axis=-1), axis=-1)[..., :-1]
    return jnp.einsum('...hgqk,...khd->...qhgd', p.astype(v.dtype), v)


def swa_band(q, k, v, sinks):
    B, L = q.shape[:2]
    blk = WINDOW
    nb = L // blk
    qb = q.reshape(B, nb, blk, N_KV_HEADS, GROUP, HEAD_DIM)

    def band(t):
        tb = jnp.pad(t, ((0, 0), (blk, 0), (0, 0), (0, 0))).reshape(B, nb + 1, blk, N_KV_HEADS, HEAD_DIM)
        return jnp.concatenate([tb[:, :-1], tb[:, 1:]], axis=2)

    kb, vb = band(k), band(v)
    qi = jnp.arange(blk)[:, None] + blk
    kj = jnp.arange(2 * blk)[None, :]
    rel = qi - kj
    in_window = (rel >= 0) & (rel < WINDOW)
    real_key = (jnp.arange(nb)[:, None, None] * blk + kj[None] - blk) >= 0
    mask = (in_window[None] & real_key)[:, None, None]
    o = sink_attention(qb, kb, vb, mask, sinks)
    return o.reshape(B, L, ATTN_Q)


def spatial_gating(u, vn, sg_w, sg_b, chunk):
    B, L, _ = u.shape
    nc = L // chunk
    u5 = u.reshape(B, nc, chunk, SG_GROUPS, SG_GROUP_DIM)
    v5 = vn.reshape(B, nc, chunk, SG_GROUPS, SG_GROUP_DIM)
    w = jnp.tril(sg_w[:, :chunk, :chunk])
    mixed = jnp.einsum('gts,bnsgc->bntgc', w, v5) + sg_b[:, :chunk].T[:, :, None]
    return (u5 * mixed).reshape(B, L, SG_WIDTH)


def memory_kv(mem, g, w_mem_kv):
    B = mem.shape[0]
    kv = jnp.einsum('bmd,de->bme', rms_norm(mem, g), w_mem_kv)
    k, v = jnp.split(kv, 2, axis=-1)
    return (k.reshape(B, -1, MEM_HEADS, MEM_HEAD_DIM), v.reshape(B, -1, MEM_HEADS, MEM_HEAD_DIM))


def mem_attention(q, mk, mv):
    s = jnp.einsum('blhd,bmhd->bhlm', q, mk, preferred_element_type=jnp.float32) * (MEM_HEAD_DIM ** -0.5)
    p = jax.nn.softmax(s, axis=-1)
    return jnp.einsum('bhlm,bmhd->blhd', p.astype(mv.dtype), mv)


def conv_ffn(h, conv_past, w_up, conv_w, conv_b, w_down):
    L = h.shape[1]
    up = jnp.einsum('bld,df->blf', h, w_up)
    ext = jnp.concatenate([conv_past.astype(up.dtype), up], axis=1)
    c = conv_b
    for j in range(CONV_WIDTH):
        c = c + ext[:, j:j + L] * conv_w[j]
    gate, val = jnp.split(c, 2, axis=-1)
    act = jax.nn.gelu(gate, approximate=True) * val
    return jnp.einsum('blf,fd->bld', act, w_down), ext[:, L:]


def decoder_layer(x, start, win_k, win_v, mem_k, mem_v, conv_past, lp):
    B, L, _ = x.shape
    pos = start + jnp.arange(L, dtype=jnp.int32)
    h = rms_norm(x, lp['pre_mix_g'])
    z = jnp.einsum('bld,de->ble', h, lp['w_in'])
    q, k, v, sg_u, sg_v, mq, gate_logits = jnp.split(z, IN_OFFSETS, axis=-1)
    q = rope(q.reshape(B, L, N_Q_HEADS, HEAD_DIM), pos).reshape(B, L, N_KV_HEADS, GROUP, HEAD_DIM)
    k = rope(k.reshape(B, L, N_KV_HEADS, HEAD_DIM), pos)
    v = v.reshape(B, L, N_KV_HEADS, HEAD_DIM)
    if win_k is None:
        attn = swa_band(q, k, v, lp['sinks'])
        wb = min(WINDOW, L)
        new_wk, new_wv = k[:, L - wb:], v[:, L - wb:]
        chunk = CHUNK
        conv_past = jnp.zeros((B, CONV_WIDTH - 1, 2 * D_FF), x.dtype)
    else:
        wb = win_k.shape[1]
        kk = jnp.concatenate([win_k.astype(k.dtype), k], axis=1)
        vv = jnp.concatenate([win_v.astype(v.dtype), v], axis=1)
        kpos = start - wb + jnp.arange(wb + L, dtype=jnp.int32)
        rel = pos[:, None] - kpos[None, :]
        mask = (rel >= 0) & (rel < WINDOW)
        attn = sink_attention(q, kk, vv, mask, lp['sinks']).reshape(B, L, ATTN_Q)
        new_wk, new_wv = kk[:, L:], vv[:, L:]
        chunk = L
    u = jax.nn.gelu(sg_u, approximate=False)
    vn = layer_norm(jax.nn.gelu(sg_v, approximate=False), lp['sg_ln_g'], lp['sg_ln_b'])
    sg = spatial_gating(u, vn, lp['sg_w'], lp['sg_b'], chunk)
    memo = mem_attention(mq.reshape(B, L, MEM_HEADS, MEM_HEAD_DIM), mem_k.astype(x.dtype), mem_v.astype(x.dtype))
    branches = jnp.stack([attn, sg, memo.reshape(B, L, MEM_Q)], axis=2)
    proj = jnp.einsum('blnc,ncd->blnd', branches, lp['w_o'])
    gates = jax.nn.sigmoid(gate_logits.reshape(B, L, N_BRANCHES, D_MODEL))
    mixed = jnp.sum(gates * proj, axis=2)
    x = x + rms_norm(mixed, lp['post_mix_g'])
    f, new_conv = conv_ffn(rms_norm(x, lp['pre_ffn_g']), conv_past, lp['w_up'], lp['conv_w'], lp['conv_b'], lp['w_down'])
    x = x + rms_norm(f, lp['post_ffn_g'])
    return x, new_wk, new_wv, vn, new_conv


def setup_inputs(seed: int = 0) -> dict:
    key = jax.random.key(seed)
    ks = jax.random.split(key, 32)
    f32 = jnp.float32
    wb = min(WINDOW, PAST_LEN)

    def nrm(k, shape, scale=1.0):
        return jax.random.normal(k, shape, f32) * scale

    def gain(k, shape):
        return 1.0 + 0.05 * jax.random.normal(k, shape, f32)

    return {
        'x_prompt': nrm(ks[0], (BATCH, SEQ, D_MODEL)),
        'x_sample': nrm(ks[1], (DEC_BATCH, DEC_SEQ, D_MODEL)),
        'cache_win_k': nrm(ks[2], (DEPTH, DEC_BATCH, wb, N_KV_HEADS, HEAD_DIM)),
        'cache_win_v': nrm(ks[3], (DEPTH, DEC_BATCH, wb, N_KV_HEADS, HEAD_DIM)),
        'cache_mem_k': nrm(ks[4], (DEPTH, DEC_BATCH, N_MEM, MEM_HEADS, MEM_HEAD_DIM)),
        'cache_mem_v': nrm(ks[5], (DEPTH, DEC_BATCH, N_MEM, MEM_HEADS, MEM_HEAD_DIM)),
        'state_conv': nrm(ks[6], (DEPTH, DEC_BATCH, CONV_WIDTH - 1, 2 * D_FF)),
        'mem_prompt': nrm(ks[7], (BATCH, N_MEM, D_MODEL)),
        'pre_mix_g': gain(ks[8], (DEPTH, D_MODEL)),
        'w_in': nrm(ks[9], (DEPTH, D_MODEL, IN_WIDTH), D_MODEL ** -0.5),
        'attn_sinks': nrm(ks[10], (DEPTH, N_Q_HEADS), 0.5),
        'sg_ln_g': gain(ks[11], (DEPTH, SG_WIDTH)),
        'sg_ln_b': nrm(ks[12], (DEPTH, SG_WIDTH), 0.02),
        'sg_w': nrm(ks[13], (DEPTH, SG_GROUPS, CHUNK, CHUNK), CHUNK ** -0.5),
        'sg_b': 1.0 + nrm(ks[14], (DEPTH, SG_GROUPS, CHUNK), 0.1),
        'mem_norm_g': gain(ks[15], (DEPTH, D_MODEL)),
        'w_mem_kv': nrm(ks[16], (DEPTH, D_MODEL, 2 * MEM_Q), D_MODEL ** -0.5),
        'w_o': nrm(ks[17], (DEPTH, N_BRANCHES, BRANCH_WIDTH, D_MODEL), BRANCH_WIDTH ** -0.5),
        'post_mix_g': gain(ks[18], (DEPTH, D_MODEL)),
        'pre_ffn_g': gain(ks[19], (DEPTH, D_MODEL)),
        'w_up': nrm(ks[20], (DEPTH, D_MODEL, 2 * D_FF), D_MODEL ** -0.5),
        'conv_w': nrm(ks[21], (DEPTH, CONV_WIDTH, 2 * D_FF), CONV_WIDTH ** -0.5),
        'conv_b': nrm(ks[22], (DEPTH, 2 * D_FF), 0.01),
        'w_down': nrm(ks[23], (DEPTH, D_FF, D_MODEL), D_FF ** -0.5),
        'post_ffn_g': gain(ks[24], (DEPTH, D_MODEL)),
    }


def reference(x_prompt, x_sample, cache_win_k, cache_win_v, cache_mem_k, cache_mem_v, state_conv, mem_prompt,
              pre_mix_g, w_in, attn_sinks, sg_ln_g, sg_ln_b, sg_w, sg_b, mem_norm_g, w_mem_kv, w_o,
              post_mix_g, pre_ffn_g, w_up, conv_w, conv_b, w_down, post_ffn_g):
    y_p, y_s = x_prompt, x_sample
    wk_p, wv_p, mk_p, mv_p, cv_p = [], [], [], [], []
    wk_s, wv_s, sgv_s, cv_s = [], [], [], []
    for l in range(DEPTH):
        lp = {
            'pre_mix_g': pre_mix_g[l], 'w_in': w_in[l], 'sinks': attn_sinks[l],
            'sg_ln_g': sg_ln_g[l], 'sg_ln_b': sg_ln_b[l], 'sg_w': sg_w[l], 'sg_b': sg_b[l],
            'w_o': w_o[l], 'post_mix_g': post_mix_g[l], 'pre_ffn_g': pre_ffn_g[l],
            'w_up': w_up[l], 'conv_w': conv_w[l], 'conv_b': conv_b[l], 'w_down': w_down[l],
            'post_ffn_g': post_ffn_g[l],
        }
        mem_k_l, mem_v_l = memory_kv(mem_prompt, mem_norm_g[l], w_mem_kv[l])
        y_p, a_k, a_v, _, a_c = decoder_layer(y_p, 0, None, None, mem_k_l, mem_v_l, None, lp)
        wk_p.append(a_k); wv_p.append(a_v); mk_p.append(mem_k_l); mv_p.append(mem_v_l); cv_p.append(a_c)
        y_s, b_k, b_v, b_sg, b_c = decoder_layer(y_s, PAST_LEN, cache_win_k[l], cache_win_v[l],
                                                 cache_mem_k[l], cache_mem_v[l], state_conv[l], lp)
        wk_s.append(b_k); wv_s.append(b_v); sgv_s.append(b_sg); cv_s.append(b_c)
    return (y_p, y_s,
            jnp.stack(wk_p), jnp.stack(wv_p), jnp.stack(mk_p), jnp.stack(mv_p), jnp.stack(cv_p),
            jnp.stack(wk_s), jnp.stack(wv_s), jnp.stack(sgv_s), jnp.stack(cv_s))
```

```python
import numpy as np
import concourse.bass as bass
import concourse.mybir as mybir
from concourse.bass_utils import run_bass_kernel_spmd

F32 = mybir.dt.float32
BF16 = mybir.dt.bfloat16
AF = mybir.ActivationFunctionType
ALU = mybir.AluOpType
AX = mybir.AxisListType

NCORES = 8
D = 1024
SEQ = 16384
TOK = SEQ // NCORES
NOWN = TOK // 128
NPB = NOWN + 2
SEQS = 16
INW = 5376
DFF = 2816
NFC = 22
EPS = 1e-6
NEG = -30000.0
PAST = 16384
ZG = [(0, 512), (512, 256), (768, 512), (1280, 512), (1792, 512)] + [(2304 + 512 * i, 512) for i in range(6)]


class Sched:
    ENGS = ("pe", "act", "dve", "pool", "sp")

    def __init__(self, nc):
        self.nc = nc
        self.q = {e: [] for e in self.ENGS}
        self.sem = {}
        self.cnt = {}
        self.waited = {e: {} for e in self.ENGS}
        self.lastw = {}
        self.readers = {}
        for e in ("pe", "act", "dve", "pool"):
            self._mk(e)

    def _mk(self, name):
        self.sem[name] = self.nc.alloc_semaphore("s_" + name)
        self.cnt[name] = 0

    def op(self, eng, fn, r=(), w=(), dma=None, after=()):
        deps = {}

        def need(tok):
            if tok is None:
                return
            s, v = tok
            if deps.get(s, 0) < v:
                deps[s] = v

        w = list(w) + [x for x in r if len(x) == 2 and x[0] == "b" and x[1].isdigit()]
        r = [x for x in r if not (len(x) == 2 and x[0] == "b" and x[1].isdigit())]
        for x in r:
            need(self.lastw.get(x))
        for x in w:
            need(self.lastw.get(x))
            for t in self.readers.get(x, ()):
                need(t)
        for t in after:
            need(t)
        waits = []
        for s, v in deps.items():
            if s == "pe" and eng == "pe":
                continue
            if self.waited[eng].get(s, 0) >= v:
                continue
            self.waited[eng][s] = v
            waits.append((s, v))
        if dma is not None:
            if dma not in self.sem:
                self._mk(dma)
            stream, inc = dma, 16
        else:
            stream, inc = eng, 1
        self.cnt[stream] += inc
        tok = (stream, self.cnt[stream])
        self.q[eng].append((waits, fn, stream, inc))
        for x in w:
            self.lastw[x] = tok
            self.readers[x] = []
        for x in r:
            self.readers.setdefault(x, []).append(tok)
        return tok

    def barrier(self, keep_streams=(), keep_res=()):
        for e in self.ENGS:
            waits = []
            for s, v in self.cnt.items():
                if s.startswith(tuple(keep_streams)) if keep_streams else False:
                    continue
                if v > self.waited[e].get(s, 0):
                    self.waited[e][s] = v
                    waits.append((s, v))
            self.q[e].append((waits, None, None, 0))
        self.lastw = {k: v for k, v in self.lastw.items() if keep_res and k.startswith(tuple(keep_res))}
        self.readers = {}

    def emit(self):
        nc = self.nc
        handles = {"pe": "tensor", "act": "scalar", "dve": "vector", "pool": "gpsimd", "sp": "sync"}
        with nc.Block() as block:
            for en in self.ENGS:
                def make(en):
                    def f(eng):
                        for waits, fn, stream, inc in self.q[en]:
                            for s, v in waits:
                                eng.wait_ge(self.sem[s], v)
                            if fn is None:
                                continue
                            ins = fn(eng)
                            ins.then_inc(self.sem[stream], inc)
                    return f
                getattr(block, handles[en])(make(en))


class Arena:
    def __init__(self, nc, nbytes):
        self.t = nc.alloc_sbuf_tensor("arena", [128, nbytes // 4], F32)
        self.cap = nbytes
        self.off = 0

    def carve(self, shape_free, dtype):
        n = int(np.prod(shape_free))
        nb = n * (2 if dtype == BF16 else 4)
        nb = (nb + 31) // 32 * 32
        assert self.off + nb <= self.cap, ("SBUF arena overflow", self.off, nb, self.cap)
        ap = self.t[:, self.off // 4:(self.off + nb) // 4]
        self.off += nb
        if dtype == BF16:
            ap = ap.bitcast(BF16)
        ap = ap[:, 0:n]
        if len(shape_free) == 2:
            ap = ap.rearrange("p (a b) -> p a b", a=shape_free[0])
        elif len(shape_free) == 3:
            ap = ap.rearrange("p (a b c) -> p a b c", a=shape_free[0], b=shape_free[1])
        elif len(shape_free) == 4:
            ap = ap.rearrange("p (a b c d) -> p a b c d", a=shape_free[0], b=shape_free[1], c=shape_free[2])
        return ap


def flat(ap):
    n = len(ap.shape)
    if n == 2:
        return ap
    if n == 3:
        return ap.rearrange("p a b -> p (a b)")
    if n == 4:
        return ap.rearrange("p a b c -> p (a b c)")
    return ap.rearrange("p a b c d -> p (a b c d)")


def build_program():
    nc = bass.Bass("TRN2", target_bir_lowering=False)
    S = Sched(nc)

    def din(name, shape):
        return nc.dram_tensor(name, list(shape), F32, kind="ExternalInput").ap()

    def dout(name, shape):
        return nc.dram_tensor(name, list(shape), F32, kind="ExternalOutput").ap()

    xp = din("xp", [NPB * 128, D])
    xs = din("xs", [128, D])
    cwk = din("cwk", [SEQS, 128, 128])
    cwv = din("cwv", [SEQS, 128, 128])
    cmk = din("cmk", [SEQS, 256, 512])
    cmv = din("cmv", [SEQS, 256, 512])
    scv = din("scv", [32, 2 * DFF])
    memp = din("memp", [256, D])
    w_in = din("w_in", [D, INW])
    w_mkv = din("w_mkv", [D, 1024])
    w_o = din("w_o", [1536, D])
    w_up = din("w_up", [D, 2 * DFF])
    w_down = din("w_down", [DFF, D])
    g_pre = din("g_pre", [1, D])
    g_post = din("g_post", [1, D])
    g_pffn = din("g_pffn", [1, D])
    g_qffn = din("g_qffn", [1, D])
    g_mem = din("g_mem", [1, D])
    ln_g = din("ln_g", [1, 512])
    ln_b = din("ln_b", [1, 512])
    sinks = din("sinks", [1, 8])
    sg_w = din("sg_w", [4, 128, 128])
    sg_b = din("sg_b", [4, 128])
    conv_w = din("conv_w", [3, 2 * DFF])
    conv_b = din("conv_b", [1, 2 * DFF])
    ident = din("ident", [128, 128])
    ropec = din("ropec", [NPB + 1, 128, 64])
    ropes = din("ropes", [NPB + 1, 128, 64])
    maskp_d = din("maskp", [128, 256])
    maskf_d = din("maskf", [128, 256])
    maskp8_d = din("maskp8", [128, 256])
    maskf8_d = din("maskf8", [128, 256])
    masks_d = din("masks", [128, 256])
    tril_d = din("tril", [128, 128])
    bdm_d = din("bdm", [128, 128])
    e8_d = din("e8", [8, 128])
    hv_d = din("hv", [128, 1])

    yp = dout("yp", [TOK, D])
    ys = dout("ys", [128, D])
    wkp = dout("wkp", [128, 128])
    wvp = dout("wvp", [128, 128])
    mkp = dout("mkp", [256, 512])
    mvp = dout("mvp", [256, 512])
    cvp = dout("cvp", [88, 128])
    wks = dout("wks", [SEQS, 128, 128])
    wvs = dout("wvs", [SEQS, 128, 128])
    sgv = dout("sgv", [128, 512])
    cvs = dout("cvs", [1408, 128])
    x1s = nc.dram_tensor("x1s", [(NOWN + 2) * 128, D], F32).ap()

    A = Arena(nc, 212480)
    PS = nc.alloc_psum_tensor("PS", [128, 4096], F32)

    def bank(i):
        return PS[:, i * 512:(i + 1) * 512]

    def bankb(i):
        return bank(i).bitcast(BF16)

    rot = {"i": 0, "n": 6}

    def nb():
        i = rot["i"] % rot["n"]
        rot["i"] = (i + 1) % rot["n"]
        return i

    out_toks = []

    def dma(eng, out, in_, r=(), w=(), stream=None, after=()):
        return S.op(eng, lambda e: e.dma_start(out=out, in_=in_), r=r, w=w, dma=stream, after=after)

    import os
    KSTOP = int(os.environ.get("KSTOP", "99"))

    def finish():
        fin = {}
        for s_, v in S.cnt.items():
            if s_.startswith("d_"):
                fin[s_] = v
        S.q["sp"].append(([(s_, v) for s_, v in fin.items() if v > S.waited["sp"].get(s_, 0)], None, None, 0))
        S.emit()
        return nc

    idf = A.carve([128], F32)
    idb = A.carve([128], BF16)
    negh = A.carve([1], F32)
    hvt = A.carve([1], F32)
    carry = A.carve([2, 44], F32)
    cwt = A.carve([4, 44], F32)
    maskp8 = A.carve([256], BF16)
    maskf8 = A.carve([256], BF16)
    WA_OFF = A.off
    WA = A.carve([8, 5632], BF16)
    WB_OFF = A.off
    WB = A.carve([22, 1024], BF16)
    WIN = flat(WA)[:, 0:8 * INW].rearrange("p (k e) -> p k e", k=8)
    WO = WB[:, 0:12, :]
    WMKV = WB[:, 12:20, :]
    WUP = WA
    WDN = WB
    P_MARK = A.off

    dma("sp", idf, ident, w=["idf"], stream="d_c0")
    dma("pool", idb, ident, w=["idb"], stream="d_c1")
    dma("sp", hvt, hv_d, w=["hvt"], stream="d_c2")
    S.op("dve", lambda e: e.memset(negh, -0.5), w=["negh"])

    dma("pool", maskp8, maskp8_d, w=["maskp8"], stream="d_c24")
    dma("pool", maskf8, maskf8_d, w=["maskf8"], stream="d_c25")
    w_in_r = w_in.rearrange("(k p) e -> p k e", p=128)
    w_mkv_r = w_mkv.rearrange("(k p) e -> p k e", p=128)
    w_o_r = w_o.rearrange("(k p) e -> p k e", p=128)
    for half in range(2):
        dma("pool", WMKV[:, :, half * 512:(half + 1) * 512], w_mkv_r[:, :, half * 512:(half + 1) * 512],
            w=["wmkv%d" % half], stream="d_WB%d" % (12 + half))
    for ci, (c0, cw) in enumerate(ZG):
        dma("pool", WIN[:, :, c0:c0 + cw], w_in_r[:, :, c0:c0 + cw], w=["win%d" % ci], stream="d_WA%d" % ci)
    for br in range(3):
        dma("pool", WO[:, br * 4:(br + 1) * 4, :], w_o_r[:, br * 4:(br + 1) * 4, :], w=["wo%d" % br], stream="d_WB%d" % br)

    def rstd_from(ms, out, eps, tag):
        S.op("pool", lambda e: e.tensor_scalar(out=out, in0=ms, scalar1=eps, scalar2=None, op0=ALU.add), r=[tag + "ms"], w=[tag + "rs"])
        S.op("pool", lambda e: e.tensor_tensor(out=out, in0=out, in1=negh, op=ALU.pow), r=[tag + "rs", "negh"], w=[tag + "rs"])

    def transposes(srcs, src_res, dst, dst_res, dt, evac_eng="act"):
        b = nb()
        n = len(srcs)
        pv = bankb(b) if dt == BF16 else bank(b)
        idt = idb if dt == BF16 else idf

        def f(e):
            ins = None
            for i, s in enumerate(srcs):
                ins = e.transpose(out=pv[:, i * 128:(i + 1) * 128], in_=s, identity=idt)
            return ins
        S.op("pe", f, r=list(src_res) + ["idb", "idf"], w=["b%d" % b])
        src = pv[:, 0:n * 128]
        if len(dst.shape) == 3:
            src = src.rearrange("p (a b) -> p a b", a=dst.shape[1])
        if evac_eng == "act":
            S.op("act", lambda e: e.copy(out=dst, in_=src), r=["b%d" % b], w=list(dst_res))
        else:
            S.op(evac_eng, lambda e: e.tensor_copy(out=dst, in_=src), r=["b%d" % b], w=list(dst_res))

    gpre = A.carve([D], F32)
    gpost = A.carve([D], F32)
    lng = A.carve([512], F32)
    lnb = A.carve([512], F32)
    snk = A.carve([8], F32)
    maskp = A.carve([256], F32)
    maskf = A.carve([256], F32)
    masks = A.carve([256], F32)
    nsnk = A.carve([8], F32)
    WT = A.carve([4, 128], BF16)
    WTS = A.carve([4, 128], BF16)
    sgbT = A.carve([4], F32)
    sgbS = A.carve([4], F32)
    mkT = A.carve([4, 256], BF16)
    mvb = A.carve([2, 512], BF16)
    P1_MARK = A.off

    for t, src, nm, st in [(gpre, g_pre, "gpre", 3), (gpost, g_post, "gpost", 4), (lng, ln_g, "lng", 5), (lnb, ln_b, "lnb", 6), (snk, sinks, "snk", 7)]:
        dma("sp", t, src.partition_broadcast(128), w=[nm], stream="d_c%d" % st)
    dma("sp", maskp, maskp_d, w=["maskp"], stream="d_c8")
    dma("sp", maskf, maskf_d, w=["maskf"], stream="d_c9")
    dma("sp", masks, masks_d, w=["masks"], stream="d_c10")
    S.op("dve", lambda e: e.tensor_scalar(out=nsnk, in0=snk, scalar1=-1.0, scalar2=None, op0=ALU.mult), r=["snk"], w=["nsnk"])

    gmem = A.carve([D], F32)
    mx0 = A.carve([D], F32)
    mx1 = A.carve([D], F32)
    mnb = A.carve([2, D], BF16)
    mnT = A.carve([8, 256], BF16)
    junk0 = A.carve([D], BF16)
    st0 = A.carve([8], F32)
    trilt = A.carve([128], F32)
    bdmt = A.carve([128], F32)
    e8t = A.carve([128], F32)
    wraw = A.carve([4, 128], F32)
    wmsk = A.carve([4, 128], BF16)
    w8 = A.carve([4, 8], F32)
    r8 = A.carve([4, 16, 8], F32)
    wrep = A.carve([4, 128], BF16)
    sgbr = A.carve([128], F32)
    mo = A.carve([2, 512], F32)
    mo2 = A.carve([2, 512], F32)
    cinp = [A.carve([128], F32) for _ in range(2)]

    dma("sp", gmem, g_mem.partition_broadcast(128), w=["gmem"], stream="d_c11")
    dma("sp", mx0, memp[0:128, :], w=["mx0"], stream="d_c12")
    dma("sp", mx1, memp[128:256, :], w=["mx1"], stream="d_c13")
    dma("sp", trilt, tril_d, w=["trilt"], stream="d_c14")
    dma("sp", bdmt, bdm_d, w=["bdmt"], stream="d_c15")
    dma("sp", e8t[0:8, :], e8_d, w=["e8t"], stream="d_c16")
    dma("sp", wraw, sg_w.rearrange("g t s -> t g s"), w=["wraw"], stream="d_c17")
    dma("sp", w8[0:8, :, :], sg_w[:, 0:8, 0:8].rearrange("g t s -> t g s"), w=["w8"], stream="d_c18")
    dma("sp", sgbr[0:4, :], sg_b, w=["sgbr"], stream="d_c19")

    if KSTOP == -1:
        return finish()
    for mb, mx in enumerate((mx0, mx1)):
        S.op("act", lambda e, mx=mx, mb=mb: e.activation(out=junk0, in_=mx, func=AF.Square, scale=1.0 / 32.0, accum_out=st0[:, mb:mb + 1]),
             r=["mx%d" % mb], w=["junk0", "m%dms" % mb])
        rstd_from(st0[:, mb:mb + 1], st0[:, 2 + mb:3 + mb], EPS, "m%d" % mb)
        S.op("dve", lambda e, mx=mx, mb=mb: e.scalar_tensor_tensor(out=mnb[:, mb, :], in0=mx, scalar=st0[:, 2 + mb:3 + mb], in1=gmem, op0=ALU.mult, op1=ALU.mult),
             r=["mx%d" % mb, "m%drs" % mb, "gmem"], w=["mnb%d" % mb])
        transposes([mnb[:, mb, k * 128:(k + 1) * 128] for k in range(8)], ["mnb%d" % mb],
                   mnT[:, :, mb * 128:(mb + 1) * 128], ["mnT%d" % mb], BF16)
    for mb in range(2):
        for kv in range(2):
            b = nb()

            def f(e, mb=mb, kv=kv, b=b):
                ins = None
                for k in range(8):
                    ins = e.matmul(bank(b), lhsT=mnT[:, k, mb * 128:(mb + 1) * 128], rhs=WMKV[:, k, kv * 512:(kv + 1) * 512], start=(k == 0), stop=(k == 7))
                return ins
            S.op("pe", f, r=["mnT0", "mnT1", "wmkv%d" % kv], w=["b%d" % b])
            dst = (mo if kv == 0 else mo2)[:, mb, :]
            S.op("act", lambda e, dst=dst, b=b: e.copy(out=dst, in_=bank(b)), r=["b%d" % b], w=["mo%d%d" % (kv, mb)])
            if kv == 1:
                S.op("dve", lambda e, b=b, mb=mb: e.tensor_copy(out=mvb[:, mb, :], in_=bank(b)), r=["b%d" % b], w=["mvb"])
            out_toks.append(dma("sp", (mkp if kv == 0 else mvp)[mb * 128:(mb + 1) * 128, :], dst, r=["mo%d%d" % (kv, mb)], w=["o_m%d%d" % (kv, mb)], stream="d_o%d" % (mb * 2 + kv)))
    for h in range(4):
        b = nb()

        def f(e, h=h, b=b):
            ins = None
            for k in range(8):
                ins = e.matmul(bank(b)[:, 0:256], lhsT=WMKV[:, k, h * 128:(h + 1) * 128], rhs=mnT[:, k, :], start=(k == 0), stop=(k == 7))
            return ins
        S.op("pe", f, r=["mnT0", "mnT1", "wmkv0"], w=["b%d" % b])
        S.op("act", lambda e, h=h, b=b: e.copy(out=mkT[:, h, :], in_=bank(b)[:, 0:256]), r=["b%d" % b], w=["mkT"])

    if KSTOP == -2:
        return finish()
    for g in range(4):
        S.op("dve", lambda e, g=g: e.tensor_tensor(out=wmsk[:, g, :], in0=wraw[:, g, :], in1=trilt, op=ALU.mult), r=["wraw", "trilt"], w=["wmsk"])
    transposes([wmsk[:, g, :] for g in range(4)], ["wmsk"], flat(WT), ["WT"], BF16)
    if KSTOP == -3:
        return finish()
    S.op("dve", lambda e: e.tensor_copy(out=r8[0:8], in_=bass.AP(w8.tensor, w8[0:8].offset, [list(w8[0:8].ap[0]), [8, 4], [0, 16], [1, 8]])), r=["w8"], w=["r8"])
    for g in range(4):
        b = nb()
        S.op("pe", lambda e, g=g, b=b: e.matmul(bank(b)[:, 0:128], lhsT=e8t[0:8, :], rhs=r8[0:8, g].rearrange("p a b -> p (a b)"), start=True, stop=True),
             r=["e8t", "r8"], w=["b%d" % b])
        S.op("dve", lambda e, g=g, b=b: e.tensor_tensor(out=wrep[:, g, :], in0=bank(b)[:, 0:128], in1=bdmt, op=ALU.mult), r=["b%d" % b, "bdmt"], w=["wrep"])
    transposes([wrep[:, g, :] for g in range(4)], ["wrep"], flat(WTS), ["WTS"], BF16)
    if KSTOP == -4:
        return finish()
    b = nb()
    S.op("pe", lambda e, b=b: e.transpose(out=bank(b)[:, 0:4], in_=sgbr[0:4, :], identity=idf[0:4, 0:4]), r=["sgbr", "idf"], w=["b%d" % b])
    S.op("act", lambda e, b=b: e.copy(out=sgbT, in_=bank(b)[:, 0:4]), r=["b%d" % b], w=["sgbT"])
    b = nb()
    S.op("pe", lambda e, b=b: e.matmul(bank(b)[:, 0:4], lhsT=e8t[0:8, :], rhs=sgbT[0:8, :], start=True, stop=True), r=["e8t", "sgbT"], w=["b%d" % b])
    S.op("act", lambda e, b=b: e.copy(out=sgbS, in_=bank(b)[:, 0:4]), r=["b%d" % b], w=["sgbS"])

    for part in range(2):
        cin = cinp[part]
        for r_ in range(2):
            rr = part * 2 + r_
            src = (conv_w[rr] if rr < 3 else conv_b[0]).rearrange("(c p) -> c p", p=128)
            dma("sp", cin[r_ * 44:(r_ + 1) * 44, :], src, w=["cin%d" % part], stream="d_c%d" % (20 + rr))
        b = nb()
        S.op("pe", lambda e, b=b, cin=cin: e.transpose(out=bank(b)[:, 0:88], in_=cin[0:88, :], identity=idf[0:88, 0:88]), r=["cin%d" % part, "idf"], w=["b%d" % b])
        S.op("act", lambda e, b=b, part=part: e.copy(out=flat(cwt)[:, part * 88:(part + 1) * 88], in_=bank(b)[:, 0:88]), r=["b%d" % b], w=["cwt"])
    if KSTOP == 1:
        return finish()
    S.barrier(keep_streams=("d_WA", "d_WB0", "d_WB1", "d_WB2"), keep_res=("win", "wo"))
    A.off = P1_MARK

    TB = Arena.__new__(Arena)
    TB.t = A.t
    TB.off = WB_OFF + 12 * 2048
    TB.cap = WB_OFF + 22 * 2048
    tg = TB.carve([3072], F32)
    acc = TB.carve([D], F32)
    tmp2 = TB.carve([D], F32)
    XIN = [A.carve([D], F32) for _ in range(2)]
    RC = [A.carve([64], F32) for _ in range(2)]
    RS = [A.carve([64], F32) for _ in range(2)]
    hb = A.carve([D], BF16)
    junk = A.carve([D], BF16)
    hT = A.carve([8, 128], BF16)
    stt = A.carve([128], F32)
    tq = A.carve([512], F32)
    uq = A.carve([512], F32)
    qr = A.carve([512], BF16)
    kr = A.carve([128], F32)
    kbb = A.carve([128], BF16)
    vf = A.carve([128], F32)
    vring = A.carve([2, 128], BF16)
    kTring = A.carve([2, 128], BF16)
    qT = A.carve([4, 128], BF16)
    usb = A.carve([512], F32)
    gvs = A.carve([512], F32)
    vn = gvs
    vnb = A.carve([512], BF16)
    sgo = A.carve([512], BF16)
    mqb = A.carve([512], BF16)
    mqT = A.carve([4, 128], BF16)
    ssb8 = A.carve([8, 256], F32)
    pb8 = A.carve([8, 256], BF16)
    pT8 = A.carve([8, 2, 128], BF16)
    ssb = ssb8[:, 0:4]
    pb = pb8[:, 0:4]
    pT = pT8[:, 0:4]
    em, pmb, pmT = ssb, pb, pT
    brT = [A.carve([4, 128], BF16) for _ in range(3)]
    P1_END = A.off

    def stat(i, n=1):
        return stt[:, i:i + n]

    def head(xsrc, slot, ridx):
        xin = XIN[slot]
        xr = "xin%d" % slot
        dma("sp", xin, xsrc, w=[xr], stream="d_xin%d" % slot)
        dma("sp", RC[slot], ropec[ridx], w=["rc%d" % slot], stream="d_rc%d" % slot)
        dma("sp", RS[slot], ropes[ridx], w=["rs%d" % slot], stream="d_rs%d" % slot)
        S.op("act", lambda e: e.activation(out=junk, in_=xin, func=AF.Square, scale=1.0 / 32.0, accum_out=stat(0)), r=[xr], w=["junk", "ams"])
        rstd_from(stat(0), stat(1), EPS, "a")
        S.op("dve", lambda e: e.scalar_tensor_tensor(out=hb, in0=xin, scalar=stat(1), in1=gpre, op0=ALU.mult, op1=ALU.mult), r=[xr, "ars", "gpre"], w=["hb"])
        transposes([hb[:, k * 128:(k + 1) * 128] for k in range(8)], ["hb"], flat(hT), ["hT"], BF16)

    def zmm(gi, b):
        c0, cw = ZG[gi]

        def f(e):
            ins = None
            for k in range(8):
                ins = e.matmul(bank(b)[:, 0:cw], lhsT=hT[:, k, :], rhs=WIN[:, k, c0:c0 + cw], start=(k == 0), stop=(k == 7))
            return ins
        S.op("pe", f, r=["hT", "win%d" % gi], w=["b%d" % b])

    def rope_ops(src, nh, slot, b, t_, u_):
        src4 = src.rearrange("p (h a i) -> p h a i", h=nh, a=2)
        swp = bass.AP(src.tensor, src.offset + 32, [list(src.ap[0]), [64, nh], [-32, 2], [1, 32]])
        rc, rs_ = RC[slot], RS[slot]
        cb = bass.AP(rc.tensor, rc.offset, [list(rc.ap[0]), [0, nh], [32, 2], [1, 32]])
        sb = bass.AP(rs_.tensor, rs_.offset, [list(rs_.ap[0]), [0, nh], [32, 2], [1, 32]])
        t4 = t_.rearrange("p (h a i) -> p h a i", h=nh, a=2)
        u4 = u_.rearrange("p (h a i) -> p h a i", h=nh, a=2)
        S.op("dve", lambda e: e.tensor_tensor(out=t4, in0=src4, in1=cb, op=ALU.mult), r=["b%d" % b, "rc%d" % slot], w=["tq"])
        S.op("dve", lambda e: e.tensor_tensor(out=u4, in0=swp, in1=sb, op=ALU.mult), r=["b%d" % b, "rs%d" % slot], w=["uq"])

    def proj_kv(slot):
        b1 = nb()
        zmm(1, b1)
        rope_ops(bank(b1)[:, 0:128], 2, slot, b1, tq[:, 0:128], uq[:, 0:128])
        S.op("dve", lambda e: e.tensor_tensor(out=kr, in0=tq[:, 0:128], in1=uq[:, 0:128], op=ALU.add), r=["tq", "uq"], w=["kr"])
        S.op("act", lambda e: e.copy(out=kbb, in_=kr), r=["kr"], w=["kbb"])
        S.op("act", lambda e: e.copy(out=vf, in_=bank(b1)[:, 128:256]), r=["b%d" % b1], w=["vf"])

    def proj_q(slot):
        b0 = nb()
        zmm(0, b0)
        rope_ops(bank(b0), 8, slot, b0, tq, uq)
        qr_perm = qr.rearrange("p (h e d) -> p e h d", h=4, e=2)
        S.op("dve", lambda e: e.tensor_tensor(out=qr_perm, in0=tq.rearrange("p (e h d) -> p e h d", e=2, h=4), in1=uq.rearrange("p (e h d) -> p e h d", e=2, h=4), op=ALU.add),
             r=["tq", "uq"], w=["qr"])

    def proj_sgu():
        b2 = nb()
        zmm(2, b2)
        S.op("act", lambda e: e.activation(out=usb, in_=bank(b2), func=AF.Gelu), r=["b%d" % b2], w=["usb"])

    def proj_sgv():
        b3 = nb()
        zmm(3, b3)
        S.op("act", lambda e: e.activation(out=gvs, in_=bank(b3), func=AF.Gelu, accum_out=stat(2)), r=["b%d" % b3], w=["gvs", "lsum"])
        S.op("dve", lambda e: e.tensor_scalar(out=stat(3), in0=stat(2), scalar1=-1.0 / 512.0, scalar2=None, op0=ALU.mult), r=["lsum"], w=["lnm"])
        S.op("dve", lambda e: e.tensor_scalar(out=gvs, in0=gvs, scalar1=stat(3), scalar2=None, op0=ALU.add), r=["gvs", "lnm"], w=["gvs"])
        S.op("act", lambda e: e.activation(out=junk[:, 0:512], in_=gvs, func=AF.Square, scale=float(512 ** -0.5), accum_out=stat(4)), r=["gvs"], w=["junk", "lms"])
        rstd_from(stat(4), stat(5), EPS, "l")
        S.op("dve", lambda e: e.scalar_tensor_tensor(out=gvs, in0=gvs, scalar=stat(5), in1=lng, op0=ALU.mult, op1=ALU.mult), r=["gvs", "lrs", "lng"], w=["gvs"])
        S.op("dve", lambda e: e.tensor_tensor(out=gvs, in0=gvs, in1=lnb, op=ALU.add), r=["gvs", "lnb"], w=["gvs"])
        S.op("act", lambda e: e.copy(out=vnb, in_=gvs), r=["gvs"], w=["vnb"])

    def proj_mq():
        b4 = nb()
        zmm(4, b4)
        S.op("act", lambda e: e.copy(out=mqb, in_=bank(b4)), r=["b%d" % b4], w=["mqb"])

    def proj_gates(lo, hi):
        for gi in range(lo, hi):
            bg = nb()
            zmm(5 + gi, bg)
            S.op("act", lambda e, gi=gi, bg=bg: e.activation(out=tg[:, gi * 512:(gi + 1) * 512], in_=bank(bg), func=AF.Tanh, scale=0.5), r=["b%d" % bg], w=["tg%d" % gi])

    def kv_publish(slot_kv):
        transposes([kbb], ["kbb"], kTring[:, slot_kv, :], ["kT%d" % slot_kv], BF16)
        S.op("act", lambda e: e.copy(out=vring[:, slot_kv, :], in_=vf), r=["vf"], w=["v%d" % slot_kv])

    def softmax_direct(src_res, src_ap, scale, sink_cols, dst32, dst32_res, dstb, dstb_res, so, sx, nh):
        M, NM, SM, DF, RI = so + 8, so + 8 + nh, so + 8 + 2 * nh, so + 8 + 3 * nh, so + 8 + 4 * nh
        src_res = list(src_res)
        S.op("dve", lambda e: e.tensor_reduce(out=stat(M, nh), in_=src_ap, axis=AX.X, op=ALU.max), r=src_res, w=["smx" + sx])
        if sink_cols is not None:
            S.op("dve", lambda e: e.scalar_tensor_tensor(out=stat(NM, nh), in0=stat(M, nh), scalar=-scale, in1=nsnk[:, sink_cols:sink_cols + nh], op0=ALU.mult, op1=ALU.min),
                 r=["smx" + sx, "nsnk"], w=["snm" + sx])
        else:
            S.op("dve", lambda e: e.tensor_scalar(out=stat(NM, nh), in0=stat(M, nh), scalar1=-scale, scalar2=None, op0=ALU.mult), r=["smx" + sx], w=["snm" + sx])

        def fexp(e):
            ins = None
            for h in range(nh):
                ins = e.activation(out=dst32[:, h, :], in_=src_ap[:, h, :], func=AF.Exp, bias=stat(NM + h), scale=scale, accum_out=stat(SM + h))
            return ins
        S.op("act", fexp, r=src_res + ["snm" + sx], w=[dst32_res, "ssum" + sx])
        if sink_cols is not None:
            S.op("dve", lambda e: e.tensor_tensor(out=stat(DF, nh), in0=snk[:, sink_cols:sink_cols + nh], in1=stat(NM, nh), op=ALU.add), r=["snk", "snm" + sx], w=["sdf" + sx])
            S.op("act", lambda e: e.activation(out=stat(DF, nh), in_=stat(DF, nh), func=AF.Exp), r=["sdf" + sx], w=["sdf" + sx])
            S.op("dve", lambda e: e.tensor_tensor(out=stat(SM, nh), in0=stat(SM, nh), in1=stat(DF, nh), op=ALU.add), r=["ssum" + sx, "sdf" + sx], w=["ssum" + sx])
        S.op("dve", lambda e: e.reciprocal(out=stat(RI, nh), in_=stat(SM, nh)), r=["ssum" + sx], w=["srin" + sx])
        rin = stat(RI, nh)
        rb = bass.AP(rin.tensor, rin.offset, [list(rin.ap[0]), [1, nh], [0, 256]])
        S.op("dve", lambda e: e.tensor_tensor(out=dstb, in0=dst32, in1=rb, op=ALU.mult), r=[dst32_res, "srin" + sx], w=[dstb_res])

    def softmax4(src_res, src_ap, mask_ap, mask_res, scale, sink_cols, dst32, dst32_res, dstb, dstb_res, so=44, sx="", nh=4):
        M, NM, SM, DF, RI = so + 8, so + 8 + nh, so + 8 + 2 * nh, so + 8 + 3 * nh, so + 8 + 4 * nh
        if mask_ap is not None:
            mk = bass.AP(mask_ap.tensor, mask_ap.offset, [list(mask_ap.ap[0]), [0, nh], [1, 256]])
            S.op("dve", lambda e: e.scalar_tensor_tensor(out=dst32, in0=src_ap, scalar=scale, in1=mk, op0=ALU.mult, op1=ALU.add),
                 r=list(src_res) + [mask_res], w=[dst32_res])
        else:
            S.op("act", lambda e: e.activation(out=dst32, in_=src_ap, func=AF.Identity, scale=scale), r=list(src_res), w=[dst32_res])
        S.op("dve", lambda e: e.tensor_reduce(out=stat(M, nh), in_=dst32, axis=AX.X, op=ALU.max), r=[dst32_res], w=["smx" + sx])
        if sink_cols is not None:
            S.op("dve", lambda e: e.tensor_tensor(out=stat(M, nh), in0=stat(M, nh), in1=snk[:, sink_cols:sink_cols + nh], op=ALU.max), r=["smx" + sx, "snk"], w=["smx" + sx])
        S.op("dve", lambda e: e.tensor_scalar(out=stat(NM, nh), in0=stat(M, nh), scalar1=-1.0, scalar2=None, op0=ALU.mult), r=["smx" + sx], w=["snm" + sx])

        def fexp(e):
            ins = None
            for h in range(nh):
                ins = e.activation(out=dst32[:, h, :], in_=dst32[:, h, :], func=AF.Exp, bias=stat(NM + h), scale=1.0, accum_out=stat(SM + h))
            return ins
        S.op("act", fexp, r=[dst32_res, "snm" + sx], w=[dst32_res, "ssum" + sx])
        if sink_cols is not None:
            S.op("dve", lambda e: e.tensor_tensor(out=stat(DF, nh), in0=snk[:, sink_cols:sink_cols + nh], in1=stat(NM, nh), op=ALU.add), r=["snk", "snm" + sx], w=["sdf" + sx])
            S.op("act", lambda e: e.activation(out=stat(DF, nh), in_=stat(DF, nh), func=AF.Exp), r=["sdf" + sx], w=["sdf" + sx])
            S.op("dve", lambda e: e.tensor_tensor(out=stat(SM, nh), in0=stat(SM, nh), in1=stat(DF, nh), op=ALU.add), r=["ssum" + sx, "sdf" + sx], w=["ssum" + sx])
        S.op("dve", lambda e: e.reciprocal(out=stat(RI, nh), in_=stat(SM, nh)), r=["ssum" + sx], w=["srin" + sx])
        rin = stat(RI, nh)
        rb = bass.AP(rin.tensor, rin.offset, [list(rin.ap[0]), [1, nh], [0, 256]])
        S.op("dve", lambda e: e.tensor_tensor(out=dstb, in0=dst32, in1=rb, op=ALU.mult), r=[dst32_res, "srin" + sx], w=[dstb_res])

    OB = 3

    def attn_q_transposes():
        transposes([qr[:, h * 128:(h + 1) * 128] for h in range(4)], ["qr"], flat(qT), ["qT"], BF16)

    def attn_scores8(cur, mask8, mask8_res):
        prev = 1 - cur

        def fs(e):
            ins = None
            for e_ in range(2):
                for h in range(4):
                    for kbi, sl in enumerate((prev, cur)):
                        c0 = 4 * 512 + (e_ * 4 + h) * 256 + kbi * 128
                        e.matmul(PS[:, c0:c0 + 128], lhsT=qT[e_ * 64:(e_ + 1) * 64, h, :], rhs=kTring[e_ * 64:(e_ + 1) * 64, sl, :],
                                 start=(h % 2 == 0 and kbi == 0), stop=False, skip_group_check=True)
            for hh in range(8):
                c0 = 4 * 512 + hh * 256
                ins = e.matmul(PS[:, c0:c0 + 256], lhsT=idb, rhs=mask8, start=False, stop=True, skip_group_check=True)
            return ins
        S.op("pe", fs, r=["qT", "kT0", "kT1", "idb", mask8_res], w=["b4", "b5", "b6", "b7"])

    def attn_softmax8():
        sc = PS[:, 4 * 512:8 * 512].rearrange("p (h k) -> p h k", h=8)
        softmax_direct(["b4", "b5", "b6", "b7"], sc, 0.125, 0, ssb8, "ssb", pb8, "pb", so=0, sx="8", nh=8)

    def attn_pv8(cur):
        prev = 1 - cur
        transposes([pb8[:, hh, kbi * 128:(kbi + 1) * 128] for hh in range(4) for kbi in range(2)], ["pb"], flat(pT8[:, 0:4]), ["pT"], BF16)
        transposes([pb8[:, hh, kbi * 128:(kbi + 1) * 128] for hh in range(4, 8) for kbi in range(2)], ["pb"], flat(pT8[:, 4:8]), ["pT2"], BF16)

        def fpv(e):
            ins = None
            for hh in range(8):
                e_ = hh // 4
                cc, par = hh // 2, hh % 2
                for kbi, sl in enumerate((prev, cur)):
                    ins = e.matmul(bank(OB)[par * 64:(par + 1) * 64, cc * 128:(cc + 1) * 128], lhsT=vring[:, sl, e_ * 64:(e_ + 1) * 64],
                                   rhs=pT8[:, hh, kbi, :], start=(kbi == 0), stop=(kbi == 1), skip_group_check=True)
            return ins
        S.op("pe", fpv, r=["pT", "pT2", "v0", "v1"], w=["b%d" % OB])

    def attn_evac():
        S.op("act", lambda e: e.copy(out=flat(brT[0]), in_=bank(OB)), r=["b%d" % OB], w=["brT0"])

    def mem_scores_softmax_only():
        sc = PS[:, 6 * 512:8 * 512].rearrange("p (h k) -> p h k", h=4)
        softmax_direct(["b6", "b7"], sc, float(128 ** -0.5), None, em, "ssb", pmb, "pb", so=44, sx="", nh=4)

    def mem_scores_softmax():
        mem_scores_softmax_only()
        transposes([pmb[:, h, mbi * 128:(mbi + 1) * 128] for h in range(4) for mbi in range(2)], ["pb"], flat(pmT), ["pT"], BF16)

    def mem_scores():
        transposes([mqb[:, h * 128:(h + 1) * 128] for h in range(4)], ["mqb"], flat(mqT), ["mqT"], BF16)

        def fs(e):
            ins = None
            for h in range(4):
                ins = e.matmul(PS[:, 6 * 512 + h * 256: 6 * 512 + (h + 1) * 256], lhsT=mqT[:, h, :], rhs=mkT[:, h, :], start=True, stop=True)
            return ins
        S.op("pe", fs, r=["mqT", "mkT"], w=["b6", "b7"])
        mem_scores_softmax_only()

    def mem_pv():
        transposes([pmb[:, h, mbi * 128:(mbi + 1) * 128] for h in range(4) for mbi in range(2)], ["pb"], flat(pmT), ["pT"], BF16)

        def fpv(e):
            ins = None
            for h in range(4):
                for mbi in range(2):
                    ins = e.matmul(bank(OB)[:, h * 128:(h + 1) * 128], lhsT=mvb[:, mbi, h * 128:(h + 1) * 128], rhs=pmT[:, h, mbi, :],
                                   start=(mbi == 0), stop=(mbi == 1), skip_group_check=True)
            return ins
        S.op("pe", fpv, r=["pT", "mvb"], w=["b%d" % OB])
        S.op("act", lambda e: e.copy(out=flat(brT[2]), in_=bank(OB)), r=["b%d" % OB], w=["brT2"])

    def sg_mix(wt, wt_res, bias, bias_res):
        b = nb()

        def f(e):
            ins = None
            for g in range(4):
                ins = e.matmul(bank(b)[:, g * 128:(g + 1) * 128], lhsT=wt[:, g, :], rhs=vnb[:, g * 128:(g + 1) * 128], start=True, stop=True)
            return ins
        S.op("pe", f, r=["vnb", wt_res], w=["b%d" % b])

        def fo(e):
            ins = None
            for g in range(4):
                ins = e.scalar_tensor_tensor(out=sgo[:, g * 128:(g + 1) * 128], in0=bank(b)[:, g * 128:(g + 1) * 128], scalar=bias[:, g:g + 1],
                                             in1=usb[:, g * 128:(g + 1) * 128], op0=ALU.add, op1=ALU.mult)
            return ins
        S.op("dve", fo, r=["b%d" % b, "usb", bias_res], w=["sgo"])
        transposes([sgo[:, c * 128:(c + 1) * 128] for c in range(4)], ["sgo"], flat(brT[1]), ["brT1"], BF16)

    def proj_branch(br):
        for half in range(2):
            b = nb()

            def f(e, half=half, b=b):
                ins = None
                for cc in range(4):
                    ins = e.matmul(bank(b), lhsT=brT[br][:, cc, :], rhs=WO[:, br * 4 + cc, half * 512:(half + 1) * 512], start=(cc == 0), stop=(cc == 3))
                return ins
            S.op("pe", f, r=["brT%d" % br, "wo%d" % br], w=["b%d" % b])
            gi = br * 2 + half
            ah = acc[:, half * 512:(half + 1) * 512]
            th = tmp2[:, half * 512:(half + 1) * 512]
            if br == 0:
                S.op("dve", lambda e, gi=gi, b=b, ah=ah: e.scalar_tensor_tensor(out=ah, in0=tg[:, gi * 512:(gi + 1) * 512], scalar=1.0, in1=bank(b), op0=ALU.add, op1=ALU.mult),
                     r=["tg%d" % gi, "b%d" % b], w=["acc%d" % half])
            else:
                S.op("dve", lambda e, gi=gi, b=b, th=th: e.scalar_tensor_tensor(out=th, in0=tg[:, gi * 512:(gi + 1) * 512], scalar=1.0, in1=bank(b), op0=ALU.add, op1=ALU.mult),
                     r=["tg%d" % gi, "b%d" % b], w=["tmp%d" % half])
                S.op("dve", lambda e, ah=ah, th=th: e.tensor_tensor(out=ah, in0=ah, in1=th, op=ALU.add), r=["acc%d" % half, "tmp%d" % half], w=["acc%d" % half])

    def back2(slot, x1row, store_eng="pool"):
        xin = XIN[slot]
        xr = "xin%d" % slot
        S.op("act", lambda e: e.activation(out=junk, in_=acc, func=AF.Square, scale=1.0 / 32.0, accum_out=stat(6)), r=["acc0", "acc1"], w=["junk", "pms"])
        rstd_from(stat(6), stat(7), 4.0 * EPS, "p")
        S.op("dve", lambda e: e.scalar_tensor_tensor(out=acc, in0=acc, scalar=stat(7), in1=gpost, op0=ALU.mult, op1=ALU.mult), r=["acc0", "acc1", "prs", "gpost"], w=["acc0", "acc1"])
        S.op("dve", lambda e: e.tensor_tensor(out=xin, in0=xin, in1=acc, op=ALU.add), r=[xr, "acc0", "acc1"], w=[xr])
        dma(store_eng, x1s[x1row * 128:(x1row + 1) * 128, :], xin, r=[xr], w=["x1s%d" % x1row],
            stream="d_x1o%d%s" % (slot, "" if store_eng == "pool" else "h"))

    rot["n"] = 3
    rot["i"] = 0
    head(xp[0:128, :], 0, 0)
    proj_kv(0)
    kv_publish(0)
    head(xp[128:256, :], 1, 1)
    proj_kv(1)
    proj_q(1)
    for bi in range(1, NPB):
        slot = bi % 2
        kv_publish(slot)
        if bi == NPB - 1:
            out_toks.append(dma("pool", wkp, kr, r=["kr"], w=["o_wkp"], stream="d_o4"))
            out_toks.append(dma("pool", wvp, vf, r=["vf"], w=["o_wvp"], stream="d_o5"))
        mk_ap, mk_res = (maskf8, "maskf8") if bi == 2 else (maskp8, "maskp8")
        attn_q_transposes()
        attn_scores8(slot, mk_ap, mk_res)
        attn_softmax8()
        proj_sgu()
        proj_sgv()
        proj_mq()
        proj_gates(0, 4)
        attn_pv8(slot)
        sg_mix(WT, "WT", sgbT, "sgbT")
        attn_evac()
        mem_scores()
        proj_gates(4, 6)
        proj_branch(0)
        proj_branch(1)
        mem_pv()
        proj_branch(2)
        nslot = 1 - slot
        if bi + 1 < NPB:
            head(xp[(bi + 1) * 128:(bi + 2) * 128, :], nslot, bi + 1)
        else:
            head(xs, nslot, NPB)
        proj_kv(nslot)
        proj_q(nslot)
        back2(slot, bi - 1)
        if KSTOP == 3 and bi == 2:
            return finish()
    if KSTOP == 4:
        return finish()

    proj_sgu()
    proj_sgv()
    proj_mq()
    proj_gates(0, 6)
    out_toks.append(dma("sp", sgv, gvs, r=["gvs"], w=["o_sgv"], stream="d_o6"))
    out_toks.append(dma("sp", wks[:, 0:120, :], cwk[:, 8:128, :], w=["o_wks_a"], stream="d_o7"))
    out_toks.append(dma("sp", wvs[:, 0:120, :], cwv[:, 8:128, :], w=["o_wvs_a"], stream="d_o8"))
    for q in range(SEQS):
        out_toks.append(dma("sp", wks[q, 120:128, :], kr[q * 8:(q + 1) * 8, :], r=["kr"], w=["o_wks_b%d" % q], stream="d_o9"))
        out_toks.append(dma("sp", wvs[q, 120:128, :], vf[q * 8:(q + 1) * 8, :], r=["vf"], w=["o_wvs_b%d" % q], stream="d_o10"))
    if KSTOP == 5:
        return finish()
    S.barrier()
    SB = Arena.__new__(Arena)
    SB.t = A.t
    SB.off = WA_OFF
    SB.cap = WA_OFF + 8 * 5632 * 2
    kqT = SB.carve([SEQS, 4, 256], BF16)
    Zq = [SB.carve([SEQS, 128], BF16) for _ in range(2)]
    Zs = [SB.carve([SEQS, 128], BF16) for _ in range(2)]
    cwkb = SB.carve([SEQS, 128], BF16)
    cwkT = SB.carve([SEQS, 128], BF16)
    cwvb = SB.carve([SEQS, 128], BF16)
    NST = 4
    kst = [SB.carve([2, 512], BF16) for _ in range(NST)]
    vst = [SB.carve([2, 512], BF16) for _ in range(NST)]

    for z in range(2):
        S.op("dve", lambda e, z=z: e.memset(flat(Zq[z]), 0.0), w=["Zq%d" % z])
        S.op("dve", lambda e, z=z: e.memset(flat(Zs[z]), 0.0), w=["Zs%d" % z])
    def k_load(q):
        dma("pool", kst[q % NST], cmk[q].rearrange("(mb m) c -> m mb c", mb=2), w=["kst%d" % (q % NST)], stream="d_sk%d" % (q % NST))

    def v_load(q):
        dma("pool", vst[q % NST], cmv[q].rearrange("(mb m) c -> m mb c", mb=2), w=["vst%d" % (q % NST)], stream="d_sv%d" % (q % NST))

    for q in range(NST):
        k_load(q)
    dma("pool", cwkb, cwk.rearrange("q j c -> j q c"), w=["cwkb"], stream="d_s0")
    dma("pool", cwvb, cwv.rearrange("q j c -> j q c"), w=["cwvb"], stream="d_s1")
    for half in range(2):
        transposes([cwkb[:, half * 8 + i, :] for i in range(8)], ["cwkb"], cwkT[:, half * 8:(half + 1) * 8, :], ["cwkT%d" % half], BF16)
    for q in range(SEQS):
        st = kst[q % NST]
        transposes([st[:, mbi, h * 128:(h + 1) * 128] for h in range(4) for mbi in range(2)], ["kst%d" % (q % NST)],
                   kqT[:, q], ["kqT%d" % q], BF16)
        if q + NST < SEQS:
            k_load(q + NST)
    for q in range(NST):
        v_load(q)

    kv_publish(1)
    transposes([qr[:, h * 128:(h + 1) * 128] for h in range(4)], ["qr"], flat(qT), ["qT"], BF16)
    transposes([mqb[:, h * 128:(h + 1) * 128] for h in range(4)], ["mqb"], flat(mqT), ["mqT"], BF16)

    def diag_fill(dst, src):
        d = bass.AP(dst.tensor, dst.offset, [list(dst.ap[0]), [136, 16], [1, 8]])
        s = src.rearrange("p (q t) -> p q t", q=16)
        return d, s

    ob = OB
    for grp in range(2):
        e_ = grp
        for h in range(4):
            z = Zs[h % 2]
            zr = "Zs%d" % (h % 2)
            d_, s_ = diag_fill(z, qT[:, h, :])
            S.op("dve", lambda e, d_=d_, s_=s_: e.tensor_copy(out=d_, in_=s_), r=["qT"], w=[zr])

            def fs(e, h=h, e_=e_, z=z):
                ins = None
                base = 6 * 512 + h * 256
                for q in range(SEQS):
                    ins = e.matmul(PS[:, base:base + 128], lhsT=z[e_ * 64:(e_ + 1) * 64, q, :], rhs=cwkT[e_ * 64:(e_ + 1) * 64, q, :],
                                   start=(q == 0 and h % 2 == 0), stop=(q == SEQS - 1), skip_group_check=True)
                ins = e.matmul(PS[:, base + 128:base + 256], lhsT=qT[e_ * 64:(e_ + 1) * 64, h, :], rhs=kTring[e_ * 64:(e_ + 1) * 64, 1, :],
                               start=False, stop=True, skip_group_check=True)
                return ins
            S.op("pe", fs, r=[zr, "cwkT0", "cwkT1", "qT", "kT1"], w=["b6", "b7"])
        sc = PS[:, 6 * 512:8 * 512].rearrange("p (h k) -> p h k", h=4)
        softmax4(["b6", "b7"], sc, masks, "masks", 0.125, grp * 4, ssb, "ssb", pb, "pb")
        transposes([pb[:, h, kbi * 128:(kbi + 1) * 128] for h in range(4) for kbi in range(2)], ["pb"], flat(pT), ["pT"], BF16)

        def fpv(e, e_=e_):
            ins = None
            for h in range(4):
                hh = e_ * 4 + h
                cc, par = hh // 2, hh % 2
                o = bank(ob)[par * 64:(par + 1) * 64, cc * 128:(cc + 1) * 128]
                ins = e.matmul(o, lhsT=vring[:, 1, e_ * 64:(e_ + 1) * 64], rhs=pT[:, h, 1, :], start=True, stop=False, skip_group_check=True)
                for q in range(SEQS):
                    ins = e.matmul(o[:, q * 8:(q + 1) * 8], lhsT=cwvb[:, q, e_ * 64:(e_ + 1) * 64], rhs=pT[:, h, 0, q * 8:(q + 1) * 8],
                                   start=False, stop=(q == SEQS - 1), skip_group_check=True)
            return ins
        S.op("pe", fpv, r=["pT", "v1", "cwvb"], w=["b%d" % ob])
    S.op("act", lambda e: e.copy(out=flat(brT[0]), in_=bank(ob)), r=["b%d" % ob], w=["brT0"])

    sg_mix(WTS, "WTS", sgbS, "sgbS")

    for h in range(4):
        z = Zq[h % 2]
        zr = "Zq%d" % (h % 2)
        d_, s_ = diag_fill(z, mqT[:, h, :])
        S.op("dve", lambda e, d_=d_, s_=s_: e.tensor_copy(out=d_, in_=s_), r=["mqT"], w=[zr])

        def fs(e, h=h, z=z):
            ins = None
            base = 6 * 512 + h * 256
            for q in range(SEQS):
                ins = e.matmul(PS[:, base:base + 256], lhsT=z[:, q, :], rhs=kqT[:, q, h, :], start=(q == 0 and h % 2 == 0), stop=(q == SEQS - 1), skip_group_check=True)
            return ins
        S.op("pe", fs, r=[zr] + ["kqT%d" % q for q in range(SEQS)], w=["b6", "b7"])
    mem_scores_softmax()
    ob2 = OB
    for q in range(SEQS):
        st = vst[q % NST]

        def fpv(e, q=q, st=st):
            ins = None
            for h in range(4):
                for mbi in range(2):
                    ins = e.matmul(bank(ob2)[:, h * 128 + q * 8: h * 128 + (q + 1) * 8], lhsT=st[:, mbi, h * 128:(h + 1) * 128], rhs=pmT[:, h, mbi, q * 8:(q + 1) * 8],
                                   start=(q == 0 and h == 0 and mbi == 0), stop=(mbi == 1), skip_group_check=True)
            return ins
        S.op("pe", fpv, r=["pT", "vst%d" % (q % NST)], w=["b%d" % ob2])
        if q + NST < SEQS:
            v_load(q + NST)
    S.op("act", lambda e: e.copy(out=flat(brT[2]), in_=bank(ob2)), r=["b%d" % ob2], w=["brT2"])
    fence = [(st_, v_) for st_, v_ in S.cnt.items() if v_ > 0 and (st_ in ("pe", "act", "dve", "pool") or st_.startswith("d_s"))]
    w_up_r = w_up.rearrange("(k p) e -> p k e", p=128)
    for c in (0, 5, 6, 1, 7, 2, 8, 3, 9, 4, 10):
        dma("pool", WUP[:, :, c * 512:(c + 1) * 512], w_up_r[:, :, c * 512:(c + 1) * 512], w=["wup%d" % c], stream="d_WA%d" % c, after=fence)
    proj_branch(0)
    proj_branch(1)
    proj_branch(2)
    back2(0, NOWN + 1, store_eng="sp")
    if KSTOP == 6:
        return finish()

    S.barrier(keep_streams=("d_WA",), keep_res=("wup",))

    rot["n"] = 4
    rot["i"] = 0
    A.off = P_MARK
    gpf = A.carve([D], F32)
    gqf = A.carve([D], F32)
    P2_MARK = A.off
    T = {}

    def carve_p2(ntok):
        A.off = P2_MARK
        nblk = ntok // 128
        T["X1"] = [A.carve([nblk, D], F32) for _ in range(2)]
        T["h2"] = [A.carve([D], BF16) for _ in range(2)]
        T["junk2"] = None
        T["st2"] = A.carve([16], F32)
        T["h2T"] = [A.carve([8, ntok + 2], BF16) for _ in range(2 if ntok > 128 else 1)]
        T["actT"] = A.carve([NFC, ntok], BF16)
        T["EXT"] = [[A.carve([ntok + 4 if ntok > 128 else 160], F32) for _ in range(2)] for _ in range(2)]
        T["CG"] = [A.carve([ntok], F32) for _ in range(2)]
        T["CV"] = [A.carve([ntok], F32) for _ in range(2)]
        T["GG"] = [A.carve([ntok], F32) for _ in range(2)]
        T["fsb"] = A.carve([ntok // 128, D], F32)
        T["ysb"] = None
        if ntok > 128:
            T["osb"] = T["fsb"][:, 0, 0:128]
            T["osb_res"] = "fsb00"
        else:
            T["osb"] = A.carve([128], F32)
            T["osb_res"] = "osb"

    carve_p2(128)
    XH = A.carve([D], F32)
    scvr4 = [A.carve([1408], F32) for _ in range(2)]
    osb11 = scvr4[0].rearrange("p (a c) -> p a c", a=11)
    scvT = A.carve([44, 32], F32)
    ncs = A.carve([1408], F32)

    dma("sp", gpf, g_pffn.partition_broadcast(128), w=["gpf"], stream="d_c3")
    dma("sp", gqf, g_qffn.partition_broadcast(128), w=["gqf"], stream="d_c4")
    w_dn_r = w_down.rearrange("(k p) e -> p k e", p=128)
    for c4, (f0, f1) in enumerate(((0, 4), (4, 10), (10, 16), (16, 22))):
        dma("pool", WDN[:, f0:f1, :], w_dn_r[:, f0:f1, :], w=["wdn%d" % c4], stream="d_WB%d" % c4)

    for cg_ in range(4):
        scvr = scvr4[cg_ % 2]
        dma("sp", scvr[0:32, :], scv[:, cg_ * 1408:(cg_ + 1) * 1408], w=["scvr%d" % (cg_ % 2)], stream="d_c%d" % (7 + cg_ % 2))
        b = nb()

        def f(e, b=b, scvr=scvr):
            ins = None
            for i in range(11):
                ins = e.transpose(out=bank(b)[:, i * 32:(i + 1) * 32], in_=scvr[0:32, i * 128:(i + 1) * 128], identity=idf[0:32, 0:32])
            return ins
        S.op("pe", f, r=["scvr%d" % (cg_ % 2), "idf"], w=["b%d" % b])
        S.op("act", lambda e, b=b, cg_=cg_: e.copy(out=scvT[:, cg_ * 11:(cg_ + 1) * 11, :], in_=bank(b)[:, 0:352].rearrange("p (a b) -> p a b", a=11)), r=["b%d" % b], w=["scvT"])

    def ffn_front_a(nrows, xt_ap, xres, j):
        h2, st2 = T["h2"][j], T["st2"]
        c = 4 * j
        S.op("act", lambda e: e.activation(out=h2[0:nrows], in_=xt_ap[0:nrows], func=AF.Square, scale=1.0 / 32.0, accum_out=st2[0:nrows, c:c + 1]), r=[xres], w=["h2_%d" % j, "fms%d" % j])
        S.op("pool", lambda e: e.tensor_scalar(out=st2[0:nrows, c + 1:c + 2], in0=st2[0:nrows, c:c + 1], scalar1=EPS, scalar2=None, op0=ALU.add), r=["fms%d" % j], w=["frs%d" % j])
        S.op("pool", lambda e: e.tensor_tensor(out=st2[0:nrows, c + 1:c + 2], in0=st2[0:nrows, c + 1:c + 2], in1=negh[0:nrows], op=ALU.pow), r=["frs%d" % j, "negh"], w=["frs%d" % j])
        S.op("dve", lambda e: e.scalar_tensor_tensor(out=h2[0:nrows], in0=xt_ap[0:nrows], scalar=st2[0:nrows, c + 1:c + 2], in1=gpf[0:nrows], op0=ALU.mult, op1=ALU.mult),
             r=[xres, "frs%d" % j, "gpf"], w=["h2_%d" % j])

    def ffn_front_b(nrows, col0, j, hsel=0):
        h2, h2T = T["h2"][j], T["h2T"][hsel]
        hres = "h2T%d" % hsel
        b = nb()
        pv = bankb(b)

        def f(e):
            ins = None
            for k in range(8):
                ins = e.transpose(out=pv[:, k * 128:k * 128 + nrows], in_=h2[0:nrows, k * 128:(k + 1) * 128], identity=idb[0:nrows, 0:nrows])
            return ins
        S.op("pe", f, r=["h2_%d" % j, "idb"], w=["b%d" % b])
        src = pv[:, 0:1024].rearrange("p (k t) -> p k t", k=8)[:, :, 0:nrows]
        S.op("act", lambda e: e.copy(out=h2T[:, :, col0:col0 + nrows], in_=src), r=["b%d" % b], w=[hres])

    def nb2():
        if rot["i"] % 2:
            nb()
        b = nb()
        nb()
        return b

    def ffn_tile(N, sample, x1_tiles, yout, hooks=None, hsel=0):
        actT, EXT, CG, CV, GG, fsb, ysb, st2, junk2 = (T[k] for k in ("actT", "EXT", "CG", "CV", "GG", "fsb", "ysb", "st2", "junk2"))
        h2T = T["h2T"][hsel]
        hres = "h2T%d" % hsel
        n_act = 128 if sample else N

        NTB = n_act // 128

        def down_slice(fc):
            def f(e):
                ins = None
                for j in range(NTB):
                    for half in range(2):
                        o = PS[:, (4 + 2 * j + half) * 512:(5 + 2 * j + half) * 512]
                        ins = e.matmul(o, lhsT=actT[:, fc, j * 128:(j + 1) * 128], rhs=WDN[:, fc, half * 512:(half + 1) * 512],
                                       start=(fc == 0), stop=(fc == NFC - 1))
                return ins
            S.op("pe", f, r=["actT%d" % fc, "wdn%d" % (0 if fc < 4 else 1 if fc < 10 else 2 if fc < 16 else 3)], w=["b%d" % (4 + i) for i in range(2 * NTB)])

        def tail_ops(fc):
            bi_ = fc % 2
            gg = GG[bi_]
            S.op("act", lambda e: e.activation(out=gg[:, 0:n_act], in_=CG[bi_][:, 0:n_act], func=AF.Gelu_apprx_tanh), r=["cg%d" % bi_], w=["gg%d" % bi_])
            S.op("pool", lambda e: e.tensor_tensor(out=actT[:, fc, 0:n_act], in0=gg[:, 0:n_act], in1=CV[bi_][:, 0:n_act], op=ALU.mult),
                 r=["gg%d" % bi_, "cv%d" % bi_], w=["actT%d" % fc])

        for fc in range(NFC):
            bi_ = fc % 2
            stage = []
            for gv in range(2):
                fcc = fc + gv * NFC
                b = nb()

                def f(e, fcc=fcc, b=b):
                    ins = None
                    for k in range(8):
                        ins = e.matmul(bank(b)[:, 0:N], lhsT=WUP[:, k, fcc * 128:(fcc + 1) * 128], rhs=h2T[:, k, 0:N], start=(k == 0), stop=(k == 7))
                    return ins
                S.op("pe", f, r=[hres, "wup%d" % (fcc // 4)], w=["b%d" % b])
                ext = EXT[bi_][gv]
                er = "ext%d%d" % (bi_, gv)
                erc = er + "c"
                cdst = (CG if gv == 0 else CV)[bi_]
                cres = ("cg%d" if gv == 0 else "cv%d") % bi_
                w0 = cwt[:, 0, fcc:fcc + 1]
                w1 = cwt[:, 1, fcc:fcc + 1]
                w2 = cwt[:, 2, fcc:fcc + 1]
                bb = cwt[:, 3, fcc:fcc + 1]
                if not sample:
                    S.op("pool", lambda e, ext=ext, fcc=fcc: e.tensor_copy(out=ext[:, 0:2], in_=carry[:, :, fcc]), r=["carry%d" % fcc], w=[erc])
                    S.op("act", lambda e, ext=ext, b=b: e.copy(out=ext[:, 2:2 + N], in_=bank(b)[:, 0:N]), r=["b%d" % b], w=[er])
                    S.op("pool", lambda e, ext=ext, fcc=fcc: e.tensor_copy(out=carry[:, :, fcc], in_=ext[:, N:N + 2]), r=[er], w=["carry%d" % fcc])
                    v2, v1, v0 = ext[:, 2:N + 2], ext[:, 1:N + 1], ext[:, 0:N]
                    cd = cdst[:, 0:N]
                else:
                    S.op("dve", lambda e, fcc=fcc, b=b: e.tensor_scalar(out=carry[:, :, fcc], in0=bank(b)[:, 0:2], scalar1=hvt, scalar2=None, op0=ALU.mult),
                         r=["b%d" % b, "hvt"], w=["carry%d" % fcc])
                    e3 = ext[:, 0:160].rearrange("p (q t) -> p q t", q=16)
                    S.op("pool", lambda e, e3=e3, fcc=fcc: e.tensor_copy(out=e3[:, :, 0:2], in_=scvT[:, fcc, :].rearrange("p (q i) -> p q i", q=16)), r=["scvT"], w=[erc])
                    S.op("act", lambda e, e3=e3, b=b: e.copy(out=e3[:, :, 2:10], in_=bank(b)[:, 2:130].rearrange("p (q t) -> p q t", q=16)), r=["b%d" % b], w=[er])
                    nview = bass.AP(ncs.tensor, ncs.offset + fcc, [list(ncs.ap[0]), [88, 16], [44, 2]])
                    S.op("pool", lambda e, e3=e3, nview=nview: e.tensor_copy(out=nview, in_=e3[:, :, 8:10]), r=[er], w=["ncs"])
                    v2, v1, v0 = e3[:, :, 2:10], e3[:, :, 1:9], e3[:, :, 0:8]
                    cd = cdst[:, 0:128].rearrange("p (q t) -> p q t", q=16)
                stage.append((cd, v2, v1, v0, w2, w1, w0, bb, er, erc, cres))
            for (cd, v2, v1, v0, w2, w1, w0, bb, er, erc, cres) in stage:
                S.op("act", lambda e, cd=cd, v2=v2, w2=w2, bb=bb: e.activation(out=cd, in_=v2, func=AF.Identity, scale=w2, bias=bb), r=[er, "cwt"], w=[cres])
            for (cd, v2, v1, v0, w2, w1, w0, bb, er, erc, cres) in stage:
                S.op("dve", lambda e, cd=cd, v1=v1, w1=w1: e.scalar_tensor_tensor(out=cd, in0=v1, scalar=w1, in1=cd, op0=ALU.mult, op1=ALU.add), r=[er, erc, cres, "cwt"], w=[cres])
                S.op("dve", lambda e, cd=cd, v0=v0, w0=w0: e.scalar_tensor_tensor(out=cd, in0=v0, scalar=w0, in1=cd, op0=ALU.mult, op1=ALU.add), r=[er, erc, cres, "cwt"], w=[cres])
            if fc > 0:
                tail_ops(fc - 1)
            if fc >= 2:
                down_slice(fc - 2)
            for hk in (hooks or {}).get(fc, ()):
                hk()
        tail_ops(NFC - 1)
        down_slice(NFC - 2)
        down_slice(NFC - 1)
        for j in range(NTB):
            for half in range(2):
                bk = 4 + 2 * j + half
                eng = "act" if half == 0 else "dve"
                if eng == "act":
                    S.op("act", lambda e, j=j, half=half, bk=bk: e.copy(out=fsb[:, j, half * 512:(half + 1) * 512], in_=bank(bk)), r=["b%d" % bk], w=["fsb%d%d" % (j, half)])
                else:
                    S.op("dve", lambda e, j=j, half=half, bk=bk: e.tensor_copy(out=fsb[:, j, half * 512:(half + 1) * 512], in_=bank(bk)), r=["b%d" % bk], w=["fsb%d%d" % (j, half)])

        def make_tail(j):
            def run():
                fres = ["fsb%d0" % j, "fsb%d1" % j]
                ftok = fsb[:, j, :]
                junk_ = T["h2"][j]
                c = 8 + 2 * j
                S.op("act", lambda e: e.activation(out=junk_, in_=ftok, func=AF.Square, scale=1.0 / 32.0, accum_out=st2[:, c:c + 1]), r=fres, w=["h2_%d" % j, "gms%d" % j])
                S.op("pool", lambda e: e.tensor_scalar(out=st2[:, c + 1:c + 2], in0=st2[:, c:c + 1], scalar1=EPS, scalar2=None, op0=ALU.add), r=["gms%d" % j], w=["grs%d" % j])
                S.op("pool", lambda e: e.tensor_tensor(out=st2[:, c + 1:c + 2], in0=st2[:, c + 1:c + 2], in1=negh, op=ALU.pow), r=["grs%d" % j, "negh"], w=["grs%d" % j])
                x1t, x1r = x1_tiles[j]
                S.op("dve", lambda e: e.scalar_tensor_tensor(out=ftok, in0=ftok, scalar=st2[:, c + 1:c + 2], in1=gqf, op0=ALU.mult, op1=ALU.mult), r=fres + ["grs%d" % j, "gqf"], w=fres)
                S.op("dve", lambda e: e.tensor_tensor(out=ftok, in0=ftok, in1=x1t, op=ALU.add), r=fres + [x1r], w=fres)
                dma("pool", yout[j], ftok, r=fres, w=["o_y"], stream="d_yo%d" % (j % 2))
            return run
        return [make_tail(j) for j in range(n_act // 128)]

    X1a = T["X1"][0][:, 0, :]
    dma("sp", XH[0:2, :], x1s[126:128, :], w=["XH"], stream="d_x2h")
    dma("sp", X1a, x1s[(NOWN + 1) * 128:(NOWN + 2) * 128, :], w=["X10a"], stream="d_x2a")
    ffn_front_a(2, XH, "XH", 0)
    ffn_front_b(2, 0, 0)
    ffn_front_a(128, X1a, "X10a", 1)
    ffn_front_b(128, 2, 1)
    for tl in ffn_tile(130, True, [(X1a, "X10a")], [ys]):
        tl()
    for g0 in range(0, 11, 4):
        n_ = min(4, 11 - g0)
        b = nb()

        def f(e, b=b, g0=g0, n_=n_):
            ins = None
            for i in range(n_):
                ins = e.transpose(out=bank(b)[:, i * 128:(i + 1) * 128], in_=ncs[:, (g0 + i) * 128:(g0 + i + 1) * 128], identity=idf)
            return ins
        S.op("pe", f, r=["ncs", "idf"], w=["b%d" % b])
        S.op("act", lambda e, b=b, g0=g0, n_=n_: e.copy(out=osb11[:, g0:g0 + n_, :], in_=bank(b)[:, 0:n_ * 128].rearrange("p (a c) -> p a c", a=n_)), r=["b%d" % b], w=["scvr0"])
    dma("pool", cvs.rearrange("(c p) f -> p c f", p=128), osb11, r=["scvr0"], w=["o_cvs"], stream="d_o11")
    if KSTOP == 7:
        return finish()
    S.barrier()
    carve_p2(256)
    osb = T["osb"]
    def p2_load(ti):
        xt = T["X1"][ti % 2]
        for j in range(2):
            row = 1 + ti * 2 + j
            dma("sp", xt[:, j, :], x1s[row * 128:(row + 1) * 128, :], w=["X1%d%s" % (ti % 2, "ab"[j])], stream="d_x2%d%s" % (ti % 2, "ab"[j]))

    def p2_fa(ti, j):
        return lambda: ffn_front_a(128, T["X1"][ti % 2][:, j, :], "X1%d%s" % (ti % 2, "ab"[j]), j)

    def p2_fb(ti, j):
        return lambda: ffn_front_b(128, j * 128, j, hsel=ti % 2)

    NT = NOWN // 2
    p2_load(0)
    for j in range(2):
        p2_fa(0, j)()
        p2_fb(0, j)()
    pending = []
    for ti in range(NT):
        xt = T["X1"][ti % 2]
        res = ["X1%d%s" % (ti % 2, "ab"[j]) for j in range(2)]
        hooks = {}
        if pending:
            hooks[1] = [pending[0]]
            hooks[3] = [pending[1]]
        if ti + 1 < NT:
            hooks[5] = [lambda ti=ti: p2_load(ti + 1)]
            hooks[7] = [p2_fa(ti + 1, 0)]
            hooks[9] = [p2_fa(ti + 1, 1)]
            hooks[13] = [p2_fb(ti + 1, 0)]
            hooks[15] = [p2_fb(ti + 1, 1)]
        pending = ffn_tile(256, False, [(xt[:, j, :], res[j]) for j in range(2)], [yp[(ti * 2 + j) * 128:(ti * 2 + j + 1) * 128, :] for j in range(2)],
                           hooks=hooks, hsel=ti % 2)
    for tl in pending:
        tl()
    b = nb()
    S.op("pe", lambda e, b=b: e.transpose(out=bank(b)[0:88, 0:128], in_=flat(carry), identity=idf), r=["carry%d" % f_ for f_ in range(44)] + ["idf"], w=["b%d" % b])
    S.op("act", lambda e, b=b, osb=osb: e.copy(out=osb[0:88, :], in_=bank(b)[0:88, 0:128]), r=["b%d" % b], w=[T["osb_res"]])
    dma("pool", cvp, osb[0:88, :], r=[T["osb_res"]], w=["o_cvp"], stream="d_o12")

    return finish()


_CACHE = {}


def _rope_tables():
    half = 32
    inv = (10000.0 ** (-np.arange(half, dtype=np.float32) / half)).astype(np.float32)
    return inv


def _host_consts(core):
    inv = _rope_tables()
    pos = np.zeros((NPB + 1, 128), np.float32)
    for b in range(NPB):
        pos[b] = core * TOK - 256 + b * 128 + np.arange(128)
    pos[NPB] = PAST + (np.arange(128) % 8)
    ang = (pos[:, :, None].astype(np.float32) * inv[None, None, :]).astype(np.float32)
    c = np.cos(ang).astype(np.float32)
    s = np.sin(ang).astype(np.float32)
    ropec = np.concatenate([c, c], axis=-1).astype(np.float32)
    ropes = np.concatenate([-s, s], axis=-1).astype(np.float32)
    i = np.arange(128)[:, None]
    j = np.arange(128)[None, :]
    maskp = np.concatenate([np.where(j > i, 0.0, NEG), np.where(j <= i, 0.0, NEG)], axis=1).astype(np.float32)
    maskf = maskp.copy()
    if core == 0:
        maskf[:, 0:128] = NEG
    NEG8 = 8.0 * NEG
    maskp8 = np.concatenate([np.where(j > i, 0.0, NEG8), np.where(j <= i, 0.0, NEG8)], axis=1).astype(np.float32)
    maskf8 = maskp8.copy()
    if core == 0:
        maskf8[:, 0:128] = NEG8
    t = (np.arange(128) % 8)[:, None]
    q = (np.arange(128) // 8)[:, None]
    tt = (np.arange(128) % 8)[None, :]
    qq = (np.arange(128) // 8)[None, :]
    masks = np.concatenate([np.where(j >= t + 1, 0.0, NEG), np.where((qq == q) & (tt <= t), 0.0, NEG)], axis=1).astype(np.float32)
    tril = (j <= i).astype(np.float32)
    bdm = ((qq == q) & (tt <= t)).astype(np.float32)
    e8 = np.tile(np.eye(8, dtype=np.float32), (1, 16))
    hv = np.full((128, 1), 0.0 if core == 0 else 1.0, np.float32)
    return dict(ropec=ropec, ropes=ropes, maskp=maskp, maskf=maskf, masks=masks, maskp8=maskp8, maskf8=maskf8, tril=tril, bdm=bdm, e8=e8, hv=hv,
                ident=np.eye(128, dtype=np.float32))


def kernel(x_prompt, x_sample, cache_win_k, cache_win_v, cache_mem_k, cache_mem_v, state_conv, mem_prompt,
           pre_mix_g, w_in, attn_sinks, sg_ln_g, sg_ln_b, sg_w, sg_b, mem_norm_g, w_mem_kv, w_o,
           post_mix_g, pre_ffn_g, w_up, conv_w, conv_b, w_down, post_ffn_g):
    f = lambda a: np.ascontiguousarray(np.asarray(a, dtype=np.float32))
    if "nc" not in _CACHE:
        _CACHE["nc"] = build_program()
    nc = _CACHE["nc"]
    xpr = f(x_prompt)[0]
    xpad = np.concatenate([np.zeros((256, D), np.float32), xpr], axis=0)
    shared = dict(
        memp=f(mem_prompt)[0], w_in=f(w_in)[0], w_mkv=f(w_mem_kv)[0], w_o=f(w_o)[0].reshape(1536, D), w_up=f(w_up)[0], w_down=f(w_down)[0],
        g_pre=f(pre_mix_g), g_post=f(post_mix_g), g_pffn=f(pre_ffn_g), g_qffn=f(post_ffn_g), g_mem=f(mem_norm_g),
        ln_g=f(sg_ln_g), ln_b=f(sg_ln_b), sinks=f(attn_sinks), sg_w=f(sg_w)[0], sg_b=f(sg_b)[0], conv_w=f(conv_w)[0], conv_b=f(conv_b),
    )
    in_maps = []
    for c in range(NCORES):
        m = dict(shared)
        m.update(_host_consts(c))
        m["xp"] = np.ascontiguousarray(xpad[c * TOK:c * TOK + NPB * 128])
        sl = slice(c * SEQS, (c + 1) * SEQS)
        m["xs"] = f(x_sample)[sl].reshape(128, D)
        m["cwk"] = f(cache_win_k)[0, sl].reshape(SEQS, 128, 128)
        m["cwv"] = f(cache_win_v)[0, sl].reshape(SEQS, 128, 128)
        m["cmk"] = f(cache_mem_k)[0, sl].reshape(SEQS, 256, 512)
        m["cmv"] = f(cache_mem_v)[0, sl].reshape(SEQS, 256, 512)
        m["scv"] = f(state_conv)[0, sl].reshape(32, 2 * DFF)
        in_maps.append(m)
    res = run_bass_kernel_spmd(nc, in_maps, core_ids=list(range(NCORES))).results
    y_p = np.concatenate([r["yp"] for r in res], axis=0)[None]
    y_s = np.concatenate([r["ys"].reshape(SEQS, 8, D) for r in res], axis=0)
    last = res[NCORES - 1]
    wk_p = last["wkp"].reshape(1, 1, 128, 2, 64)
    wv_p = last["wvp"].reshape(1, 1, 128, 2, 64)
    mk_p = res[0]["mkp"].reshape(1, 1, 256, 4, 128)
    mv_p = res[0]["mvp"].reshape(1, 1, 256, 4, 128)
    cv_p = last["cvp"].reshape(1, 1, 2, 2 * DFF)
    wk_s = np.concatenate([r["wks"] for r in res], axis=0).reshape(1, 128, 128, 2, 64)
    wv_s = np.concatenate([r["wvs"] for r in res], axis=0).reshape(1, 128, 128, 2, 64)
    sgv_s = np.concatenate([r["sgv"].reshape(SEQS, 8, 512) for r in res], axis=0)[None]
    cv_s = np.concatenate([r["cvs"].reshape(SEQS, 2, 2 * DFF) for r in res], axis=0)[None]
    outs = (y_p, y_s, wk_p, wv_p, mk_p, mv_p, cv_p, wk_s, wv_s, sgv_s, cv_s)
    return tuple(np.ascontiguousarray(o, dtype=np.float32) for o in outs)
```

```python
import numpy as np
import concourse.bass as bass
import concourse.mybir as mybir
from concourse.bass_utils import run_bass_kernel_spmd

F32 = mybir.dt.float32
BF16 = mybir.dt.bfloat16
AF = mybir.ActivationFunctionType
ALU = mybir.AluOpType
AX = mybir.AxisListType

NCORES = 8
D = 1024
SEQ = 16384
TOK = SEQ // NCORES
NOWN = TOK // 128
NPB = NOWN + 2
SEQS = 16
INW = 5376
DFF = 2816
NFC = 22
EPS = 1e-6
NEG = -30000.0
PAST = 16384
ZG = [(0, 512), (512, 256), (768, 512), (1280, 512), (1792, 512)] + [(2304 + 512 * i, 512) for i in range(6)]


class Sched:
    ENGS = ("pe", "act", "dve", "pool", "sp")

    def __init__(self, nc):
        self.nc = nc
        self.q = {e: [] for e in self.ENGS}
        self.sem = {}
        self.cnt = {}
        self.waited = {e: {} for e in self.ENGS}
        self.lastw = {}
        self.readers = {}
        self.snap = {}
        self.seq = 0
        for e in ("pe", "act", "dve", "pool"):
            self._mk(e)

    def _mk(self, name):
        self.sem[name] = self.nc.alloc_semaphore("s_" + name)
        self.cnt[name] = 0

    def op(self, eng, fn, r=(), w=(), dma=None, after=()):
        deps = {}

        def need(tok):
            if tok is None:
                return
            s, v = tok
            if deps.get(s, 0) < v:
                deps[s] = v

        w = list(w) + [x for x in r if len(x) == 2 and x[0] == "b" and x[1].isdigit()]
        r = [x for x in r if not (len(x) == 2 and x[0] == "b" and x[1].isdigit())]
        for x in r:
            need(self.lastw.get(x))
        for x in w:
            need(self.lastw.get(x))
            for t in self.readers.get(x, ()):
                need(t)
        for t in after:
            need(t)
        waits = []
        wd = self.waited[eng]
        for s, v in sorted(deps.items(), key=lambda kv: -self.snap.get(kv, (0, None))[0]):
            if s == "pe" and eng == "pe":
                continue
            if wd.get(s, 0) >= v:
                continue
            wd[s] = v
            waits.append((s, v))
            sn = self.snap.get((s, v))
            if sn is not None and sn[1] is not None:
                for k2, v2 in sn[1].items():
                    if wd.get(k2, 0) < v2:
                        wd[k2] = v2
        if dma is not None:
            if dma not in self.sem:
                self._mk(dma)
            stream, inc = dma, 16
        else:
            stream, inc = eng, 1
        self.cnt[stream] += inc
        tok = (stream, self.cnt[stream])
        self.seq += 1
        self.snap[tok] = (self.seq, dict(wd))
        self.q[eng].append((waits, fn, stream, inc))
        for x in w:
            self.lastw[x] = tok
            self.readers[x] = []
        for x in r:
            self.readers.setdefault(x, []).append(tok)
        return tok

    def barrier(self, keep_streams=(), keep_res=()):
        for e in self.ENGS:
            waits = []
            for s, v in self.cnt.items():
                if s.startswith(tuple(keep_streams)) if keep_streams else False:
                    continue
                if v > self.waited[e].get(s, 0):
                    self.waited[e][s] = v
                    waits.append((s, v))
            self.q[e].append((waits, None, None, 0))
        self.lastw = {k: v for k, v in self.lastw.items() if keep_res and k.startswith(tuple(keep_res))}
        self.readers = {}

    def emit(self):
        nc = self.nc
        handles = {"pe": "tensor", "act": "scalar", "dve": "vector", "pool": "gpsimd", "sp": "sync"}
        with nc.Block() as block:
            for en in self.ENGS:
                def make(en):
                    def f(eng):
                        for waits, fn, stream, inc in self.q[en]:
                            for s, v in waits:
                                eng.wait_ge(self.sem[s], v)
                            if fn is None:
                                continue
                            ins = fn(eng)
                            ins.then_inc(self.sem[stream], inc)
                    return f
                getattr(block, handles[en])(make(en))


class Arena:
    def __init__(self, nc, nbytes):
        self.t = nc.alloc_sbuf_tensor("arena", [128, nbytes // 4], F32)
        self.cap = nbytes
        self.off = 0

    def carve(self, shape_free, dtype):
        n = int(np.prod(shape_free))
        nb = n * (2 if dtype == BF16 else 4)
        nb = (nb + 31) // 32 * 32
        assert self.off + nb <= self.cap, ("SBUF arena overflow", self.off, nb, self.cap)
        ap = self.t[:, self.off // 4:(self.off + nb) // 4]
        self.off += nb
        if dtype == BF16:
            ap = ap.bitcast(BF16)
        ap = ap[:, 0:n]
        if len(shape_free) == 2:
            ap = ap.rearrange("p (a b) -> p a b", a=shape_free[0])
        elif len(shape_free) == 3:
            ap = ap.rearrange("p (a b c) -> p a b c", a=shape_free[0], b=shape_free[1])
        elif len(shape_free) == 4:
            ap = ap.rearrange("p (a b c d) -> p a b c d", a=shape_free[0], b=shape_free[1], c=shape_free[2])
        return ap


def flat(ap):
    n = len(ap.shape)
    if n == 2:
        return ap
    if n == 3:
        return ap.rearrange("p a b -> p (a b)")
    if n == 4:
        return ap.rearrange("p a b c -> p (a b c)")
    return ap.rearrange("p a b c d -> p (a b c d)")


def build_program():
    nc = bass.Bass("TRN2", target_bir_lowering=False)
    S = Sched(nc)

    def din(name, shape):
        return nc.dram_tensor(name, list(shape), F32, kind="ExternalInput").ap()

    def dout(name, shape):
        return nc.dram_tensor(name, list(shape), F32, kind="ExternalOutput").ap()

    xp = din("xp", [NPB * 128, D])
    xs = din("xs", [128, D])
    cwk = din("cwk", [SEQS, 128, 128])
    cwv = din("cwv", [SEQS, 128, 128])
    cmk = din("cmk", [SEQS, 256, 512])
    cmv = din("cmv", [SEQS, 256, 512])
    scv = din("scv", [32, 2 * DFF])
    memp = din("memp", [256, D])
    w_in = din("w_in", [D, INW])
    w_mkv = din("w_mkv", [D, 1024])
    w_o = din("w_o", [1536, D])
    w_up = din("w_up", [D, 2 * DFF])
    w_down = din("w_down", [DFF, D])
    g_pre = din("g_pre", [1, D])
    g_post = din("g_post", [1, D])
    g_pffn = din("g_pffn", [1, D])
    g_qffn = din("g_qffn", [1, D])
    g_mem = din("g_mem", [1, D])
    ln_g = din("ln_g", [1, 512])
    ln_b = din("ln_b", [1, 512])
    sinks = din("sinks", [1, 8])
    sg_w = din("sg_w", [4, 128, 128])
    sg_b = din("sg_b", [4, 128])
    conv_w = din("conv_w", [3, 2 * DFF])
    conv_b = din("conv_b", [1, 2 * DFF])
    ident = din("ident", [128, 128])
    ropec = din("ropec", [NPB + 1, 128, 64])
    ropes = din("ropes", [NPB + 1, 128, 64])
    maskp_d = din("maskp", [128, 256])
    maskf_d = din("maskf", [128, 256])
    maskp8_d = din("maskp8", [128, 256])
    maskf8_d = din("maskf8", [128, 256])
    masks_d = din("masks", [128, 256])
    tril_d = din("tril", [128, 128])
    bdm_d = din("bdm", [128, 128])
    e8_d = din("e8", [8, 128])
    hv_d = din("hv", [128, 1])

    yp = dout("yp", [TOK, D])
    ys = dout("ys", [128, D])
    wkp = dout("wkp", [128, 128])
    wvp = dout("wvp", [128, 128])
    mkp = dout("mkp", [256, 512])
    mvp = dout("mvp", [256, 512])
    cvp = dout("cvp", [88, 128])
    wks = dout("wks", [SEQS, 128, 128])
    wvs = dout("wvs", [SEQS, 128, 128])
    sgv = dout("sgv", [128, 512])
    cvs = dout("cvs", [1408, 128])
    x1s = nc.dram_tensor("x1s", [(NOWN + 2) * 128, D], F32).ap()

    A = Arena(nc, 212480)
    PS = nc.alloc_psum_tensor("PS", [128, 4096], F32)

    def bank(i):
        return PS[:, i * 512:(i + 1) * 512]

    def bankb(i):
        return bank(i).bitcast(BF16)

    rot = {"i": 0, "n": 6}

    def nb():
        i = rot["i"] % rot["n"]
        rot["i"] = (i + 1) % rot["n"]
        return i

    out_toks = []

    def dma(eng, out, in_, r=(), w=(), stream=None, after=()):
        return S.op(eng, lambda e: e.dma_start(out=out, in_=in_), r=r, w=w, dma=stream, after=after)

    import os
    KSTOP = int(os.environ.get("KSTOP", "99"))

    def finish():
        fin = {}
        for s_, v in S.cnt.items():
            if s_.startswith("d_"):
                fin[s_] = v
        S.q["sp"].append(([(s_, v) for s_, v in fin.items() if v > S.waited["sp"].get(s_, 0)], None, None, 0))
        S.emit()
        return nc

    idf = A.carve([128], F32)
    idb = A.carve([128], BF16)
    negh = A.carve([1], F32)
    hvt = A.carve([1], F32)
    carry = A.carve([2, 44], F32)
    cwt = A.carve([4, 44], F32)
    maskp8 = A.carve([256], BF16)
    maskf8 = A.carve([256], BF16)
    WA_OFF = A.off
    WA = A.carve([8, 5632], BF16)
    WB_OFF = A.off
    WB = A.carve([22, 1024], BF16)
    WIN = flat(WA)[:, 0:8 * INW].rearrange("p (k e) -> p k e", k=8)
    WO = WB[:, 0:12, :]
    WMKV = WB[:, 12:20, :]
    WUP = WA
    WDN = WB
    P_MARK = A.off

    dma("sp", idf, ident, w=["idf"], stream="d_c0")
    dma("pool", idb, ident, w=["idb"], stream="d_c1")
    dma("sp", hvt, hv_d, w=["hvt"], stream="d_c2")
    S.op("dve", lambda e: e.memset(negh, -0.5), w=["negh"])

    dma("pool", maskp8, maskp8_d, w=["maskp8"], stream="d_c24")
    dma("pool", maskf8, maskf8_d, w=["maskf8"], stream="d_c25")
    w_in_r = w_in.rearrange("(k p) e -> p k e", p=128)
    w_mkv_r = w_mkv.rearrange("(k p) e -> p k e", p=128)
    w_o_r = w_o.rearrange("(k p) e -> p k e", p=128)
    for half in range(2):
        dma("pool", WMKV[:, :, half * 512:(half + 1) * 512], w_mkv_r[:, :, half * 512:(half + 1) * 512],
            w=["wmkv%d" % half], stream="d_WB%d" % (12 + half))
    for ci, (c0, cw) in enumerate(ZG):
        dma("pool", WIN[:, :, c0:c0 + cw], w_in_r[:, :, c0:c0 + cw], w=["win%d" % ci], stream="d_WA%d" % ci)
    for br in range(3):
        dma("pool", WO[:, br * 4:(br + 1) * 4, :], w_o_r[:, br * 4:(br + 1) * 4, :], w=["wo%d" % br], stream="d_WB%d" % br)

    def rstd_from(ms, out, eps, tag):
        S.op("pool", lambda e: e.tensor_scalar(out=out, in0=ms, scalar1=eps, scalar2=None, op0=ALU.add), r=[tag + "ms"], w=[tag + "rs"])
        S.op("pool", lambda e: e.tensor_tensor(out=out, in0=out, in1=negh, op=ALU.pow), r=[tag + "rs", "negh"], w=[tag + "rs"])

    def transposes(srcs, src_res, dst, dst_res, dt, evac_eng="act"):
        b = nb()
        n = len(srcs)
        pv = bankb(b) if dt == BF16 else bank(b)
        idt = idb if dt == BF16 else idf

        def f(e):
            ins = None
            for i, s in enumerate(srcs):
                ins = e.transpose(out=pv[:, i * 128:(i + 1) * 128], in_=s, identity=idt)
            return ins
        S.op("pe", f, r=list(src_res) + ["idb", "idf"], w=["b%d" % b])
        src = pv[:, 0:n * 128]
        if len(dst.shape) == 3:
            src = src.rearrange("p (a b) -> p a b", a=dst.shape[1])
        if evac_eng == "act":
            S.op("act", lambda e: e.copy(out=dst, in_=src), r=["b%d" % b], w=list(dst_res))
        else:
            S.op(evac_eng, lambda e: e.tensor_copy(out=dst, in_=src), r=["b%d" % b], w=list(dst_res))

    gpre = A.carve([D], F32)
    gpost = A.carve([D], F32)
    lng = A.carve([512], F32)
    lnb = A.carve([512], F32)
    snk = A.carve([8], F32)
    maskp = A.carve([256], F32)
    maskf = A.carve([256], F32)
    masks = A.carve([256], F32)
    nsnk = A.carve([8], F32)
    WT = A.carve([4, 128], BF16)
    WTS = A.carve([4, 128], BF16)
    sgbT = A.carve([4], F32)
    sgbS = A.carve([4], F32)
    mkT = A.carve([4, 256], BF16)
    mvb = A.carve([2, 512], BF16)
    P1_MARK = A.off

    for t, src, nm, st in [(gpre, g_pre, "gpre", 3), (gpost, g_post, "gpost", 4), (lng, ln_g, "lng", 5), (lnb, ln_b, "lnb", 6), (snk, sinks, "snk", 7)]:
        dma("sp", t, src.partition_broadcast(128), w=[nm], stream="d_c%d" % st)
    dma("sp", maskp, maskp_d, w=["maskp"], stream="d_c8")
    dma("sp", maskf, maskf_d, w=["maskf"], stream="d_c9")
    dma("sp", masks, masks_d, w=["masks"], stream="d_c10")
    S.op("dve", lambda e: e.tensor_scalar(out=nsnk, in0=snk, scalar1=-1.0, scalar2=None, op0=ALU.mult), r=["snk"], w=["nsnk"])

    gmem = A.carve([D], F32)
    mx0 = A.carve([D], F32)
    mx1 = A.carve([D], F32)
    mnb = A.carve([2, D], BF16)
    mnT = A.carve([8, 256], BF16)
    junk0 = A.carve([D], BF16)
    st0 = A.carve([8], F32)
    trilt = A.carve([128], F32)
    bdmt = A.carve([128], F32)
    e8t = A.carve([128], F32)
    wraw = A.carve([4, 128], F32)
    wmsk = A.carve([4, 128], BF16)
    w8 = A.carve([4, 8], F32)
    r8 = A.carve([4, 16, 8], F32)
    wrep = A.carve([4, 128], BF16)
    sgbr = A.carve([128], F32)
    mo = A.carve([2, 512], F32)
    mo2 = A.carve([2, 512], F32)
    cinp = [A.carve([128], F32) for _ in range(2)]

    dma("sp", gmem, g_mem.partition_broadcast(128), w=["gmem"], stream="d_c11")
    dma("sp", mx0, memp[0:128, :], w=["mx0"], stream="d_c12")
    dma("sp", mx1, memp[128:256, :], w=["mx1"], stream="d_c13")
    dma("sp", trilt, tril_d, w=["trilt"], stream="d_c14")
    dma("sp", bdmt, bdm_d, w=["bdmt"], stream="d_c15")
    dma("sp", e8t[0:8, :], e8_d, w=["e8t"], stream="d_c16")
    dma("sp", wraw, sg_w.rearrange("g t s -> t g s"), w=["wraw"], stream="d_c17")
    dma("sp", w8[0:8, :, :], sg_w[:, 0:8, 0:8].rearrange("g t s -> t g s"), w=["w8"], stream="d_c18")
    dma("sp", sgbr[0:4, :], sg_b, w=["sgbr"], stream="d_c19")

    if KSTOP == -1:
        return finish()
    for mb, mx in enumerate((mx0, mx1)):
        S.op("act", lambda e, mx=mx, mb=mb: e.activation(out=junk0, in_=mx, func=AF.Square, scale=1.0 / 32.0, accum_out=st0[:, mb:mb + 1]),
             r=["mx%d" % mb], w=["junk0", "m%dms" % mb])
        rstd_from(st0[:, mb:mb + 1], st0[:, 2 + mb:3 + mb], EPS, "m%d" % mb)
        S.op("dve", lambda e, mx=mx, mb=mb: e.scalar_tensor_tensor(out=mnb[:, mb, :], in0=mx, scalar=st0[:, 2 + mb:3 + mb], in1=gmem, op0=ALU.mult, op1=ALU.mult),
             r=["mx%d" % mb, "m%drs" % mb, "gmem"], w=["mnb%d" % mb])
        transposes([mnb[:, mb, k * 128:(k + 1) * 128] for k in range(8)], ["mnb%d" % mb],
                   mnT[:, :, mb * 128:(mb + 1) * 128], ["mnT%d" % mb], BF16)
    for mb in range(2):
        for kv in range(2):
            b = nb()

            def f(e, mb=mb, kv=kv, b=b):
                ins = None
                for k in range(8):
                    ins = e.matmul(bank(b), lhsT=mnT[:, k, mb * 128:(mb + 1) * 128], rhs=WMKV[:, k, kv * 512:(kv + 1) * 512], start=(k == 0), stop=(k == 7))
                return ins
            S.op("pe", f, r=["mnT0", "mnT1", "wmkv%d" % kv], w=["b%d" % b])
            dst = (mo if kv == 0 else mo2)[:, mb, :]
            S.op("act", lambda e, dst=dst, b=b: e.copy(out=dst, in_=bank(b)), r=["b%d" % b], w=["mo%d%d" % (kv, mb)])
            if kv == 1:
                S.op("dve", lambda e, b=b, mb=mb: e.tensor_copy(out=mvb[:, mb, :], in_=bank(b)), r=["b%d" % b], w=["mvb"])
            out_toks.append(dma("sp", (mkp if kv == 0 else mvp)[mb * 128:(mb + 1) * 128, :], dst, r=["mo%d%d" % (kv, mb)], w=["o_m%d%d" % (kv, mb)], stream="d_o%d" % (mb * 2 + kv)))
    for h in range(4):
        b = nb()

        def f(e, h=h, b=b):
            ins = None
            for k in range(8):
                ins = e.matmul(bank(b)[:, 0:256], lhsT=WMKV[:, k, h * 128:(h + 1) * 128], rhs=mnT[:, k, :], start=(k == 0), stop=(k == 7))
            return ins
        S.op("pe", f, r=["mnT0", "mnT1", "wmkv0"], w=["b%d" % b])
        S.op("act", lambda e, h=h, b=b: e.copy(out=mkT[:, h, :], in_=bank(b)[:, 0:256]), r=["b%d" % b], w=["mkT"])

    if KSTOP == -2:
        return finish()
    for g in range(4):
        S.op("dve", lambda e, g=g: e.tensor_tensor(out=wmsk[:, g, :], in0=wraw[:, g, :], in1=trilt, op=ALU.mult), r=["wraw", "trilt"], w=["wmsk"])
    transposes([wmsk[:, g, :] for g in range(4)], ["wmsk"], flat(WT), ["WT"], BF16)
    if KSTOP == -3:
        return finish()
    S.op("dve", lambda e: e.tensor_copy(out=r8[0:8], in_=bass.AP(w8.tensor, w8[0:8].offset, [list(w8[0:8].ap[0]), [8, 4], [0, 16], [1, 8]])), r=["w8"], w=["r8"])
    for g in range(4):
        b = nb()
        S.op("pe", lambda e, g=g, b=b: e.matmul(bank(b)[:, 0:128], lhsT=e8t[0:8, :], rhs=r8[0:8, g].rearrange("p a b -> p (a b)"), start=True, stop=True),
             r=["e8t", "r8"], w=["b%d" % b])
        S.op("dve", lambda e, g=g, b=b: e.tensor_tensor(out=wrep[:, g, :], in0=bank(b)[:, 0:128], in1=bdmt, op=ALU.mult), r=["b%d" % b, "bdmt"], w=["wrep"])
    transposes([wrep[:, g, :] for g in range(4)], ["wrep"], flat(WTS), ["WTS"], BF16)
    if KSTOP == -4:
        return finish()
    b = nb()
    S.op("pe", lambda e, b=b: e.transpose(out=bank(b)[:, 0:4], in_=sgbr[0:4, :], identity=idf[0:4, 0:4]), r=["sgbr", "idf"], w=["b%d" % b])
    S.op("act", lambda e, b=b: e.copy(out=sgbT, in_=bank(b)[:, 0:4]), r=["b%d" % b], w=["sgbT"])
    b = nb()
    S.op("pe", lambda e, b=b: e.matmul(bank(b)[:, 0:4], lhsT=e8t[0:8, :], rhs=sgbT[0:8, :], start=True, stop=True), r=["e8t", "sgbT"], w=["b%d" % b])
    S.op("act", lambda e, b=b: e.copy(out=sgbS, in_=bank(b)[:, 0:4]), r=["b%d" % b], w=["sgbS"])

    for part in range(2):
        cin = cinp[part]
        for r_ in range(2):
            rr = part * 2 + r_
            src = (conv_w[rr] if rr < 3 else conv_b[0]).rearrange("(c p) -> c p", p=128)
            dma("sp", cin[r_ * 44:(r_ + 1) * 44, :], src, w=["cin%d" % part], stream="d_c%d" % (20 + rr))
        b = nb()
        S.op("pe", lambda e, b=b, cin=cin: e.transpose(out=bank(b)[:, 0:88], in_=cin[0:88, :], identity=idf[0:88, 0:88]), r=["cin%d" % part, "idf"], w=["b%d" % b])
        S.op("act", lambda e, b=b, part=part: e.copy(out=flat(cwt)[:, part * 88:(part + 1) * 88], in_=bank(b)[:, 0:88]), r=["b%d" % b], w=["cwt"])
    if KSTOP == 1:
        return finish()
    S.barrier(keep_streams=("d_WA", "d_WB0", "d_WB1", "d_WB2"), keep_res=("win", "wo"))
    A.off = P1_MARK

    TB = Arena.__new__(Arena)
    TB.t = A.t
    TB.off = WB_OFF + 12 * 2048
    TB.cap = WB_OFF + 22 * 2048
    tg = TB.carve([3072], F32)
    acc = TB.carve([D], F32)
    tmp2 = TB.carve([D], F32)
    XIN = [A.carve([D], F32) for _ in range(2)]
    RC = [A.carve([64], F32) for _ in range(2)]
    RS = [A.carve([64], F32) for _ in range(2)]
    hb = A.carve([D], BF16)
    junk = A.carve([D], BF16)
    hT = A.carve([8, 128], BF16)
    stt = A.carve([128], F32)
    tq = A.carve([512], F32)
    uq = A.carve([512], F32)
    qr = A.carve([512], BF16)
    kr = A.carve([128], F32)
    kbb = A.carve([128], BF16)
    vf = A.carve([128], F32)
    vring = A.carve([2, 128], BF16)
    kTring = A.carve([2, 128], BF16)
    qT = A.carve([4, 128], BF16)
    usb = A.carve([512], F32)
    gvs = A.carve([512], F32)
    vn = gvs
    vnb = A.carve([512], BF16)
    sgo = A.carve([512], BF16)
    mqb = A.carve([512], BF16)
    mqT = A.carve([4, 128], BF16)
    ssb8 = A.carve([8, 256], F32)
    pb8 = A.carve([8, 256], BF16)
    pT8 = A.carve([8, 2, 128], BF16)
    ssb = ssb8[:, 0:4]
    pb = pb8[:, 0:4]
    pT = pT8[:, 0:4]
    em, pmb, pmT = ssb, pb, pT
    brT = [A.carve([4, 128], BF16) for _ in range(3)]
    P1_END = A.off

    def stat(i, n=1):
        return stt[:, i:i + n]

    def head(xsrc, slot, ridx):
        xin = XIN[slot]
        xr = "xin%d" % slot
        dma("sp", xin, xsrc, w=[xr], stream="d_xin%d" % slot)
        dma("sp", RC[slot], ropec[ridx], w=["rc%d" % slot], stream="d_rc%d" % slot)
        dma("sp", RS[slot], ropes[ridx], w=["rs%d" % slot], stream="d_rs%d" % slot)
        S.op("act", lambda e: e.activation(out=junk, in_=xin, func=AF.Square, scale=1.0 / 32.0, accum_out=stat(0)), r=[xr], w=["junk", "ams"])
        rstd_from(stat(0), stat(1), EPS, "a")
        S.op("dve", lambda e: e.scalar_tensor_tensor(out=hb, in0=xin, scalar=stat(1), in1=gpre, op0=ALU.mult, op1=ALU.mult), r=[xr, "ars", "gpre"], w=["hb"])
        transposes([hb[:, k * 128:(k + 1) * 128] for k in range(8)], ["hb"], flat(hT), ["hT"], BF16)

    def zmm(gi, b):
        c0, cw = ZG[gi]

        def f(e):
            ins = None
            for k in range(8):
                ins = e.matmul(bank(b)[:, 0:cw], lhsT=hT[:, k, :], rhs=WIN[:, k, c0:c0 + cw], start=(k == 0), stop=(k == 7))
            return ins
        S.op("pe", f, r=["hT", "win%d" % gi], w=["b%d" % b])

    def rope_ops(src, nh, slot, b, t_, u_):
        src4 = src.rearrange("p (h a i) -> p h a i", h=nh, a=2)
        swp = bass.AP(src.tensor, src.offset + 32, [list(src.ap[0]), [64, nh], [-32, 2], [1, 32]])
        rc, rs_ = RC[slot], RS[slot]
        cb = bass.AP(rc.tensor, rc.offset, [list(rc.ap[0]), [0, nh], [32, 2], [1, 32]])
        sb = bass.AP(rs_.tensor, rs_.offset, [list(rs_.ap[0]), [0, nh], [32, 2], [1, 32]])
        t4 = t_.rearrange("p (h a i) -> p h a i", h=nh, a=2)
        u4 = u_.rearrange("p (h a i) -> p h a i", h=nh, a=2)
        S.op("dve", lambda e: e.tensor_tensor(out=t4, in0=src4, in1=cb, op=ALU.mult), r=["b%d" % b, "rc%d" % slot], w=["tq"])
        S.op("dve", lambda e: e.tensor_tensor(out=u4, in0=swp, in1=sb, op=ALU.mult), r=["b%d" % b, "rs%d" % slot], w=["uq"])

    def proj_kv(slot):
        b1 = nb()
        zmm(1, b1)
        rope_ops(bank(b1)[:, 0:128], 2, slot, b1, tq[:, 0:128], uq[:, 0:128])
        S.op("dve", lambda e: e.tensor_tensor(out=kr, in0=tq[:, 0:128], in1=uq[:, 0:128], op=ALU.add), r=["tq", "uq"], w=["kr"])
        S.op("act", lambda e: e.copy(out=kbb, in_=kr), r=["kr"], w=["kbb"])
        S.op("act", lambda e: e.copy(out=vf, in_=bank(b1)[:, 128:256]), r=["b%d" % b1], w=["vf"])

    def proj_q(slot):
        b0 = nb()
        zmm(0, b0)
        rope_ops(bank(b0), 8, slot, b0, tq, uq)
        qr_perm = qr.rearrange("p (h e d) -> p e h d", h=4, e=2)
        S.op("dve", lambda e: e.tensor_tensor(out=qr_perm, in0=tq.rearrange("p (e h d) -> p e h d", e=2, h=4), in1=uq.rearrange("p (e h d) -> p e h d", e=2, h=4), op=ALU.add),
             r=["tq", "uq"], w=["qr"])

    def proj_sgu():
        b2 = nb()
        zmm(2, b2)
        S.op("act", lambda e: e.activation(out=usb, in_=bank(b2), func=AF.Gelu), r=["b%d" % b2], w=["usb"])

    def proj_sgv():
        b3 = nb()
        zmm(3, b3)
        S.op("act", lambda e: e.activation(out=gvs, in_=bank(b3), func=AF.Gelu, accum_out=stat(2)), r=["b%d" % b3], w=["gvs", "lsum"])
        S.op("dve", lambda e: e.tensor_scalar(out=stat(3), in0=stat(2), scalar1=-1.0 / 512.0, scalar2=None, op0=ALU.mult), r=["lsum"], w=["lnm"])
        S.op("dve", lambda e: e.tensor_scalar(out=gvs, in0=gvs, scalar1=stat(3), scalar2=None, op0=ALU.add), r=["gvs", "lnm"], w=["gvs"])
        S.op("act", lambda e: e.activation(out=junk[:, 0:512], in_=gvs, func=AF.Square, scale=float(512 ** -0.5), accum_out=stat(4)), r=["gvs"], w=["junk", "lms"])
        rstd_from(stat(4), stat(5), EPS, "l")
        S.op("dve", lambda e: e.scalar_tensor_tensor(out=gvs, in0=gvs, scalar=stat(5), in1=lng, op0=ALU.mult, op1=ALU.mult), r=["gvs", "lrs", "lng"], w=["gvs"])
        S.op("dve", lambda e: e.tensor_tensor(out=gvs, in0=gvs, in1=lnb, op=ALU.add), r=["gvs", "lnb"], w=["gvs"])
        S.op("act", lambda e: e.copy(out=vnb, in_=gvs), r=["gvs"], w=["vnb"])

    def proj_mq():
        b4 = nb()
        zmm(4, b4)
        S.op("act", lambda e: e.copy(out=mqb, in_=bank(b4)), r=["b%d" % b4], w=["mqb"])

    def proj_gates(lo, hi):
        for gi in range(lo, hi):
            bg = nb()
            zmm(5 + gi, bg)
            S.op("act", lambda e, gi=gi, bg=bg: e.activation(out=tg[:, gi * 512:(gi + 1) * 512], in_=bank(bg), func=AF.Tanh, scale=0.5), r=["b%d" % bg], w=["tg%d" % gi])

    def kv_publish(slot_kv):
        transposes([kbb], ["kbb"], kTring[:, slot_kv, :], ["kT%d" % slot_kv], BF16)
        S.op("act", lambda e: e.copy(out=vring[:, slot_kv, :], in_=vf), r=["vf"], w=["v%d" % slot_kv])

    def softmax_direct(src_res, src_ap, scale, sink_cols, dst32, dst32_res, dstb, dstb_res, so, sx, nh):
        M, NM, SM, DF, RI = so + 8, so + 8 + nh, so + 8 + 2 * nh, so + 8 + 3 * nh, so + 8 + 4 * nh
        src_res = list(src_res)
        S.op("dve", lambda e: e.tensor_reduce(out=stat(M, nh), in_=src_ap, axis=AX.X, op=ALU.max), r=src_res, w=["smx" + sx])
        if sink_cols is not None:
            S.op("dve", lambda e: e.scalar_tensor_tensor(out=stat(NM, nh), in0=stat(M, nh), scalar=-scale, in1=nsnk[:, sink_cols:sink_cols + nh], op0=ALU.mult, op1=ALU.min),
                 r=["smx" + sx, "nsnk"], w=["snm" + sx])
        else:
            S.op("dve", lambda e: e.tensor_scalar(out=stat(NM, nh), in0=stat(M, nh), scalar1=-scale, scalar2=None, op0=ALU.mult), r=["smx" + sx], w=["snm" + sx])

        def fexp(e):
            ins = None
            for h in range(nh):
                ins = e.activation(out=dst32[:, h, :], in_=src_ap[:, h, :], func=AF.Exp, bias=stat(NM + h), scale=scale, accum_out=stat(SM + h))
            return ins
        S.op("act", fexp, r=src_res + ["snm" + sx], w=[dst32_res, "ssum" + sx])
        if sink_cols is not None:
            S.op("dve", lambda e: e.tensor_tensor(out=stat(DF, nh), in0=snk[:, sink_cols:sink_cols + nh], in1=stat(NM, nh), op=ALU.add), r=["snk", "snm" + sx], w=["sdf" + sx])
            S.op("act", lambda e: e.activation(out=stat(DF, nh), in_=stat(DF, nh), func=AF.Exp), r=["sdf" + sx], w=["sdf" + sx])
            S.op("dve", lambda e: e.tensor_tensor(out=stat(SM, nh), in0=stat(SM, nh), in1=stat(DF, nh), op=ALU.add), r=["ssum" + sx, "sdf" + sx], w=["ssum" + sx])
        S.op("dve", lambda e: e.reciprocal(out=stat(RI, nh), in_=stat(SM, nh)), r=["ssum" + sx], w=["srin" + sx])
        rin = stat(RI, nh)
        rb = bass.AP(rin.tensor, rin.offset, [list(rin.ap[0]), [1, nh], [0, 256]])
        S.op("dve", lambda e: e.tensor_tensor(out=dstb, in0=dst32, in1=rb, op=ALU.mult), r=[dst32_res, "srin" + sx], w=[dstb_res])

    def softmax4(src_res, src_ap, mask_ap, mask_res, scale, sink_cols, dst32, dst32_res, dstb, dstb_res, so=44, sx="", nh=4):
        M, NM, SM, DF, RI = so + 8, so + 8 + nh, so + 8 + 2 * nh, so + 8 + 3 * nh, so + 8 + 4 * nh
        if mask_ap is not None:
            mk = bass.AP(mask_ap.tensor, mask_ap.offset, [list(mask_ap.ap[0]), [0, nh], [1, 256]])
            S.op("dve", lambda e: e.scalar_tensor_tensor(out=dst32, in0=src_ap, scalar=scale, in1=mk, op0=ALU.mult, op1=ALU.add),
                 r=list(src_res) + [mask_res], w=[dst32_res])
        else:
            S.op("act", lambda e: e.activation(out=dst32, in_=src_ap, func=AF.Identity, scale=scale), r=list(src_res), w=[dst32_res])
        S.op("dve", lambda e: e.tensor_reduce(out=stat(M, nh), in_=dst32, axis=AX.X, op=ALU.max), r=[dst32_res], w=["smx" + sx])
        if sink_cols is not None:
            S.op("dve", lambda e: e.tensor_tensor(out=stat(M, nh), in0=stat(M, nh), in1=snk[:, sink_cols:sink_cols + nh], op=ALU.max), r=["smx" + sx, "snk"], w=["smx" + sx])
        S.op("dve", lambda e: e.tensor_scalar(out=stat(NM, nh), in0=stat(M, nh), scalar1=-1.0, scalar2=None, op0=ALU.mult), r=["smx" + sx], w=["snm" + sx])

        def fexp(e):
            ins = None
            for h in range(nh):
                ins = e.activation(out=dst32[:, h, :], in_=dst32[:, h, :], func=AF.Exp, bias=stat(NM + h), scale=1.0, accum_out=stat(SM + h))
            return ins
        S.op("act", fexp, r=[dst32_res, "snm" + sx], w=[dst32_res, "ssum" + sx])
        if sink_cols is not None:
            S.op("dve", lambda e: e.tensor_tensor(out=stat(DF, nh), in0=snk[:, sink_cols:sink_cols + nh], in1=stat(NM, nh), op=ALU.add), r=["snk", "snm" + sx], w=["sdf" + sx])
            S.op("act", lambda e: e.activation(out=stat(DF, nh), in_=stat(DF, nh), func=AF.Exp), r=["sdf" + sx], w=["sdf" + sx])
            S.op("dve", lambda e: e.tensor_tensor(out=stat(SM, nh), in0=stat(SM, nh), in1=stat(DF, nh), op=ALU.add), r=["ssum" + sx, "sdf" + sx], w=["ssum" + sx])
        S.op("dve", lambda e: e.reciprocal(out=stat(RI, nh), in_=stat(SM, nh)), r=["ssum" + sx], w=["srin" + sx])
        rin = stat(RI, nh)
        rb = bass.AP(rin.tensor, rin.offset, [list(rin.ap[0]), [1, nh], [0, 256]])
        S.op("dve", lambda e: e.tensor_tensor(out=dstb, in0=dst32, in1=rb, op=ALU.mult), r=[dst32_res, "srin" + sx], w=[dstb_res])

    OB = 3

    def attn_q_transposes():
        transposes([qr[:, h * 128:(h + 1) * 128] for h in range(4)], ["qr"], flat(qT), ["qT"], BF16)

    def attn_scores8(cur, mask8, mask8_res):
        prev = 1 - cur

        def fs(e):
            ins = None
            for e_ in range(2):
                for h in range(4):
                    for kbi, sl in enumerate((prev, cur)):
                        c0 = 4 * 512 + (e_ * 4 + h) * 256 + kbi * 128
                        e.matmul(PS[:, c0:c0 + 128], lhsT=qT[e_ * 64:(e_ + 1) * 64, h, :], rhs=kTring[e_ * 64:(e_ + 1) * 64, sl, :],
                                 start=(h % 2 == 0 and kbi == 0), stop=False, skip_group_check=True)
            for hh in range(8):
                c0 = 4 * 512 + hh * 256
                ins = e.matmul(PS[:, c0:c0 + 256], lhsT=idb, rhs=mask8, start=False, stop=True, skip_group_check=True)
            return ins
        S.op("pe", fs, r=["qT", "kT0", "kT1", "idb", mask8_res], w=["b4", "b5", "b6", "b7"])

    def attn_softmax8():
        sc = PS[:, 4 * 512:8 * 512].rearrange("p (h k) -> p h k", h=8)
        softmax_direct(["b4", "b5", "b6", "b7"], sc, 0.125, 0, ssb8, "ssb", pb8, "pb", so=0, sx="8", nh=8)

    def attn_pv8(cur):
        prev = 1 - cur
        transposes([pb8[:, hh, kbi * 128:(kbi + 1) * 128] for hh in range(4) for kbi in range(2)], ["pb"], flat(pT8[:, 0:4]), ["pT"], BF16)
        transposes([pb8[:, hh, kbi * 128:(kbi + 1) * 128] for hh in range(4, 8) for kbi in range(2)], ["pb"], flat(pT8[:, 4:8]), ["pT2"], BF16)

        def fpv(e):
            ins = None
            for hh in range(8):
                e_ = hh // 4
                cc, par = hh // 2, hh % 2
                for kbi, sl in enumerate((prev, cur)):
                    ins = e.matmul(bank(OB)[par * 64:(par + 1) * 64, cc * 128:(cc + 1) * 128], lhsT=vring[:, sl, e_ * 64:(e_ + 1) * 64],
                                   rhs=pT8[:, hh, kbi, :], start=(kbi == 0), stop=(kbi == 1), skip_group_check=True)
            return ins
        S.op("pe", fpv, r=["pT", "pT2", "v0", "v1"], w=["b%d" % OB])

    def attn_evac():
        S.op("act", lambda e: e.copy(out=flat(brT[0]), in_=bank(OB)), r=["b%d" % OB], w=["brT0"])

    def mem_scores_softmax_only():
        sc = PS[:, 6 * 512:8 * 512].rearrange("p (h k) -> p h k", h=4)
        softmax_direct(["b6", "b7"], sc, float(128 ** -0.5), None, em, "ssb", pmb, "pb", so=44, sx="", nh=4)

    def mem_scores_softmax():
        mem_scores_softmax_only()
        transposes([pmb[:, h, mbi * 128:(mbi + 1) * 128] for h in range(4) for mbi in range(2)], ["pb"], flat(pmT), ["pT"], BF16)

    def mem_scores():
        transposes([mqb[:, h * 128:(h + 1) * 128] for h in range(4)], ["mqb"], flat(mqT), ["mqT"], BF16)

        def fs(e):
            ins = None
            for h in range(4):
                ins = e.matmul(PS[:, 6 * 512 + h * 256: 6 * 512 + (h + 1) * 256], lhsT=mqT[:, h, :], rhs=mkT[:, h, :], start=True, stop=True)
            return ins
        S.op("pe", fs, r=["mqT", "mkT"], w=["b6", "b7"])
        mem_scores_softmax_only()

    def mem_pv():
        transposes([pmb[:, h, mbi * 128:(mbi + 1) * 128] for h in range(4) for mbi in range(2)], ["pb"], flat(pmT), ["pT"], BF16)

        def fpv(e):
            ins = None
            for h in range(4):
                for mbi in range(2):
                    ins = e.matmul(bank(OB)[:, h * 128:(h + 1) * 128], lhsT=mvb[:, mbi, h * 128:(h + 1) * 128], rhs=pmT[:, h, mbi, :],
                                   start=(mbi == 0), stop=(mbi == 1), skip_group_check=True)
            return ins
        S.op("pe", fpv, r=["pT", "mvb"], w=["b%d" % OB])
        S.op("act", lambda e: e.copy(out=flat(brT[2]), in_=bank(OB)), r=["b%d" % OB], w=["brT2"])

    def sg_mix(wt, wt_res, bias, bias_res):
        b = nb()

        def f(e):
            ins = None
            for g in range(4):
                ins = e.matmul(bank(b)[:, g * 128:(g + 1) * 128], lhsT=wt[:, g, :], rhs=vnb[:, g * 128:(g + 1) * 128], start=True, stop=True)
            return ins
        S.op("pe", f, r=["vnb", wt_res], w=["b%d" % b])

        def fo(e):
            ins = None
            for g in range(4):
                ins = e.scalar_tensor_tensor(out=sgo[:, g * 128:(g + 1) * 128], in0=bank(b)[:, g * 128:(g + 1) * 128], scalar=bias[:, g:g + 1],
                                             in1=usb[:, g * 128:(g + 1) * 128], op0=ALU.add, op1=ALU.mult)
            return ins
        S.op("dve", fo, r=["b%d" % b, "usb", bias_res], w=["sgo"])
        transposes([sgo[:, c * 128:(c + 1) * 128] for c in range(4)], ["sgo"], flat(brT[1]), ["brT1"], BF16)

    def proj_branch(br):
        for half in range(2):
            b = nb()

            def f(e, half=half, b=b):
                ins = None
                for cc in range(4):
                    ins = e.matmul(bank(b), lhsT=brT[br][:, cc, :], rhs=WO[:, br * 4 + cc, half * 512:(half + 1) * 512], start=(cc == 0), stop=(cc == 3))
                return ins
            S.op("pe", f, r=["brT%d" % br, "wo%d" % br], w=["b%d" % b])
            gi = br * 2 + half
            ah = acc[:, half * 512:(half + 1) * 512]
            th = tmp2[:, half * 512:(half + 1) * 512]
            if br == 0:
                S.op("dve", lambda e, gi=gi, b=b, ah=ah: e.scalar_tensor_tensor(out=ah, in0=tg[:, gi * 512:(gi + 1) * 512], scalar=1.0, in1=bank(b), op0=ALU.add, op1=ALU.mult),
                     r=["tg%d" % gi, "b%d" % b], w=["acc%d" % half])
            else:
                S.op("dve", lambda e, gi=gi, b=b, th=th: e.scalar_tensor_tensor(out=th, in0=tg[:, gi * 512:(gi + 1) * 512], scalar=1.0, in1=bank(b), op0=ALU.add, op1=ALU.mult),
                     r=["tg%d" % gi, "b%d" % b], w=["tmp%d" % half])
                S.op("dve", lambda e, ah=ah, th=th: e.tensor_tensor(out=ah, in0=ah, in1=th, op=ALU.add), r=["acc%d" % half, "tmp%d" % half], w=["acc%d" % half])

    def back2(slot, x1row, store_eng="pool"):
        xin = XIN[slot]
        xr = "xin%d" % slot
        S.op("act", lambda e: e.activation(out=junk, in_=acc, func=AF.Square, scale=1.0 / 32.0, accum_out=stat(6)), r=["acc0", "acc1"], w=["junk", "pms"])
        rstd_from(stat(6), stat(7), 4.0 * EPS, "p")
        S.op("dve", lambda e: e.scalar_tensor_tensor(out=acc, in0=acc, scalar=stat(7), in1=gpost, op0=ALU.mult, op1=ALU.mult), r=["acc0", "acc1", "prs", "gpost"], w=["acc0", "acc1"])
        S.op("dve", lambda e: e.tensor_tensor(out=xin, in0=xin, in1=acc, op=ALU.add), r=[xr, "acc0", "acc1"], w=[xr])
        dma(store_eng, x1s[x1row * 128:(x1row + 1) * 128, :], xin, r=[xr], w=["x1s%d" % x1row],
            stream="d_x1o%d%s" % (slot, "" if store_eng == "pool" else "h"))

    rot["n"] = 3
    rot["i"] = 0
    head(xp[0:128, :], 0, 0)
    proj_kv(0)
    kv_publish(0)
    head(xp[128:256, :], 1, 1)
    proj_kv(1)
    proj_q(1)
    for bi in range(1, NPB):
        slot = bi % 2
        kv_publish(slot)
        if bi == NPB - 1:
            out_toks.append(dma("pool", wkp, kr, r=["kr"], w=["o_wkp"], stream="d_o4"))
            out_toks.append(dma("pool", wvp, vf, r=["vf"], w=["o_wvp"], stream="d_o5"))
        mk_ap, mk_res = (maskf8, "maskf8") if bi == 2 else (maskp8, "maskp8")
        attn_q_transposes()
        attn_scores8(slot, mk_ap, mk_res)
        attn_softmax8()
        proj_sgu()
        proj_sgv()
        proj_mq()
        proj_gates(0, 4)
        attn_pv8(slot)
        sg_mix(WT, "WT", sgbT, "sgbT")
        attn_evac()
        mem_scores()
        proj_gates(4, 6)
        proj_branch(0)
        proj_branch(1)
        mem_pv()
        proj_branch(2)
        nslot = 1 - slot
        if bi + 1 < NPB:
            head(xp[(bi + 1) * 128:(bi + 2) * 128, :], nslot, bi + 1)
        else:
            head(xs, nslot, NPB)
        proj_kv(nslot)
        proj_q(nslot)
        back2(slot, bi - 1)
        if KSTOP == 3 and bi == 2:
            return finish()
    if KSTOP == 4:
        return finish()

    proj_sgu()
    proj_sgv()
    proj_mq()
    proj_gates(0, 6)
    out_toks.append(dma("sp", sgv, gvs, r=["gvs"], w=["o_sgv"], stream="d_o6"))
    out_toks.append(dma("sp", wks[:, 0:120, :], cwk[:, 8:128, :], w=["o_wks_a"], stream="d_o7"))
    out_toks.append(dma("sp", wvs[:, 0:120, :], cwv[:, 8:128, :], w=["o_wvs_a"], stream="d_o8"))
    for q in range(SEQS):
        out_toks.append(dma("sp", wks[q, 120:128, :], kr[q * 8:(q + 1) * 8, :], r=["kr"], w=["o_wks_b%d" % q], stream="d_o9"))
        out_toks.append(dma("sp", wvs[q, 120:128, :], vf[q * 8:(q + 1) * 8, :], r=["vf"], w=["o_wvs_b%d" % q], stream="d_o10"))
    if KSTOP == 5:
        return finish()
    S.barrier()
    SB = Arena.__new__(Arena)
    SB.t = A.t
    SB.off = WA_OFF
    SB.cap = WA_OFF + 8 * 5632 * 2
    kqT = SB.carve([SEQS, 4, 256], BF16)
    Zq = [SB.carve([SEQS, 128], BF16) for _ in range(2)]
    Zs = [SB.carve([SEQS, 128], BF16) for _ in range(2)]
    cwkb = SB.carve([SEQS, 128], BF16)
    cwkT = SB.carve([SEQS, 128], BF16)
    cwvb = SB.carve([SEQS, 128], BF16)
    NST = 4
    kst = [SB.carve([2, 512], BF16) for _ in range(NST)]
    vst = [SB.carve([2, 512], BF16) for _ in range(NST)]

    for z in range(2):
        S.op("dve", lambda e, z=z: e.memset(flat(Zq[z]), 0.0), w=["Zq%d" % z])
        S.op("dve", lambda e, z=z: e.memset(flat(Zs[z]), 0.0), w=["Zs%d" % z])
    def k_load(q):
        dma("pool", kst[q % NST], cmk[q].rearrange("(mb m) c -> m mb c", mb=2), w=["kst%d" % (q % NST)], stream="d_sk%d" % (q % NST))

    def v_load(q):
        dma("pool", vst[q % NST], cmv[q].rearrange("(mb m) c -> m mb c", mb=2), w=["vst%d" % (q % NST)], stream="d_sv%d" % (q % NST))

    for q in range(NST):
        k_load(q)
    dma("pool", cwkb, cwk.rearrange("q j c -> j q c"), w=["cwkb"], stream="d_s0")
    dma("pool", cwvb, cwv.rearrange("q j c -> j q c"), w=["cwvb"], stream="d_s1")
    for half in range(2):
        transposes([cwkb[:, half * 8 + i, :] for i in range(8)], ["cwkb"], cwkT[:, half * 8:(half + 1) * 8, :], ["cwkT%d" % half], BF16)
    for q in range(SEQS):
        st = kst[q % NST]
        transposes([st[:, mbi, h * 128:(h + 1) * 128] for h in range(4) for mbi in range(2)], ["kst%d" % (q % NST)],
                   kqT[:, q], ["kqT%d" % q], BF16)
        if q + NST < SEQS:
            k_load(q + NST)
    for q in range(NST):
        v_load(q)

    kv_publish(1)
    transposes([qr[:, h * 128:(h + 1) * 128] for h in range(4)], ["qr"], flat(qT), ["qT"], BF16)
    transposes([mqb[:, h * 128:(h + 1) * 128] for h in range(4)], ["mqb"], flat(mqT), ["mqT"], BF16)

    def diag_fill(dst, src):
        d = bass.AP(dst.tensor, dst.offset, [list(dst.ap[0]), [136, 16], [1, 8]])
        s = src.rearrange("p (q t) -> p q t", q=16)
        return d, s

    ob = OB
    for grp in range(2):
        e_ = grp
        for h in range(4):
            z = Zs[h % 2]
            zr = "Zs%d" % (h % 2)
            d_, s_ = diag_fill(z, qT[:, h, :])
            S.op("dve", lambda e, d_=d_, s_=s_: e.tensor_copy(out=d_, in_=s_), r=["qT"], w=[zr])

            def fs(e, h=h, e_=e_, z=z):
                ins = None
                base = 6 * 512 + h * 256
                for q in range(SEQS):
                    ins = e.matmul(PS[:, base:base + 128], lhsT=z[e_ * 64:(e_ + 1) * 64, q, :], rhs=cwkT[e_ * 64:(e_ + 1) * 64, q, :],
                                   start=(q == 0 and h % 2 == 0), stop=(q == SEQS - 1), skip_group_check=True)
                ins = e.matmul(PS[:, base + 128:base + 256], lhsT=qT[e_ * 64:(e_ + 1) * 64, h, :], rhs=kTring[e_ * 64:(e_ + 1) * 64, 1, :],
                               start=False, stop=True, skip_group_check=True)
                return ins
            S.op("pe", fs, r=[zr, "cwkT0", "cwkT1", "qT", "kT1"], w=["b6", "b7"])
        sc = PS[:, 6 * 512:8 * 512].rearrange("p (h k) -> p h k", h=4)
        softmax4(["b6", "b7"], sc, masks, "masks", 0.125, grp * 4, ssb, "ssb", pb, "pb")
        transposes([pb[:, h, kbi * 128:(kbi + 1) * 128] for h in range(4) for kbi in range(2)], ["pb"], flat(pT), ["pT"], BF16)

        def fpv(e, e_=e_):
            ins = None
            for h in range(4):
                hh = e_ * 4 + h
                cc, par = hh // 2, hh % 2
                o = bank(ob)[par * 64:(par + 1) * 64, cc * 128:(cc + 1) * 128]
                ins = e.matmul(o, lhsT=vring[:, 1, e_ * 64:(e_ + 1) * 64], rhs=pT[:, h, 1, :], start=True, stop=False, skip_group_check=True)
                for q in range(SEQS):
                    ins = e.matmul(o[:, q * 8:(q + 1) * 8], lhsT=cwvb[:, q, e_ * 64:(e_ + 1) * 64], rhs=pT[:, h, 0, q * 8:(q + 1) * 8],
                                   start=False, stop=(q == SEQS - 1), skip_group_check=True)
            return ins
        S.op("pe", fpv, r=["pT", "v1", "cwvb"], w=["b%d" % ob])
    S.op("act", lambda e: e.copy(out=flat(brT[0]), in_=bank(ob)), r=["b%d" % ob], w=["brT0"])

    sg_mix(WTS, "WTS", sgbS, "sgbS")

    for h in range(4):
        z = Zq[h % 2]
        zr = "Zq%d" % (h % 2)
        d_, s_ = diag_fill(z, mqT[:, h, :])
        S.op("dve", lambda e, d_=d_, s_=s_: e.tensor_copy(out=d_, in_=s_), r=["mqT"], w=[zr])

        def fs(e, h=h, z=z):
            ins = None
            base = 6 * 512 + h * 256
            for q in range(SEQS):
                ins = e.matmul(PS[:, base:base + 256], lhsT=z[:, q, :], rhs=kqT[:, q, h, :], start=(q == 0 and h % 2 == 0), stop=(q == SEQS - 1), skip_group_check=True)
            return ins
        S.op("pe", fs, r=[zr] + ["kqT%d" % q for q in range(SEQS)], w=["b6", "b7"])
    mem_scores_softmax()
    ob2 = OB
    for q in range(SEQS):
        st = vst[q % NST]

        def fpv(e, q=q, st=st):
            ins = None
            for h in range(4):
                for mbi in range(2):
                    ins = e.matmul(bank(ob2)[:, h * 128 + q * 8: h * 128 + (q + 1) * 8], lhsT=st[:, mbi, h * 128:(h + 1) * 128], rhs=pmT[:, h, mbi, q * 8:(q + 1) * 8],
                                   start=(q == 0 and h == 0 and mbi == 0), stop=(mbi == 1), skip_group_check=True)
            return ins
        S.op("pe", fpv, r=["pT", "vst%d" % (q % NST)], w=["b%d" % ob2])
        if q + NST < SEQS:
            v_load(q + NST)
    S.op("act", lambda e: e.copy(out=flat(brT[2]), in_=bank(ob2)), r=["b%d" % ob2], w=["brT2"])
    fence = [(st_, v_) for st_, v_ in S.cnt.items() if v_ > 0 and (st_ in ("pe", "act", "dve", "pool") or st_.startswith("d_s"))]
    w_up_r = w_up.rearrange("(k p) e -> p k e", p=128)
    for c in (0, 5, 6, 1, 7, 2, 8, 3, 9, 4, 10):
        dma("pool", WUP[:, :, c * 512:(c + 1) * 512], w_up_r[:, :, c * 512:(c + 1) * 512], w=["wup%d" % c], stream="d_WA%d" % c, after=fence)
    proj_branch(0)
    proj_branch(1)
    proj_branch(2)
    back2(0, NOWN + 1, store_eng="sp")
    if KSTOP == 6:
        return finish()

    S.barrier(keep_streams=("d_WA",), keep_res=("wup",))

    rot["n"] = 4
    rot["i"] = 0
    A.off = P_MARK
    gpf = A.carve([D], F32)
    gqf = A.carve([D], F32)
    P2_MARK = A.off
    T = {}

    def carve_p2(ntok):
        A.off = P2_MARK
        nblk = ntok // 128
        T["X1"] = [A.carve([nblk, D], F32) for _ in range(2)]
        T["h2"] = [A.carve([D], BF16) for _ in range(2)]
        T["junk2"] = None
        T["st2"] = A.carve([16], F32)
        T["h2T"] = [A.carve([8, ntok + 2], BF16) for _ in range(2 if ntok > 128 else 1)]
        T["actT"] = A.carve([NFC, ntok], BF16)
        T["EXT"] = [[A.carve([ntok + 4 if ntok > 128 else 160], F32) for _ in range(2)] for _ in range(2)]
        T["CG"] = [A.carve([ntok], F32) for _ in range(2)]
        T["CV"] = [A.carve([ntok], F32) for _ in range(2)]
        T["GG"] = [A.carve([ntok], F32) for _ in range(2)]
        T["fsb"] = A.carve([ntok // 128, D], F32)
        T["ysb"] = None
        if ntok > 128:
            T["osb"] = T["fsb"][:, 0, 0:128]
            T["osb_res"] = "fsb00"
        else:
            T["osb"] = A.carve([128], F32)
            T["osb_res"] = "osb"

    carve_p2(128)
    XH = A.carve([D], F32)
    scvr4 = [A.carve([1408], F32) for _ in range(2)]
    osb11 = scvr4[0].rearrange("p (a c) -> p a c", a=11)
    scvT = A.carve([44, 32], F32)
    ncs = A.carve([1408], F32)

    dma("sp", gpf, g_pffn.partition_broadcast(128), w=["gpf"], stream="d_c3")
    dma("sp", gqf, g_qffn.partition_broadcast(128), w=["gqf"], stream="d_c4")
    w_dn_r = w_down.rearrange("(k p) e -> p k e", p=128)
    for c4, (f0, f1) in enumerate(((0, 4), (4, 10), (10, 16), (16, 22))):
        dma("pool", WDN[:, f0:f1, :], w_dn_r[:, f0:f1, :], w=["wdn%d" % c4], stream="d_WB%d" % c4)

    for cg_ in range(4):
        scvr = scvr4[cg_ % 2]
        dma("sp", scvr[0:32, :], scv[:, cg_ * 1408:(cg_ + 1) * 1408], w=["scvr%d" % (cg_ % 2)], stream="d_c%d" % (7 + cg_ % 2))
        b = nb()

        def f(e, b=b, scvr=scvr):
            ins = None
            for i in range(11):
                ins = e.transpose(out=bank(b)[:, i * 32:(i + 1) * 32], in_=scvr[0:32, i * 128:(i + 1) * 128], identity=idf[0:32, 0:32])
            return ins
        S.op("pe", f, r=["scvr%d" % (cg_ % 2), "idf"], w=["b%d" % b])
        S.op("act", lambda e, b=b, cg_=cg_: e.copy(out=scvT[:, cg_ * 11:(cg_ + 1) * 11, :], in_=bank(b)[:, 0:352].rearrange("p (a b) -> p a b", a=11)), r=["b%d" % b], w=["scvT"])

    def ffn_front_a(nrows, xt_ap, xres, j):
        h2, st2 = T["h2"][j], T["st2"]
        c = 4 * j
        S.op("act", lambda e: e.activation(out=h2[0:nrows], in_=xt_ap[0:nrows], func=AF.Square, scale=1.0 / 32.0, accum_out=st2[0:nrows, c:c + 1]), r=[xres], w=["h2_%d" % j, "fms%d" % j])
        S.op("pool", lambda e: e.tensor_scalar(out=st2[0:nrows, c + 1:c + 2], in0=st2[0:nrows, c:c + 1], scalar1=EPS, scalar2=None, op0=ALU.add), r=["fms%d" % j], w=["frs%d" % j])
        S.op("pool", lambda e: e.tensor_tensor(out=st2[0:nrows, c + 1:c + 2], in0=st2[0:nrows, c + 1:c + 2], in1=negh[0:nrows], op=ALU.pow), r=["frs%d" % j, "negh"], w=["frs%d" % j])
        S.op("dve", lambda e: e.scalar_tensor_tensor(out=h2[0:nrows], in0=xt_ap[0:nrows], scalar=st2[0:nrows, c + 1:c + 2], in1=gpf[0:nrows], op0=ALU.mult, op1=ALU.mult),
             r=[xres, "frs%d" % j, "gpf"], w=["h2_%d" % j])

    def ffn_front_b(nrows, col0, j, hsel=0):
        h2, h2T = T["h2"][j], T["h2T"][hsel]
        hres = "h2T%d" % hsel
        b = nb()
        pv = bankb(b)

        def f(e):
            ins = None
            for k in range(8):
                ins = e.transpose(out=pv[:, k * 128:k * 128 + nrows], in_=h2[0:nrows, k * 128:(k + 1) * 128], identity=idb[0:nrows, 0:nrows])
            return ins
        S.op("pe", f, r=["h2_%d" % j, "idb"], w=["b%d" % b])
        src = pv[:, 0:1024].rearrange("p (k t) -> p k t", k=8)[:, :, 0:nrows]
        S.op("act", lambda e: e.copy(out=h2T[:, :, col0:col0 + nrows], in_=src), r=["b%d" % b], w=[hres])

    def nb2():
        if rot["i"] % 2:
            nb()
        b = nb()
        nb()
        return b

    def ffn_tile(N, sample, x1_tiles, yout, hooks=None, hsel=0):
        actT, EXT, CG, CV, GG, fsb, ysb, st2, junk2 = (T[k] for k in ("actT", "EXT", "CG", "CV", "GG", "fsb", "ysb", "st2", "junk2"))
        h2T = T["h2T"][hsel]
        hres = "h2T%d" % hsel
        n_act = 128 if sample else N

        NTB = n_act // 128

        def down_slice(fc):
            def f(e):
                ins = None
                for j in range(NTB):
                    for half in range(2):
                        o = PS[:, (4 + 2 * j + half) * 512:(5 + 2 * j + half) * 512]
                        ins = e.matmul(o, lhsT=actT[:, fc, j * 128:(j + 1) * 128], rhs=WDN[:, fc, half * 512:(half + 1) * 512],
                                       start=(fc == 0), stop=(fc == NFC - 1))
                return ins
            S.op("pe", f, r=["actT%d" % fc, "wdn%d" % (0 if fc < 4 else 1 if fc < 10 else 2 if fc < 16 else 3)], w=["b%d" % (4 + i) for i in range(2 * NTB)])

        def tail_ops(fc):
            bi_ = fc % 2
            gg = GG[bi_]
            S.op("act", lambda e: e.activation(out=gg[:, 0:n_act], in_=CG[bi_][:, 0:n_act], func=AF.Gelu_apprx_tanh), r=["cg%d" % bi_], w=["gg%d" % bi_])
            S.op("pool", lambda e: e.tensor_tensor(out=actT[:, fc, 0:n_act], in0=gg[:, 0:n_act], in1=CV[bi_][:, 0:n_act], op=ALU.mult),
                 r=["gg%d" % bi_, "cv%d" % bi_], w=["actT%d" % fc])

        for fc in range(NFC):
            bi_ = fc % 2
            stage = []
            for gv in range(2):
                fcc = fc + gv * NFC
                b = nb()

                def f(e, fcc=fcc, b=b):
                    ins = None
                    for k in range(8):
                        ins = e.matmul(bank(b)[:, 0:N], lhsT=WUP[:, k, fcc * 128:(fcc + 1) * 128], rhs=h2T[:, k, 0:N], start=(k == 0), stop=(k == 7))
                    return ins
                S.op("pe", f, r=[hres, "wup%d" % (fcc // 4)], w=["b%d" % b])
                ext = EXT[bi_][gv]
                er = "ext%d%d" % (bi_, gv)
                erc = er + "c"
                cdst = (CG if gv == 0 else CV)[bi_]
                cres = ("cg%d" if gv == 0 else "cv%d") % bi_
                w0 = cwt[:, 0, fcc:fcc + 1]
                w1 = cwt[:, 1, fcc:fcc + 1]
                w2 = cwt[:, 2, fcc:fcc + 1]
                bb = cwt[:, 3, fcc:fcc + 1]
                if not sample:
                    S.op("pool", lambda e, ext=ext, fcc=fcc: e.tensor_copy(out=ext[:, 0:2], in_=carry[:, :, fcc]), r=["carry%d" % fcc], w=[erc])
                    S.op("act", lambda e, ext=ext, b=b: e.copy(out=ext[:, 2:2 + N], in_=bank(b)[:, 0:N]), r=["b%d" % b], w=[er])
                    S.op("pool", lambda e, ext=ext, fcc=fcc: e.tensor_copy(out=carry[:, :, fcc], in_=ext[:, N:N + 2]), r=[er], w=["carry%d" % fcc])
                    v2, v1, v0 = ext[:, 2:N + 2], ext[:, 1:N + 1], ext[:, 0:N]
                    cd = cdst[:, 0:N]
                else:
                    S.op("dve", lambda e, fcc=fcc, b=b: e.tensor_scalar(out=carry[:, :, fcc], in0=bank(b)[:, 0:2], scalar1=hvt, scalar2=None, op0=ALU.mult),
                         r=["b%d" % b, "hvt"], w=["carry%d" % fcc])
                    e3 = ext[:, 0:160].rearrange("p (q t) -> p q t", q=16)
                    S.op("pool", lambda e, e3=e3, fcc=fcc: e.tensor_copy(out=e3[:, :, 0:2], in_=scvT[:, fcc, :].rearrange("p (q i) -> p q i", q=16)), r=["scvT"], w=[erc])
                    S.op("act", lambda e, e3=e3, b=b: e.copy(out=e3[:, :, 2:10], in_=bank(b)[:, 2:130].rearrange("p (q t) -> p q t", q=16)), r=["b%d" % b], w=[er])
                    nview = bass.AP(ncs.tensor, ncs.offset + fcc, [list(ncs.ap[0]), [88, 16], [44, 2]])
                    S.op("pool", lambda e, e3=e3, nview=nview: e.tensor_copy(out=nview, in_=e3[:, :, 8:10]), r=[er], w=["ncs"])
                    v2, v1, v0 = e3[:, :, 2:10], e3[:, :, 1:9], e3[:, :, 0:8]
                    cd = cdst[:, 0:128].rearrange("p (q t) -> p q t", q=16)
                stage.append((cd, v2, v1, v0, w2, w1, w0, bb, er, erc, cres))
            for (cd, v2, v1, v0, w2, w1, w0, bb, er, erc, cres) in stage:
                S.op("act", lambda e, cd=cd, v2=v2, w2=w2, bb=bb: e.activation(out=cd, in_=v2, func=AF.Identity, scale=w2, bias=bb), r=[er, "cwt"], w=[cres])
            for (cd, v2, v1, v0, w2, w1, w0, bb, er, erc, cres) in stage:
                S.op("dve", lambda e, cd=cd, v1=v1, w1=w1: e.scalar_tensor_tensor(out=cd, in0=v1, scalar=w1, in1=cd, op0=ALU.mult, op1=ALU.add), r=[er, erc, cres, "cwt"], w=[cres])
                S.op("dve", lambda e, cd=cd, v0=v0, w0=w0: e.scalar_tensor_tensor(out=cd, in0=v0, scalar=w0, in1=cd, op0=ALU.mult, op1=ALU.add), r=[er, erc, cres, "cwt"], w=[cres])
            if fc > 0:
                tail_ops(fc - 1)
            if fc >= 2:
                down_slice(fc - 2)
            for hk in (hooks or {}).get(fc, ()):
                hk()
        tail_ops(NFC - 1)
        down_slice(NFC - 2)
        down_slice(NFC - 1)
        for j in range(NTB):
            for half in range(2):
                bk = 4 + 2 * j + half
                eng = "act" if half == 0 else "dve"
                if eng == "act":
                    S.op("act", lambda e, j=j, half=half, bk=bk: e.copy(out=fsb[:, j, half * 512:(half + 1) * 512], in_=bank(bk)), r=["b%d" % bk], w=["fsb%d%d" % (j, half)])
                else:
                    S.op("dve", lambda e, j=j, half=half, bk=bk: e.tensor_copy(out=fsb[:, j, half * 512:(half + 1) * 512], in_=bank(bk)), r=["b%d" % bk], w=["fsb%d%d" % (j, half)])

        def make_tail(j):
            def run():
                fres = ["fsb%d0" % j, "fsb%d1" % j]
                ftok = fsb[:, j, :]
                junk_ = T["h2"][j]
                c = 8 + 2 * j
                S.op("act", lambda e: e.activation(out=junk_, in_=ftok, func=AF.Square, scale=1.0 / 32.0, accum_out=st2[:, c:c + 1]), r=fres, w=["h2_%d" % j, "gms%d" % j])
                S.op("pool", lambda e: e.tensor_scalar(out=st2[:, c + 1:c + 2], in0=st2[:, c:c + 1], scalar1=EPS, scalar2=None, op0=ALU.add), r=["gms%d" % j], w=["grs%d" % j])
                S.op("pool", lambda e: e.tensor_tensor(out=st2[:, c + 1:c + 2], in0=st2[:, c + 1:c + 2], in1=negh, op=ALU.pow), r=["grs%d" % j, "negh"], w=["grs%d" % j])
                x1t, x1r = x1_tiles[j]
                S.op("dve", lambda e: e.scalar_tensor_tensor(out=ftok, in0=ftok, scalar=st2[:, c + 1:c + 2], in1=gqf, op0=ALU.mult, op1=ALU.mult), r=fres + ["grs%d" % j, "gqf"], w=fres)
                S.op("dve", lambda e: e.tensor_tensor(out=ftok, in0=ftok, in1=x1t, op=ALU.add), r=fres + [x1r], w=fres)
                dma("pool", yout[j], ftok, r=fres, w=["o_y"], stream="d_yo%d" % (j % 2))
            return run
        return [make_tail(j) for j in range(n_act // 128)]

    X1a = T["X1"][0][:, 0, :]
    dma("sp", XH[0:2, :], x1s[126:128, :], w=["XH"], stream="d_x2h")
    dma("sp", X1a, x1s[(NOWN + 1) * 128:(NOWN + 2) * 128, :], w=["X10a"], stream="d_x2a")
    ffn_front_a(2, XH, "XH", 0)
    ffn_front_b(2, 0, 0)
    ffn_front_a(128, X1a, "X10a", 1)
    ffn_front_b(128, 2, 1)
    for tl in ffn_tile(130, True, [(X1a, "X10a")], [ys]):
        tl()
    for g0 in range(0, 11, 4):
        n_ = min(4, 11 - g0)
        b = nb()

        def f(e, b=b, g0=g0, n_=n_):
            ins = None
            for i in range(n_):
                ins = e.transpose(out=bank(b)[:, i * 128:(i + 1) * 128], in_=ncs[:, (g0 + i) * 128:(g0 + i + 1) * 128], identity=idf)
            return ins
        S.op("pe", f, r=["ncs", "idf"], w=["b%d" % b])
        S.op("act", lambda e, b=b, g0=g0, n_=n_: e.copy(out=osb11[:, g0:g0 + n_, :], in_=bank(b)[:, 0:n_ * 128].rearrange("p (a c) -> p a c", a=n_)), r=["b%d" % b], w=["scvr0"])
    dma("pool", cvs.rearrange("(c p) f -> p c f", p=128), osb11, r=["scvr0"], w=["o_cvs"], stream="d_o11")
    if KSTOP == 7:
        return finish()
    S.barrier()
    carve_p2(256)
    osb = T["osb"]
    def p2_load(ti):
        xt = T["X1"][ti % 2]
        for j in range(2):
            row = 1 + ti * 2 + j
            dma("sp", xt[:, j, :], x1s[row * 128:(row + 1) * 128, :], w=["X1%d%s" % (ti % 2, "ab"[j])], stream="d_x2%d%s" % (ti % 2, "ab"[j]))

    def p2_fa(ti, j):
        return lambda: ffn_front_a(128, T["X1"][ti % 2][:, j, :], "X1%d%s" % (ti % 2, "ab"[j]), j)

    def p2_fb(ti, j):
        return lambda: ffn_front_b(128, j * 128, j, hsel=ti % 2)

    NT = NOWN // 2
    p2_load(0)
    for j in range(2):
        p2_fa(0, j)()
        p2_fb(0, j)()
    pending = []
    for ti in range(NT):
        xt = T["X1"][ti % 2]
        res = ["X1%d%s" % (ti % 2, "ab"[j]) for j in range(2)]
        hooks = {}
        if pending:
            hooks[1] = [pending[0]]
            hooks[3] = [pending[1]]
        if ti + 1 < NT:
            hooks[5] = [lambda ti=ti: p2_load(ti + 1)]
            hooks[7] = [p2_fa(ti + 1, 0)]
            hooks[9] = [p2_fa(ti + 1, 1)]
            hooks[13] = [p2_fb(ti + 1, 0)]
            hooks[15] = [p2_fb(ti + 1, 1)]
        pending = ffn_tile(256, False, [(xt[:, j, :], res[j]) for j in range(2)], [yp[(ti * 2 + j) * 128:(ti * 2 + j + 1) * 128, :] for j in range(2)],
                           hooks=hooks, hsel=ti % 2)
    for tl in pending:
        tl()
    b = nb()
    S.op("pe", lambda e, b=b: e.transpose(out=bank(b)[0:88, 0:128], in_=flat(carry), identity=idf), r=["carry%d" % f_ for f_ in range(44)] + ["idf"], w=["b%d" % b])
    S.op("act", lambda e, b=b, osb=osb: e.copy(out=osb[0:88, :], in_=bank(b)[0:88, 0:128]), r=["b%d" % b], w=[T["osb_res"]])
    dma("pool", cvp, osb[0:88, :], r=[T["osb_res"]], w=["o_cvp"], stream="d_o12")

    return finish()


_CACHE = {}


def _rope_tables():
    half = 32
    inv = (10000.0 ** (-np.arange(half, dtype=np.float32) / half)).astype(np.float32)
    return inv


def _host_consts(core):
    inv = _rope_tables()
    pos = np.zeros((NPB + 1, 128), np.float32)
    for b in range(NPB):
        pos[b] = core * TOK - 256 + b * 128 + np.arange(128)
    pos[NPB] = PAST + (np.arange(128) % 8)
    ang = (pos[:, :, None].astype(np.float32) * inv[None, None, :]).astype(np.float32)
    c = np.cos(ang).astype(np.float32)
    s = np.sin(ang).astype(np.float32)
    ropec = np.concatenate([c, c], axis=-1).astype(np.float32)
    ropes = np.concatenate([-s, s], axis=-1).astype(np.float32)
    i = np.arange(128)[:, None]
    j = np.arange(128)[None, :]
    maskp = np.concatenate([np.where(j > i, 0.0, NEG), np.where(j <= i, 0.0, NEG)], axis=1).astype(np.float32)
    maskf = maskp.copy()
    if core == 0:
        maskf[:, 0:128] = NEG
    NEG8 = 8.0 * NEG
    maskp8 = np.concatenate([np.where(j > i, 0.0, NEG8), np.where(j <= i, 0.0, NEG8)], axis=1).astype(np.float32)
    maskf8 = maskp8.copy()
    if core == 0:
        maskf8[:, 0:128] = NEG8
    t = (np.arange(128) % 8)[:, None]
    q = (np.arange(128) // 8)[:, None]
    tt = (np.arange(128) % 8)[None, :]
    qq = (np.arange(128) // 8)[None, :]
    masks = np.concatenate([np.where(j >= t + 1, 0.0, NEG), np.where((qq == q) & (tt <= t), 0.0, NEG)], axis=1).astype(np.float32)
    tril = (j <= i).astype(np.float32)
    bdm = ((qq == q) & (tt <= t)).astype(np.float32)
    e8 = np.tile(np.eye(8, dtype=np.float32), (1, 16))
    hv = np.full((128, 1), 0.0 if core == 0 else 1.0, np.float32)
    return dict(ropec=ropec, ropes=ropes, maskp=maskp, maskf=maskf, masks=masks, maskp8=maskp8, maskf8=maskf8, tril=tril, bdm=bdm, e8=e8, hv=hv,
                ident=np.eye(128, dtype=np.float32))


def kernel(x_prompt, x_sample, cache_win_k, cache_win_v, cache_mem_k, cache_mem_v, state_conv, mem_prompt,
           pre_mix_g, w_in, attn_sinks, sg_ln_g, sg_ln_b, sg_w, sg_b, mem_norm_g, w_mem_kv, w_o,
           post_mix_g, pre_ffn_g, w_up, conv_w, conv_b, w_down, post_ffn_g):
    f = lambda a: np.ascontiguousarray(np.asarray(a, dtype=np.float32))
    if "nc" not in _CACHE:
        _CACHE["nc"] = build_program()
    nc = _CACHE["nc"]
    xpr = f(x_prompt)[0]
    xpad = np.concatenate([np.zeros((256, D), np.float32), xpr], axis=0)
    shared = dict(
        memp=f(mem_prompt)[0], w_in=f(w_in)[0], w_mkv=f(w_mem_kv)[0], w_o=f(w_o)[0].reshape(1536, D), w_up=f(w_up)[0], w_down=f(w_down)[0],
        g_pre=f(pre_mix_g), g_post=f(post_mix_g), g_pffn=f(pre_ffn_g), g_qffn=f(post_ffn_g), g_mem=f(mem_norm_g),
        ln_g=f(sg_ln_g), ln_b=f(sg_ln_b), sinks=f(attn_sinks), sg_w=f(sg_w)[0], sg_b=f(sg_b)[0], conv_w=f(conv_w)[0], conv_b=f(conv_b),
    )
    in_maps = []
    for c in range(NCORES):
        m = dict(shared)
        m.update(_host_consts(c))
        m["xp"] = np.ascontiguousarray(xpad[c * TOK:c * TOK + NPB * 128])
        sl = slice(c * SEQS, (c + 1) * SEQS)
        m["xs"] = f(x_sample)[sl].reshape(128, D)
        m["cwk"] = f(cache_win_k)[0, sl].reshape(SEQS, 128, 128)
        m["cwv"] = f(cache_win_v)[0, sl].reshape(SEQS, 128, 128)
        m["cmk"] = f(cache_mem_k)[0, sl].reshape(SEQS, 256, 512)
        m["cmv"] = f(cache_mem_v)[0, sl].reshape(SEQS, 256, 512)
        m["scv"] = f(state_conv)[0, sl].reshape(32, 2 * DFF)
        in_maps.append(m)
    res = run_bass_kernel_spmd(nc, in_maps, core_ids=list(range(NCORES))).results
    y_p = np.concatenate([r["yp"] for r in res], axis=0)[None]
    y_s = np.concatenate([r["ys"].reshape(SEQS, 8, D) for r in res], axis=0)
    last = res[NCORES - 1]
    wk_p = last["wkp"].reshape(1, 1, 128, 2, 64)
    wv_p = last["wvp"].reshape(1, 1, 128, 2, 64)
    mk_p = res[0]["mkp"].reshape(1, 1, 256, 4, 128)
    mv_p = res[0]["mvp"].reshape(1, 1, 256, 4, 128)
    cv_p = last["cvp"].reshape(1, 1, 2, 2 * DFF)
    wk_s = np.concatenate([r["wks"] for r in res], axis=0).reshape(1, 128, 128, 2, 64)
    wv_s = np.concatenate([r["wvs"] for r in res], axis=0).reshape(1, 128, 128, 2, 64)
    sgv_s = np.concatenate([r["sgv"].reshape(SEQS, 8, 512) for r in res], axis=0)[None]
    cv_s = np.concatenate([r["cvs"].reshape(SEQS, 2, 2 * DFF) for r in res], axis=0)[None]
    outs = (y_p, y_s, wk_p, wv_p, mk_p, mv_p, cv_p, wk_s, wv_s, sgv_s, cv_s)
    return tuple(np.ascontiguousarray(o, dtype=np.float32) for o in outs)
```

```python
import numpy as np
import concourse.bass as bass
import concourse.mybir as mybir
from concourse.bass_utils import run_bass_kernel_spmd

F32 = mybir.dt.float32
BF16 = mybir.dt.bfloat16
AF = mybir.ActivationFunctionType
ALU = mybir.AluOpType
AX = mybir.AxisListType

NCORES = 8
D = 1024
SEQ = 16384
TOK = SEQ // NCORES
NOWN = TOK // 128
NPB = NOWN + 2
SEQS = 16
INW = 5376
DFF = 2816
NFC = 22
EPS = 1e-6
NEG = -30000.0
PAST = 16384
ZG = [(0, 512), (512, 256), (768, 512), (1280, 512), (1792, 512)] + [(2304 + 512 * i, 512) for i in range(6)]


class Sched:
    ENGS = ("pe", "act", "dve", "pool", "sp")

    def __init__(self, nc):
        self.nc = nc
        self.q = {e: [] for e in self.ENGS}
        self.sem = {}
        self.cnt = {}
        self.waited = {e: {} for e in self.ENGS}
        self.lastw = {}
        self.readers = {}
        self.snap = {}
        self.seq = 0
        for e in ("pe", "act", "dve", "pool"):
            self._mk(e)

    def _mk(self, name):
        self.sem[name] = self.nc.alloc_semaphore("s_" + name)
        self.cnt[name] = 0

    def op(self, eng, fn, r=(), w=(), dma=None, after=()):
        deps = {}

        def need(tok):
            if tok is None:
                return
            s, v = tok
            if deps.get(s, 0) < v:
                deps[s] = v

        w = list(w) + [x for x in r if len(x) == 2 and x[0] == "b" and x[1].isdigit()]
        r = [x for x in r if not (len(x) == 2 and x[0] == "b" and x[1].isdigit())]
        for x in r:
            need(self.lastw.get(x))
        for x in w:
            need(self.lastw.get(x))
            for t in self.readers.get(x, ()):
                need(t)
        for t in after:
            need(t)
        waits = []
        wd = self.waited[eng]
        for s, v in sorted(deps.items(), key=lambda kv: -self.snap.get(kv, (0, None))[0]):
            if s == "pe" and eng == "pe":
                continue
            if wd.get(s, 0) >= v:
                continue
            wd[s] = v
            waits.append((s, v))
            sn = self.snap.get((s, v))
            if sn is not None and sn[1] is not None:
                for k2, v2 in sn[1].items():
                    if wd.get(k2, 0) < v2:
                        wd[k2] = v2
        if dma is not None:
            if dma not in self.sem:
                self._mk(dma)
            stream, inc = dma, 16
        else:
            stream, inc = eng, 1
        self.cnt[stream] += inc
        tok = (stream, self.cnt[stream])
        self.seq += 1
        self.snap[tok] = (self.seq, dict(wd))
        self.q[eng].append((waits, fn, stream, inc))
        for x in w:
            self.lastw[x] = tok
            self.readers[x] = []
        for x in r:
            self.readers.setdefault(x, []).append(tok)
        return tok

    def barrier(self, keep_streams=(), keep_res=()):
        for e in self.ENGS:
            waits = []
            for s, v in self.cnt.items():
                if s.startswith(tuple(keep_streams)) if keep_streams else False:
                    continue
                if v > self.waited[e].get(s, 0):
                    self.waited[e][s] = v
                    waits.append((s, v))
            self.q[e].append((waits, None, None, 0))
        self.lastw = {k: v for k, v in self.lastw.items() if keep_res and k.startswith(tuple(keep_res))}
        self.readers = {}

    def emit(self):
        nc = self.nc
        handles = {"pe": "tensor", "act": "scalar", "dve": "vector", "pool": "gpsimd", "sp": "sync"}
        with nc.Block() as block:
            for en in self.ENGS:
                def make(en):
                    def f(eng):
                        for waits, fn, stream, inc in self.q[en]:
                            for s, v in waits:
                                eng.wait_ge(self.sem[s], v)
                            if fn is None:
                                continue
                            ins = fn(eng)
                            ins.then_inc(self.sem[stream], inc)
                    return f
                getattr(block, handles[en])(make(en))


class Arena:
    def __init__(self, nc, nbytes):
        self.t = nc.alloc_sbuf_tensor("arena", [128, nbytes // 4], F32)
        self.cap = nbytes
        self.off = 0

    def carve(self, shape_free, dtype):
        n = int(np.prod(shape_free))
        nb = n * (2 if dtype == BF16 else 4)
        nb = (nb + 31) // 32 * 32
        assert self.off + nb <= self.cap, ("SBUF arena overflow", self.off, nb, self.cap)
        ap = self.t[:, self.off // 4:(self.off + nb) // 4]
        self.off += nb
        if dtype == BF16:
            ap = ap.bitcast(BF16)
        ap = ap[:, 0:n]
        if len(shape_free) == 2:
            ap = ap.rearrange("p (a b) -> p a b", a=shape_free[0])
        elif len(shape_free) == 3:
            ap = ap.rearrange("p (a b c) -> p a b c", a=shape_free[0], b=shape_free[1])
        elif len(shape_free) == 4:
            ap = ap.rearrange("p (a b c d) -> p a b c d", a=shape_free[0], b=shape_free[1], c=shape_free[2])
        return ap


def flat(ap):
    n = len(ap.shape)
    if n == 2:
        return ap
    if n == 3:
        return ap.rearrange("p a b -> p (a b)")
    if n == 4:
        return ap.rearrange("p a b c -> p (a b c)")
    return ap.rearrange("p a b c d -> p (a b c d)")


def build_program():
    nc = bass.Bass("TRN2", target_bir_lowering=False)
    S = Sched(nc)

    def din(name, shape):
        return nc.dram_tensor(name, list(shape), F32, kind="ExternalInput").ap()

    def dout(name, shape):
        return nc.dram_tensor(name, list(shape), F32, kind="ExternalOutput").ap()

    xp = din("xp", [NPB * 128, D])
    xs = din("xs", [128, D])
    cwk = din("cwk", [SEQS, 128, 128])
    cwv = din("cwv", [SEQS, 128, 128])
    cmk = din("cmk", [SEQS, 256, 512])
    cmv = din("cmv", [SEQS, 256, 512])
    scv = din("scv", [32, 2 * DFF])
    memp = din("memp", [256, D])
    w_in = din("w_in", [D, INW])
    w_mkv = din("w_mkv", [D, 1024])
    w_o = din("w_o", [1536, D])
    w_up = din("w_up", [D, 2 * DFF])
    w_down = din("w_down", [DFF, D])
    g_pre = din("g_pre", [1, D])
    g_post = din("g_post", [1, D])
    g_pffn = din("g_pffn", [1, D])
    g_qffn = din("g_qffn", [1, D])
    g_mem = din("g_mem", [1, D])
    ln_g = din("ln_g", [1, 512])
    ln_b = din("ln_b", [1, 512])
    sinks = din("sinks", [1, 8])
    sg_w = din("sg_w", [4, 128, 128])
    sg_b = din("sg_b", [4, 128])
    conv_w = din("conv_w", [3, 2 * DFF])
    conv_b = din("conv_b", [1, 2 * DFF])
    ident = din("ident", [128, 128])
    ropec = din("ropec", [NPB + 1, 128, 64])
    ropes = din("ropes", [NPB + 1, 128, 64])
    maskp_d = din("maskp", [128, 256])
    maskf_d = din("maskf", [128, 256])
    maskp8_d = din("maskp8", [128, 256])
    maskf8_d = din("maskf8", [128, 256])
    masks_d = din("masks", [128, 256])
    tril_d = din("tril", [128, 128])
    bdm_d = din("bdm", [128, 128])
    e8_d = din("e8", [8, 128])
    hv_d = din("hv", [128, 1])

    yp = dout("yp", [TOK, D])
    ys = dout("ys", [128, D])
    wkp = dout("wkp", [128, 128])
    wvp = dout("wvp", [128, 128])
    mkp = dout("mkp", [256, 512])
    mvp = dout("mvp", [256, 512])
    cvp = dout("cvp", [88, 128])
    wks = dout("wks", [SEQS, 128, 128])
    wvs = dout("wvs", [SEQS, 128, 128])
    sgv = dout("sgv", [128, 512])
    cvs = dout("cvs", [1408, 128])
    x1s = nc.dram_tensor("x1s", [(NOWN + 2) * 128, D], F32).ap()

    A = Arena(nc, 212480)
    PS = nc.alloc_psum_tensor("PS", [128, 4096], F32)

    def bank(i):
        return PS[:, i * 512:(i + 1) * 512]

    def bankb(i):
        return bank(i).bitcast(BF16)

    rot = {"i": 0, "n": 6}

    def nb():
        i = rot["i"] % rot["n"]
        rot["i"] = (i + 1) % rot["n"]
        return i

    out_toks = []

    def dma(eng, out, in_, r=(), w=(), stream=None, after=()):
        return S.op(eng, lambda e: e.dma_start(out=out, in_=in_), r=r, w=w, dma=stream, after=after)

    import os
    KSTOP = int(os.environ.get("KSTOP", "99"))

    def finish():
        fin = {}
        for s_, v in S.cnt.items():
            if s_.startswith("d_"):
                fin[s_] = v
        S.q["sp"].append(([(s_, v) for s_, v in fin.items() if v > S.waited["sp"].get(s_, 0)], None, None, 0))
        S.emit()
        return nc

    idf = A.carve([128], F32)
    idb = A.carve([128], BF16)
    negh = A.carve([1], F32)
    hvt = A.carve([1], F32)
    carry = A.carve([2, 44], F32)
    cwt = A.carve([4, 44], F32)
    maskp8 = A.carve([256], BF16)
    maskf8 = A.carve([256], BF16)
    WA_OFF = A.off
    WA = A.carve([8, 5632], BF16)
    WB_OFF = A.off
    WB = A.carve([22, 1024], BF16)
    WIN = flat(WA)[:, 0:8 * INW].rearrange("p (k e) -> p k e", k=8)
    WO = WB[:, 0:12, :]
    WMKV = WB[:, 12:20, :]
    WUP = WA
    WDN = WB
    P_MARK = A.off

    dma("sp", idf, ident, w=["idf"], stream="d_c0")
    dma("pool", idb, ident, w=["idb"], stream="d_c1")
    dma("sp", hvt, hv_d, w=["hvt"], stream="d_c2")
    S.op("dve", lambda e: e.memset(negh, -0.5), w=["negh"])

    dma("pool", maskp8, maskp8_d, w=["maskp8"], stream="d_c24")
    dma("pool", maskf8, maskf8_d, w=["maskf8"], stream="d_c25")
    w_in_r = w_in.rearrange("(k p) e -> p k e", p=128)
    w_mkv_r = w_mkv.rearrange("(k p) e -> p k e", p=128)
    w_o_r = w_o.rearrange("(k p) e -> p k e", p=128)
    for half in range(2):
        dma("pool", WMKV[:, :, half * 512:(half + 1) * 512], w_mkv_r[:, :, half * 512:(half + 1) * 512],
            w=["wmkv%d" % half], stream="d_WB%d" % (12 + half))
    for ci, (c0, cw) in enumerate(ZG):
        dma("pool", WIN[:, :, c0:c0 + cw], w_in_r[:, :, c0:c0 + cw], w=["win%d" % ci], stream="d_WA%d" % ci)
    for br in range(3):
        dma("pool", WO[:, br * 4:(br + 1) * 4, :], w_o_r[:, br * 4:(br + 1) * 4, :], w=["wo%d" % br], stream="d_WB%d" % br)

    def rstd_from(ms, out, eps, tag):
        S.op("pool", lambda e: e.tensor_scalar(out=out, in0=ms, scalar1=eps, scalar2=None, op0=ALU.add), r=[tag + "ms"], w=[tag + "rs"])
        S.op("pool", lambda e: e.tensor_tensor(out=out, in0=out, in1=negh, op=ALU.pow), r=[tag + "rs", "negh"], w=[tag + "rs"])

    def transposes(srcs, src_res, dst, dst_res, dt, evac_eng="act"):
        b = nb()
        n = len(srcs)
        pv = bankb(b) if dt == BF16 else bank(b)
        idt = idb if dt == BF16 else idf

        def f(e):
            ins = None
            for i, s in enumerate(srcs):
                ins = e.transpose(out=pv[:, i * 128:(i + 1) * 128], in_=s, identity=idt)
            return ins
        S.op("pe", f, r=list(src_res) + ["idb", "idf"], w=["b%d" % b])
        src = pv[:, 0:n * 128]
        if len(dst.shape) == 3:
            src = src.rearrange("p (a b) -> p a b", a=dst.shape[1])
        if evac_eng == "act":
            S.op("act", lambda e: e.copy(out=dst, in_=src), r=["b%d" % b], w=list(dst_res))
        else:
            S.op(evac_eng, lambda e: e.tensor_copy(out=dst, in_=src), r=["b%d" % b], w=list(dst_res))

    gpre = A.carve([D], F32)
    gpost = A.carve([D], F32)
    lng = A.carve([512], F32)
    lnb = A.carve([512], F32)
    snk = A.carve([8], F32)
    maskp = A.carve([256], F32)
    maskf = A.carve([256], F32)
    masks = A.carve([256], F32)
    nsnk = A.carve([8], F32)
    WT = A.carve([4, 128], BF16)
    WTS = A.carve([4, 128], BF16)
    sgbT = A.carve([4], F32)
    sgbS = A.carve([4], F32)
    mkT = A.carve([4, 256], BF16)
    mvb = A.carve([2, 512], BF16)
    P1_MARK = A.off

    for t, src, nm, st in [(gpre, g_pre, "gpre", 3), (gpost, g_post, "gpost", 4), (lng, ln_g, "lng", 5), (lnb, ln_b, "lnb", 6), (snk, sinks, "snk", 7)]:
        dma("sp", t, src.partition_broadcast(128), w=[nm], stream="d_c%d" % st)
    dma("sp", maskp, maskp_d, w=["maskp"], stream="d_c8")
    dma("sp", maskf, maskf_d, w=["maskf"], stream="d_c9")
    dma("sp", masks, masks_d, w=["masks"], stream="d_c10")
    S.op("dve", lambda e: e.tensor_scalar(out=nsnk, in0=snk, scalar1=-1.0, scalar2=None, op0=ALU.mult), r=["snk"], w=["nsnk"])

    gmem = A.carve([D], F32)
    mx0 = A.carve([D], F32)
    mx1 = A.carve([D], F32)
    mnb = A.carve([2, D], BF16)
    mnT = A.carve([8, 256], BF16)
    junk0 = A.carve([D], BF16)
    st0 = A.carve([8], F32)
    trilt = A.carve([128], F32)
    bdmt = A.carve([128], F32)
    e8t = A.carve([128], F32)
    wraw = A.carve([4, 128], F32)
    wmsk = A.carve([4, 128], BF16)
    w8 = A.carve([4, 8], F32)
    r8 = A.carve([4, 16, 8], F32)
    wrep = A.carve([4, 128], BF16)
    sgbr = A.carve([128], F32)
    mo = A.carve([2, 512], F32)
    mo2 = A.carve([2, 512], F32)
    cinp = [A.carve([128], F32) for _ in range(2)]

    dma("sp", gmem, g_mem.partition_broadcast(128), w=["gmem"], stream="d_c11")
    dma("sp", mx0, memp[0:128, :], w=["mx0"], stream="d_c12")
    dma("sp", mx1, memp[128:256, :], w=["mx1"], stream="d_c13")
    dma("sp", trilt, tril_d, w=["trilt"], stream="d_c14")
    dma("sp", bdmt, bdm_d, w=["bdmt"], stream="d_c15")
    dma("sp", e8t[0:8, :], e8_d, w=["e8t"], stream="d_c16")
    dma("sp", wraw, sg_w.rearrange("g t s -> t g s"), w=["wraw"], stream="d_c17")
    dma("sp", w8[0:8, :, :], sg_w[:, 0:8, 0:8].rearrange("g t s -> t g s"), w=["w8"], stream="d_c18")
    dma("sp", sgbr[0:4, :], sg_b, w=["sgbr"], stream="d_c19")

    if KSTOP == -1:
        return finish()
    for mb, mx in enumerate((mx0, mx1)):
        S.op("act", lambda e, mx=mx, mb=mb: e.activation(out=junk0, in_=mx, func=AF.Square, scale=1.0 / 32.0, accum_out=st0[:, mb:mb + 1]),
             r=["mx%d" % mb], w=["junk0", "m%dms" % mb])
        rstd_from(st0[:, mb:mb + 1], st0[:, 2 + mb:3 + mb], EPS, "m%d" % mb)
        S.op("dve", lambda e, mx=mx, mb=mb: e.scalar_tensor_tensor(out=mnb[:, mb, :], in0=mx, scalar=st0[:, 2 + mb:3 + mb], in1=gmem, op0=ALU.mult, op1=ALU.mult),
             r=["mx%d" % mb, "m%drs" % mb, "gmem"], w=["mnb%d" % mb])
        transposes([mnb[:, mb, k * 128:(k + 1) * 128] for k in range(8)], ["mnb%d" % mb],
                   mnT[:, :, mb * 128:(mb + 1) * 128], ["mnT%d" % mb], BF16)
    for mb in range(2):
        for kv in range(2):
            b = nb()

            def f(e, mb=mb, kv=kv, b=b):
                ins = None
                for k in range(8):
                    ins = e.matmul(bank(b), lhsT=mnT[:, k, mb * 128:(mb + 1) * 128], rhs=WMKV[:, k, kv * 512:(kv + 1) * 512], start=(k == 0), stop=(k == 7))
                return ins
            S.op("pe", f, r=["mnT0", "mnT1", "wmkv%d" % kv], w=["b%d" % b])
            dst = (mo if kv == 0 else mo2)[:, mb, :]
            S.op("act", lambda e, dst=dst, b=b: e.copy(out=dst, in_=bank(b)), r=["b%d" % b], w=["mo%d%d" % (kv, mb)])
            if kv == 1:
                S.op("dve", lambda e, b=b, mb=mb: e.tensor_copy(out=mvb[:, mb, :], in_=bank(b)), r=["b%d" % b], w=["mvb"])
            out_toks.append(dma("sp", (mkp if kv == 0 else mvp)[mb * 128:(mb + 1) * 128, :], dst, r=["mo%d%d" % (kv, mb)], w=["o_m%d%d" % (kv, mb)], stream="d_o%d" % (mb * 2 + kv)))
    for h in range(4):
        b = nb()

        def f(e, h=h, b=b):
            ins = None
            for k in range(8):
                ins = e.matmul(bank(b)[:, 0:256], lhsT=WMKV[:, k, h * 128:(h + 1) * 128], rhs=mnT[:, k, :], start=(k == 0), stop=(k == 7))
            return ins
        S.op("pe", f, r=["mnT0", "mnT1", "wmkv0"], w=["b%d" % b])
        S.op("act", lambda e, h=h, b=b: e.copy(out=mkT[:, h, :], in_=bank(b)[:, 0:256]), r=["b%d" % b], w=["mkT"])

    if KSTOP == -2:
        return finish()
    for g in range(4):
        S.op("dve", lambda e, g=g: e.tensor_tensor(out=wmsk[:, g, :], in0=wraw[:, g, :], in1=trilt, op=ALU.mult), r=["wraw", "trilt"], w=["wmsk"])
    transposes([wmsk[:, g, :] for g in range(4)], ["wmsk"], flat(WT), ["WT"], BF16)
    if KSTOP == -3:
        return finish()
    S.op("dve", lambda e: e.tensor_copy(out=r8[0:8], in_=bass.AP(w8.tensor, w8[0:8].offset, [list(w8[0:8].ap[0]), [8, 4], [0, 16], [1, 8]])), r=["w8"], w=["r8"])
    for g in range(4):
        b = nb()
        S.op("pe", lambda e, g=g, b=b: e.matmul(bank(b)[:, 0:128], lhsT=e8t[0:8, :], rhs=r8[0:8, g].rearrange("p a b -> p (a b)"), start=True, stop=True),
             r=["e8t", "r8"], w=["b%d" % b])
        S.op("dve", lambda e, g=g, b=b: e.tensor_tensor(out=wrep[:, g, :], in0=bank(b)[:, 0:128], in1=bdmt, op=ALU.mult), r=["b%d" % b, "bdmt"], w=["wrep"])
    transposes([wrep[:, g, :] for g in range(4)], ["wrep"], flat(WTS), ["WTS"], BF16)
    if KSTOP == -4:
        return finish()
    b = nb()
    S.op("pe", lambda e, b=b: e.transpose(out=bank(b)[:, 0:4], in_=sgbr[0:4, :], identity=idf[0:4, 0:4]), r=["sgbr", "idf"], w=["b%d" % b])
    S.op("act", lambda e, b=b: e.copy(out=sgbT, in_=bank(b)[:, 0:4]), r=["b%d" % b], w=["sgbT"])
    b = nb()
    S.op("pe", lambda e, b=b: e.matmul(bank(b)[:, 0:4], lhsT=e8t[0:8, :], rhs=sgbT[0:8, :], start=True, stop=True), r=["e8t", "sgbT"], w=["b%d" % b])
    S.op("act", lambda e, b=b: e.copy(out=sgbS, in_=bank(b)[:, 0:4]), r=["b%d" % b], w=["sgbS"])

    for part in range(2):
        cin = cinp[part]
        for r_ in range(2):
            rr = part * 2 + r_
            src = (conv_w[rr] if rr < 3 else conv_b[0]).rearrange("(c p) -> c p", p=128)
            dma("sp", cin[r_ * 44:(r_ + 1) * 44, :], src, w=["cin%d" % part], stream="d_c%d" % (20 + rr))
        b = nb()
        S.op("pe", lambda e, b=b, cin=cin: e.transpose(out=bank(b)[:, 0:88], in_=cin[0:88, :], identity=idf[0:88, 0:88]), r=["cin%d" % part, "idf"], w=["b%d" % b])
        S.op("act", lambda e, b=b, part=part: e.copy(out=flat(cwt)[:, part * 88:(part + 1) * 88], in_=bank(b)[:, 0:88]), r=["b%d" % b], w=["cwt"])
    if KSTOP == 1:
        return finish()
    S.barrier(keep_streams=("d_WA", "d_WB0", "d_WB1", "d_WB2"), keep_res=("win", "wo"))
    A.off = P1_MARK

    TB = Arena.__new__(Arena)
    TB.t = A.t
    TB.off = WB_OFF + 12 * 2048
    TB.cap = WB_OFF + 22 * 2048
    tg = TB.carve([3072], F32)
    acc = TB.carve([D], F32)
    tmp2 = TB.carve([D], F32)
    XIN = [A.carve([D], F32) for _ in range(2)]
    RC = [A.carve([64], F32) for _ in range(2)]
    RS = [A.carve([64], F32) for _ in range(2)]
    hb = A.carve([D], BF16)
    junk = A.carve([D], BF16)
    hT = A.carve([8, 128], BF16)
    stt = A.carve([128], F32)
    tq = A.carve([512], F32)
    uq = A.carve([512], F32)
    qr = A.carve([512], BF16)
    kr = A.carve([128], F32)
    kbb = A.carve([128], BF16)
    vf = A.carve([128], F32)
    vring = A.carve([2, 128], BF16)
    kTring = A.carve([2, 128], BF16)
    qT = A.carve([4, 128], BF16)
    usb = A.carve([512], F32)
    gvs = A.carve([512], F32)
    vn = gvs
    vnb = A.carve([512], BF16)
    sgo = A.carve([512], BF16)
    mqb = A.carve([512], BF16)
    mqT = A.carve([4, 128], BF16)
    ssb8 = A.carve([8, 256], F32)
    pb8 = A.carve([8, 256], BF16)
    pT8 = A.carve([8, 2, 128], BF16)
    ssb = ssb8[:, 0:4]
    pb = pb8[:, 0:4]
    pT = pT8[:, 0:4]
    em, pmb, pmT = ssb, pb, pT
    brT = [A.carve([4, 128], BF16) for _ in range(3)]
    P1_END = A.off

    def stat(i, n=1):
        return stt[:, i:i + n]

    def head(xsrc, slot, ridx):
        xin = XIN[slot]
        xr = "xin%d" % slot
        dma("sp", xin, xsrc, w=[xr], stream="d_xin%d" % slot)
        dma("sp", RC[slot], ropec[ridx], w=["rc%d" % slot], stream="d_rc%d" % slot)
        dma("sp", RS[slot], ropes[ridx], w=["rs%d" % slot], stream="d_rs%d" % slot)
        S.op("act", lambda e: e.activation(out=junk, in_=xin, func=AF.Square, scale=1.0 / 32.0, accum_out=stat(0)), r=[xr], w=["junk", "ams"])
        rstd_from(stat(0), stat(1), EPS, "a")
        S.op("dve", lambda e: e.scalar_tensor_tensor(out=hb, in0=xin, scalar=stat(1), in1=gpre, op0=ALU.mult, op1=ALU.mult), r=[xr, "ars", "gpre"], w=["hb"])
        transposes([hb[:, k * 128:(k + 1) * 128] for k in range(8)], ["hb"], flat(hT), ["hT"], BF16)

    def zmm(gi, b):
        c0, cw = ZG[gi]

        def f(e):
            ins = None
            for k in range(8):
                ins = e.matmul(bank(b)[:, 0:cw], lhsT=hT[:, k, :], rhs=WIN[:, k, c0:c0 + cw], start=(k == 0), stop=(k == 7))
            return ins
        S.op("pe", f, r=["hT", "win%d" % gi], w=["b%d" % b])

    def rope_ops(src, nh, slot, b, t_, u_):
        src4 = src.rearrange("p (h a i) -> p h a i", h=nh, a=2)
        swp = bass.AP(src.tensor, src.offset + 32, [list(src.ap[0]), [64, nh], [-32, 2], [1, 32]])
        rc, rs_ = RC[slot], RS[slot]
        cb = bass.AP(rc.tensor, rc.offset, [list(rc.ap[0]), [0, nh], [32, 2], [1, 32]])
        sb = bass.AP(rs_.tensor, rs_.offset, [list(rs_.ap[0]), [0, nh], [32, 2], [1, 32]])
        t4 = t_.rearrange("p (h a i) -> p h a i", h=nh, a=2)
        u4 = u_.rearrange("p (h a i) -> p h a i", h=nh, a=2)
        S.op("dve", lambda e: e.tensor_tensor(out=t4, in0=src4, in1=cb, op=ALU.mult), r=["b%d" % b, "rc%d" % slot], w=["tq"])
        S.op("dve", lambda e: e.tensor_tensor(out=u4, in0=swp, in1=sb, op=ALU.mult), r=["b%d" % b, "rs%d" % slot], w=["uq"])

    def proj_kv(slot):
        b1 = nb()
        zmm(1, b1)
        rope_ops(bank(b1)[:, 0:128], 2, slot, b1, tq[:, 0:128], uq[:, 0:128])
        S.op("dve", lambda e: e.tensor_tensor(out=kr, in0=tq[:, 0:128], in1=uq[:, 0:128], op=ALU.add), r=["tq", "uq"], w=["kr"])
        S.op("act", lambda e: e.copy(out=kbb, in_=kr), r=["kr"], w=["kbb"])
        S.op("act", lambda e: e.copy(out=vf, in_=bank(b1)[:, 128:256]), r=["b%d" % b1], w=["vf"])

    def proj_q(slot):
        b0 = nb()
        zmm(0, b0)
        rope_ops(bank(b0), 8, slot, b0, tq, uq)
        qr_perm = qr.rearrange("p (h e d) -> p e h d", h=4, e=2)
        S.op("dve", lambda e: e.tensor_tensor(out=qr_perm, in0=tq.rearrange("p (e h d) -> p e h d", e=2, h=4), in1=uq.rearrange("p (e h d) -> p e h d", e=2, h=4), op=ALU.add),
             r=["tq", "uq"], w=["qr"])

    def proj_sgu():
        b2 = nb()
        zmm(2, b2)
        S.op("act", lambda e: e.activation(out=usb, in_=bank(b2), func=AF.Gelu), r=["b%d" % b2], w=["usb"])

    def proj_sgv():
        b3 = nb()
        zmm(3, b3)
        S.op("act", lambda e: e.activation(out=gvs, in_=bank(b3), func=AF.Gelu, accum_out=stat(2)), r=["b%d" % b3], w=["gvs", "lsum"])
        S.op("dve", lambda e: e.tensor_scalar(out=stat(3), in0=stat(2), scalar1=-1.0 / 512.0, scalar2=None, op0=ALU.mult), r=["lsum"], w=["lnm"])
        S.op("dve", lambda e: e.tensor_scalar(out=gvs, in0=gvs, scalar1=stat(3), scalar2=None, op0=ALU.add), r=["gvs", "lnm"], w=["gvs"])
        S.op("act", lambda e: e.activation(out=junk[:, 0:512], in_=gvs, func=AF.Square, scale=float(512 ** -0.5), accum_out=stat(4)), r=["gvs"], w=["junk", "lms"])
        rstd_from(stat(4), stat(5), EPS, "l")
        S.op("dve", lambda e: e.scalar_tensor_tensor(out=gvs, in0=gvs, scalar=stat(5), in1=lng, op0=ALU.mult, op1=ALU.mult), r=["gvs", "lrs", "lng"], w=["gvs"])
        S.op("dve", lambda e: e.tensor_tensor(out=gvs, in0=gvs, in1=lnb, op=ALU.add), r=["gvs", "lnb"], w=["gvs"])
        S.op("act", lambda e: e.copy(out=vnb, in_=gvs), r=["gvs"], w=["vnb"])

    def proj_mq():
        b4 = nb()
        zmm(4, b4)
        S.op("act", lambda e: e.copy(out=mqb, in_=bank(b4)), r=["b%d" % b4], w=["mqb"])

    def proj_gates(lo, hi):
        for gi in range(lo, hi):
            bg = nb()
            zmm(5 + gi, bg)
            S.op("act", lambda e, gi=gi, bg=bg: e.activation(out=tg[:, gi * 512:(gi + 1) * 512], in_=bank(bg), func=AF.Tanh, scale=0.5), r=["b%d" % bg], w=["tg%d" % gi])

    def kv_publish(slot_kv):
        transposes([kbb], ["kbb"], kTring[:, slot_kv, :], ["kT%d" % slot_kv], BF16)
        S.op("act", lambda e: e.copy(out=vring[:, slot_kv, :], in_=vf), r=["vf"], w=["v%d" % slot_kv])

    def softmax_direct(src_res, src_ap, scale, sink_cols, dst32, dst32_res, dstb, dstb_res, so, sx, nh):
        M, NM, SM, DF, RI = so + 8, so + 8 + nh, so + 8 + 2 * nh, so + 8 + 3 * nh, so + 8 + 4 * nh
        src_res = list(src_res)
        S.op("dve", lambda e: e.tensor_reduce(out=stat(M, nh), in_=src_ap, axis=AX.X, op=ALU.max), r=src_res, w=["smx" + sx])
        if sink_cols is not None:
            S.op("dve", lambda e: e.scalar_tensor_tensor(out=stat(NM, nh), in0=stat(M, nh), scalar=-scale, in1=nsnk[:, sink_cols:sink_cols + nh], op0=ALU.mult, op1=ALU.min),
                 r=["smx" + sx, "nsnk"], w=["snm" + sx])
        else:
            S.op("dve", lambda e: e.tensor_scalar(out=stat(NM, nh), in0=stat(M, nh), scalar1=-scale, scalar2=None, op0=ALU.mult), r=["smx" + sx], w=["snm" + sx])

        def fexp(e):
            ins = None
            for h in range(nh):
                ins = e.activation(out=dst32[:, h, :], in_=src_ap[:, h, :], func=AF.Exp, bias=stat(NM + h), scale=scale, accum_out=stat(SM + h))
            return ins
        S.op("act", fexp, r=src_res + ["snm" + sx], w=[dst32_res, "ssum" + sx])
        if sink_cols is not None:
            S.op("dve", lambda e: e.tensor_tensor(out=stat(DF, nh), in0=snk[:, sink_cols:sink_cols + nh], in1=stat(NM, nh), op=ALU.add), r=["snk", "snm" + sx], w=["sdf" + sx])
            S.op("act", lambda e: e.activation(out=stat(DF, nh), in_=stat(DF, nh), func=AF.Exp), r=["sdf" + sx], w=["sdf" + sx])
            S.op("dve", lambda e: e.tensor_tensor(out=stat(SM, nh), in0=stat(SM, nh), in1=stat(DF, nh), op=ALU.add), r=["ssum" + sx, "sdf" + sx], w=["ssum" + sx])
        S.op("dve", lambda e: e.reciprocal(out=stat(RI, nh), in_=stat(SM, nh)), r=["ssum" + sx], w=["srin" + sx])
        rin = stat(RI, nh)
        rb = bass.AP(rin.tensor, rin.offset, [list(rin.ap[0]), [1, nh], [0, 256]])
        S.op("dve", lambda e: e.tensor_tensor(out=dstb, in0=dst32, in1=rb, op=ALU.mult), r=[dst32_res, "srin" + sx], w=[dstb_res])

    def softmax4(src_res, src_ap, mask_ap, mask_res, scale, sink_cols, dst32, dst32_res, dstb, dstb_res, so=44, sx="", nh=4):
        M, NM, SM, DF, RI = so + 8, so + 8 + nh, so + 8 + 2 * nh, so + 8 + 3 * nh, so + 8 + 4 * nh
        if mask_ap is not None:
            mk = bass.AP(mask_ap.tensor, mask_ap.offset, [list(mask_ap.ap[0]), [0, nh], [1, 256]])
            S.op("dve", lambda e: e.scalar_tensor_tensor(out=dst32, in0=src_ap, scalar=scale, in1=mk, op0=ALU.mult, op1=ALU.add),
                 r=list(src_res) + [mask_res], w=[dst32_res])
        else:
            S.op("act", lambda e: e.activation(out=dst32, in_=src_ap, func=AF.Identity, scale=scale), r=list(src_res), w=[dst32_res])
        S.op("dve", lambda e: e.tensor_reduce(out=stat(M, nh), in_=dst32, axis=AX.X, op=ALU.max), r=[dst32_res], w=["smx" + sx])
        if sink_cols is not None:
            S.op("dve", lambda e: e.tensor_tensor(out=stat(M, nh), in0=stat(M, nh), in1=snk[:, sink_cols:sink_cols + nh], op=ALU.max), r=["smx" + sx, "snk"], w=["smx" + sx])
        S.op("dve", lambda e: e.tensor_scalar(out=stat(NM, nh), in0=stat(M, nh), scalar1=-1.0, scalar2=None, op0=ALU.mult), r=["smx" + sx], w=["snm" + sx])

        def fexp(e):
            ins = None
            for h in range(nh):
                ins = e.activation(out=dst32[:, h, :], in_=dst32[:, h, :], func=AF.Exp, bias=stat(NM + h), scale=1.0, accum_out=stat(SM + h))
            return ins
        S.op("act", fexp, r=[dst32_res, "snm" + sx], w=[dst32_res, "ssum" + sx])
        if sink_cols is not None:
            S.op("dve", lambda e: e.tensor_tensor(out=stat(DF, nh), in0=snk[:, sink_cols:sink_cols + nh], in1=stat(NM, nh), op=ALU.add), r=["snk", "snm" + sx], w=["sdf" + sx])
            S.op("act", lambda e: e.activation(out=stat(DF, nh), in_=stat(DF, nh), func=AF.Exp), r=["sdf" + sx], w=["sdf" + sx])
            S.op("dve", lambda e: e.tensor_tensor(out=stat(SM, nh), in0=stat(SM, nh), in1=stat(DF, nh), op=ALU.add), r=["ssum" + sx, "sdf" + sx], w=["ssum" + sx])
        S.op("dve", lambda e: e.reciprocal(out=stat(RI, nh), in_=stat(SM, nh)), r=["ssum" + sx], w=["srin" + sx])
        rin = stat(RI, nh)
        rb = bass.AP(rin.tensor, rin.offset, [list(rin.ap[0]), [1, nh], [0, 256]])
        S.op("dve", lambda e: e.tensor_tensor(out=dstb, in0=dst32, in1=rb, op=ALU.mult), r=[dst32_res, "srin" + sx], w=[dstb_res])

    OB = 3

    def attn_q_transposes():
        transposes([qr[:, h * 128:(h + 1) * 128] for h in range(4)], ["qr"], flat(qT), ["qT"], BF16)

    def attn_scores8(cur, mask8, mask8_res):
        prev = 1 - cur

        def fs(e):
            ins = None
            for e_ in range(2):
                for h in range(4):
                    for kbi, sl in enumerate((prev, cur)):
                        c0 = 4 * 512 + (e_ * 4 + h) * 256 + kbi * 128
                        e.matmul(PS[:, c0:c0 + 128], lhsT=qT[e_ * 64:(e_ + 1) * 64, h, :], rhs=kTring[e_ * 64:(e_ + 1) * 64, sl, :],
                                 start=(h % 2 == 0 and kbi == 0), stop=False, skip_group_check=True)
            for hh in range(8):
                c0 = 4 * 512 + hh * 256
                ins = e.matmul(PS[:, c0:c0 + 256], lhsT=idb, rhs=mask8, start=False, stop=True, skip_group_check=True)
            return ins
        S.op("pe", fs, r=["qT", "kT0", "kT1", "idb", mask8_res], w=["b4", "b5", "b6", "b7"])

    def attn_softmax8():
        sc = PS[:, 4 * 512:8 * 512].rearrange("p (h k) -> p h k", h=8)
        softmax_direct(["b4", "b5", "b6", "b7"], sc, 0.125, 0, ssb8, "ssb", pb8, "pb", so=0, sx="8", nh=8)

    def attn_pv8(cur):
        prev = 1 - cur
        transposes([pb8[:, hh, kbi * 128:(kbi + 1) * 128] for hh in range(4) for kbi in range(2)], ["pb"], flat(pT8[:, 0:4]), ["pT"], BF16)
        transposes([pb8[:, hh, kbi * 128:(kbi + 1) * 128] for hh in range(4, 8) for kbi in range(2)], ["pb"], flat(pT8[:, 4:8]), ["pT2"], BF16)

        def fpv(e):
            ins = None
            for hh in range(8):
                e_ = hh // 4
                cc, par = hh // 2, hh % 2
                for kbi, sl in enumerate((prev, cur)):
                    ins = e.matmul(bank(OB)[par * 64:(par + 1) * 64, cc * 128:(cc + 1) * 128], lhsT=vring[:, sl, e_ * 64:(e_ + 1) * 64],
                                   rhs=pT8[:, hh, kbi, :], start=(kbi == 0), stop=(kbi == 1), skip_group_check=True)
            return ins
        S.op("pe", fpv, r=["pT", "pT2", "v0", "v1"], w=["b%d" % OB])

    def attn_evac():
        S.op("act", lambda e: e.copy(out=flat(brT[0]), in_=bank(OB)), r=["b%d" % OB], w=["brT0"])

    def mem_scores_softmax_only():
        sc = PS[:, 6 * 512:8 * 512].rearrange("p (h k) -> p h k", h=4)
        softmax_direct(["b6", "b7"], sc, float(128 ** -0.5), None, em, "ssb", pmb, "pb", so=44, sx="", nh=4)

    def mem_scores_softmax():
        mem_scores_softmax_only()
        transposes([pmb[:, h, mbi * 128:(mbi + 1) * 128] for h in range(4) for mbi in range(2)], ["pb"], flat(pmT), ["pT"], BF16)

    def mem_scores():
        transposes([mqb[:, h * 128:(h + 1) * 128] for h in range(4)], ["mqb"], flat(mqT), ["mqT"], BF16)

        def fs(e):
            ins = None
            for h in range(4):
                ins = e.matmul(PS[:, 6 * 512 + h * 256: 6 * 512 + (h + 1) * 256], lhsT=mqT[:, h, :], rhs=mkT[:, h, :], start=True, stop=True)
            return ins
        S.op("pe", fs, r=["mqT", "mkT"], w=["b6", "b7"])
        mem_scores_softmax_only()

    def mem_pv():
        transposes([pmb[:, h, mbi * 128:(mbi + 1) * 128] for h in range(4) for mbi in range(2)], ["pb"], flat(pmT), ["pT"], BF16)

        def fpv(e):
            ins = None
            for h in range(4):
                for mbi in range(2):
                    ins = e.matmul(bank(OB)[:, h * 128:(h + 1) * 128], lhsT=mvb[:, mbi, h * 128:(h + 1) * 128], rhs=pmT[:, h, mbi, :],
                                   start=(mbi == 0), stop=(mbi == 1), skip_group_check=True)
            return ins
        S.op("pe", fpv, r=["pT", "mvb"], w=["b%d" % OB])
        S.op("act", lambda e: e.copy(out=flat(brT[2]), in_=bank(OB)), r=["b%d" % OB], w=["brT2"])

    def sg_mix(wt, wt_res, bias, bias_res):
        b = nb()

        def f(e):
            ins = None
            for g in range(4):
                ins = e.matmul(bank(b)[:, g * 128:(g + 1) * 128], lhsT=wt[:, g, :], rhs=vnb[:, g * 128:(g + 1) * 128], start=True, stop=True)
            return ins
        S.op("pe", f, r=["vnb", wt_res], w=["b%d" % b])

        def fo(e):
            ins = None
            for g in range(4):
                ins = e.scalar_tensor_tensor(out=sgo[:, g * 128:(g + 1) * 128], in0=bank(b)[:, g * 128:(g + 1) * 128], scalar=bias[:, g:g + 1],
                                             in1=usb[:, g * 128:(g + 1) * 128], op0=ALU.add, op1=ALU.mult)
            return ins
        S.op("dve", fo, r=["b%d" % b, "usb", bias_res], w=["sgo"])
        transposes([sgo[:, c * 128:(c + 1) * 128] for c in range(4)], ["sgo"], flat(brT[1]), ["brT1"], BF16)

    def proj_branch(br):
        for half in range(2):
            b = nb()

            def f(e, half=half, b=b):
                ins = None
                for cc in range(4):
                    ins = e.matmul(bank(b), lhsT=brT[br][:, cc, :], rhs=WO[:, br * 4 + cc, half * 512:(half + 1) * 512], start=(cc == 0), stop=(cc == 3))
                return ins
            S.op("pe", f, r=["brT%d" % br, "wo%d" % br], w=["b%d" % b])
            gi = br * 2 + half
            ah = acc[:, half * 512:(half + 1) * 512]
            th = tmp2[:, half * 512:(half + 1) * 512]
            if br == 0:
                S.op("dve", lambda e, gi=gi, b=b, ah=ah: e.scalar_tensor_tensor(out=ah, in0=tg[:, gi * 512:(gi + 1) * 512], scalar=1.0, in1=bank(b), op0=ALU.add, op1=ALU.mult),
                     r=["tg%d" % gi, "b%d" % b], w=["acc%d" % half])
            else:
                S.op("dve", lambda e, gi=gi, b=b, th=th: e.scalar_tensor_tensor(out=th, in0=tg[:, gi * 512:(gi + 1) * 512], scalar=1.0, in1=bank(b), op0=ALU.add, op1=ALU.mult),
                     r=["tg%d" % gi, "b%d" % b], w=["tmp%d" % half])
                S.op("dve", lambda e, ah=ah, th=th: e.tensor_tensor(out=ah, in0=ah, in1=th, op=ALU.add), r=["acc%d" % half, "tmp%d" % half], w=["acc%d" % half])

    def back2(slot, x1row, store_eng="pool"):
        xin = XIN[slot]
        xr = "xin%d" % slot
        S.op("act", lambda e: e.activation(out=junk, in_=acc, func=AF.Square, scale=1.0 / 32.0, accum_out=stat(6)), r=["acc0", "acc1"], w=["junk", "pms"])
        rstd_from(stat(6), stat(7), 4.0 * EPS, "p")
        S.op("dve", lambda e: e.scalar_tensor_tensor(out=acc, in0=acc, scalar=stat(7), in1=gpost, op0=ALU.mult, op1=ALU.mult), r=["acc0", "acc1", "prs", "gpost"], w=["acc0", "acc1"])
        S.op("dve", lambda e: e.tensor_tensor(out=xin, in0=xin, in1=acc, op=ALU.add), r=[xr, "acc0", "acc1"], w=[xr])
        dma(store_eng, x1s[x1row * 128:(x1row + 1) * 128, :], xin, r=[xr], w=["x1s%d" % x1row],
            stream="d_x1o%d%s" % (slot, "" if store_eng == "pool" else "h"))

    rot["n"] = 3
    rot["i"] = 0
    head(xp[0:128, :], 0, 0)
    proj_kv(0)
    kv_publish(0)
    head(xp[128:256, :], 1, 1)
    proj_kv(1)
    proj_q(1)
    for bi in range(1, NPB):
        slot = bi % 2
        kv_publish(slot)
        if bi == NPB - 1:
            out_toks.append(dma("pool", wkp, kr, r=["kr"], w=["o_wkp"], stream="d_o4"))
            out_toks.append(dma("pool", wvp, vf, r=["vf"], w=["o_wvp"], stream="d_o5"))
        mk_ap, mk_res = (maskf8, "maskf8") if bi == 2 else (maskp8, "maskp8")
        attn_q_transposes()
        attn_scores8(slot, mk_ap, mk_res)
        attn_softmax8()
        proj_sgu()
        proj_sgv()
        proj_mq()
        proj_gates(0, 4)
        attn_pv8(slot)
        sg_mix(WT, "WT", sgbT, "sgbT")
        attn_evac()
        mem_scores()
        proj_gates(4, 6)
        proj_branch(0)
        proj_branch(1)
        mem_pv()
        proj_branch(2)
        nslot = 1 - slot
        if bi + 1 < NPB:
            head(xp[(bi + 1) * 128:(bi + 2) * 128, :], nslot, bi + 1)
        else:
            head(xs, nslot, NPB)
        proj_kv(nslot)
        proj_q(nslot)
        back2(slot, bi - 1)
        if KSTOP == 3 and bi == 2:
            return finish()
    if KSTOP == 4:
        return finish()

    proj_sgu()
    proj_sgv()
    proj_mq()
    proj_gates(0, 6)
    out_toks.append(dma("sp", sgv, gvs, r=["gvs"], w=["o_sgv"], stream="d_o6"))
    out_toks.append(dma("sp", wks[:, 0:120, :], cwk[:, 8:128, :], w=["o_wks_a"], stream="d_o7"))
    out_toks.append(dma("sp", wvs[:, 0:120, :], cwv[:, 8:128, :], w=["o_wvs_a"], stream="d_o8"))
    for q in range(SEQS):
        out_toks.append(dma("sp", wks[q, 120:128, :], kr[q * 8:(q + 1) * 8, :], r=["kr"], w=["o_wks_b%d" % q], stream="d_o9"))
        out_toks.append(dma("sp", wvs[q, 120:128, :], vf[q * 8:(q + 1) * 8, :], r=["vf"], w=["o_wvs_b%d" % q], stream="d_o10"))
    if KSTOP == 5:
        return finish()
    S.barrier()
    SB = Arena.__new__(Arena)
    SB.t = A.t
    SB.off = WA_OFF
    SB.cap = WA_OFF + 8 * 5632 * 2
    kqT = SB.carve([SEQS, 4, 256], BF16)
    Zq = [SB.carve([SEQS, 128], BF16) for _ in range(2)]
    Zs = [SB.carve([SEQS, 128], BF16) for _ in range(2)]
    cwkb = SB.carve([SEQS, 128], BF16)
    cwkT = SB.carve([SEQS, 128], BF16)
    cwvb = SB.carve([SEQS, 128], BF16)
    NST = 4
    kst = [SB.carve([2, 512], BF16) for _ in range(NST)]
    vst = [SB.carve([2, 512], BF16) for _ in range(NST)]

    for z in range(2):
        S.op("dve", lambda e, z=z: e.memset(flat(Zq[z]), 0.0), w=["Zq%d" % z])
        S.op("dve", lambda e, z=z: e.memset(flat(Zs[z]), 0.0), w=["Zs%d" % z])
    def k_load(q):
        dma("pool", kst[q % NST], cmk[q].rearrange("(mb m) c -> m mb c", mb=2), w=["kst%d" % (q % NST)], stream="d_sk%d" % (q % NST))

    def v_load(q):
        dma("pool", vst[q % NST], cmv[q].rearrange("(mb m) c -> m mb c", mb=2), w=["vst%d" % (q % NST)], stream="d_sv%d" % (q % NST))

    for q in range(NST):
        k_load(q)
    dma("pool", cwkb, cwk.rearrange("q j c -> j q c"), w=["cwkb"], stream="d_s0")
    dma("pool", cwvb, cwv.rearrange("q j c -> j q c"), w=["cwvb"], stream="d_s1")
    for half in range(2):
        transposes([cwkb[:, half * 8 + i, :] for i in range(8)], ["cwkb"], cwkT[:, half * 8:(half + 1) * 8, :], ["cwkT%d" % half], BF16)
    for q in range(SEQS):
        st = kst[q % NST]
        transposes([st[:, mbi, h * 128:(h + 1) * 128] for h in range(4) for mbi in range(2)], ["kst%d" % (q % NST)],
                   kqT[:, q], ["kqT%d" % q], BF16)
        if q + NST < SEQS:
            k_load(q + NST)
    for q in range(NST):
        v_load(q)

    kv_publish(1)
    transposes([qr[:, h * 128:(h + 1) * 128] for h in range(4)], ["qr"], flat(qT), ["qT"], BF16)
    transposes([mqb[:, h * 128:(h + 1) * 128] for h in range(4)], ["mqb"], flat(mqT), ["mqT"], BF16)

    def diag_fill(dst, src):
        d = bass.AP(dst.tensor, dst.offset, [list(dst.ap[0]), [136, 16], [1, 8]])
        s = src.rearrange("p (q t) -> p q t", q=16)
        return d, s

    ob = OB
    for grp in range(2):
        e_ = grp
        for h in range(4):
            z = Zs[h % 2]
            zr = "Zs%d" % (h % 2)
            d_, s_ = diag_fill(z, qT[:, h, :])
            S.op("dve", lambda e, d_=d_, s_=s_: e.tensor_copy(out=d_, in_=s_), r=["qT"], w=[zr])

            def fs(e, h=h, e_=e_, z=z):
                ins = None
                base = 6 * 512 + h * 256
                for q in range(SEQS):
                    ins = e.matmul(PS[:, base:base + 128], lhsT=z[e_ * 64:(e_ + 1) * 64, q, :], rhs=cwkT[e_ * 64:(e_ + 1) * 64, q, :],
                                   start=(q == 0 and h % 2 == 0), stop=(q == SEQS - 1), skip_group_check=True)
                ins = e.matmul(PS[:, base + 128:base + 256], lhsT=qT[e_ * 64:(e_ + 1) * 64, h, :], rhs=kTring[e_ * 64:(e_ + 1) * 64, 1, :],
                               start=False, stop=True, skip_group_check=True)
                return ins
            S.op("pe", fs, r=[zr, "cwkT0", "cwkT1", "qT", "kT1"], w=["b6", "b7"])
        sc = PS[:, 6 * 512:8 * 512].rearrange("p (h k) -> p h k", h=4)
        softmax4(["b6", "b7"], sc, masks, "masks", 0.125, grp * 4, ssb, "ssb", pb, "pb")
        transposes([pb[:, h, kbi * 128:(kbi + 1) * 128] for h in range(4) for kbi in range(2)], ["pb"], flat(pT), ["pT"], BF16)

        def fpv(e, e_=e_):
            ins = None
            for h in range(4):
                hh = e_ * 4 + h
                cc, par = hh // 2, hh % 2
                o = bank(ob)[par * 64:(par + 1) * 64, cc * 128:(cc + 1) * 128]
                ins = e.matmul(o, lhsT=vring[:, 1, e_ * 64:(e_ + 1) * 64], rhs=pT[:, h, 1, :], start=True, stop=False, skip_group_check=True)
                for q in range(SEQS):
                    ins = e.matmul(o[:, q * 8:(q + 1) * 8], lhsT=cwvb[:, q, e_ * 64:(e_ + 1) * 64], rhs=pT[:, h, 0, q * 8:(q + 1) * 8],
                                   start=False, stop=(q == SEQS - 1), skip_group_check=True)
            return ins
        S.op("pe", fpv, r=["pT", "v1", "cwvb"], w=["b%d" % ob])
    S.op("act", lambda e: e.copy(out=flat(brT[0]), in_=bank(ob)), r=["b%d" % ob], w=["brT0"])

    sg_mix(WTS, "WTS", sgbS, "sgbS")

    for h in range(4):
        z = Zq[h % 2]
        zr = "Zq%d" % (h % 2)
        d_, s_ = diag_fill(z, mqT[:, h, :])
        S.op("dve", lambda e, d_=d_, s_=s_: e.tensor_copy(out=d_, in_=s_), r=["mqT"], w=[zr])

        def fs(e, h=h, z=z):
            ins = None
            base = 6 * 512 + h * 256
            for q in range(SEQS):
                ins = e.matmul(PS[:, base:base + 256], lhsT=z[:, q, :], rhs=kqT[:, q, h, :], start=(q == 0 and h % 2 == 0), stop=(q == SEQS - 1), skip_group_check=True)
            return ins
        S.op("pe", fs, r=[zr] + ["kqT%d" % q for q in range(SEQS)], w=["b6", "b7"])
    mem_scores_softmax()
    ob2 = OB
    for q in range(SEQS):
        st = vst[q % NST]

        def fpv(e, q=q, st=st):
            ins = None
            for h in range(4):
                for mbi in range(2):
                    ins = e.matmul(bank(ob2)[:, h * 128 + q * 8: h * 128 + (q + 1) * 8], lhsT=st[:, mbi, h * 128:(h + 1) * 128], rhs=pmT[:, h, mbi, q * 8:(q + 1) * 8],
                                   start=(q == 0 and h == 0 and mbi == 0), stop=(mbi == 1), skip_group_check=True)
            return ins
        S.op("pe", fpv, r=["pT", "vst%d" % (q % NST)], w=["b%d" % ob2])
        if q + NST < SEQS:
            v_load(q + NST)
    S.op("act", lambda e: e.copy(out=flat(brT[2]), in_=bank(ob2)), r=["b%d" % ob2], w=["brT2"])
    fence = [(st_, v_) for st_, v_ in S.cnt.items() if v_ > 0 and (st_ in ("pe", "act", "dve", "pool") or st_.startswith("d_s"))]
    w_up_r = w_up.rearrange("(k p) e -> p k e", p=128)
    for c in (0, 5, 6, 1, 7, 2, 8, 3, 9, 4, 10):
        dma("pool", WUP[:, :, c * 512:(c + 1) * 512], w_up_r[:, :, c * 512:(c + 1) * 512], w=["wup%d" % c], stream="d_WA%d" % c, after=fence)
    proj_branch(0)
    proj_branch(1)
    proj_branch(2)
    back2(0, NOWN + 1, store_eng="sp")
    if KSTOP == 6:
        return finish()

    S.barrier(keep_streams=("d_WA",), keep_res=("wup",))

    rot["n"] = 4
    rot["i"] = 0
    A.off = P_MARK
    gpf = A.carve([D], F32)
    gqf = A.carve([D], F32)
    P2_MARK = A.off
    T = {}

    def carve_p2(ntok):
        A.off = P2_MARK
        nblk = ntok // 128
        T["X1"] = [A.carve([nblk, D], F32) for _ in range(2)]
        T["h2"] = [A.carve([D], BF16) for _ in range(2)]
        T["junk2"] = None
        T["st2"] = A.carve([16], F32)
        T["h2T"] = [A.carve([8, ntok + 2], BF16) for _ in range(2 if ntok > 128 else 1)]
        T["actT"] = A.carve([NFC, ntok], BF16)
        T["EXT"] = [[A.carve([ntok + 4 if ntok > 128 else 160], F32) for _ in range(2)] for _ in range(2)]
        T["CG"] = [A.carve([ntok], F32) for _ in range(2)]
        T["CV"] = [A.carve([ntok], F32) for _ in range(2)]
        T["GG"] = [A.carve([ntok], F32) for _ in range(2)]
        T["fsb"] = A.carve([ntok // 128, D], F32)
        T["ysb"] = None
        if ntok > 128:
            T["osb"] = T["fsb"][:, 0, 0:128]
            T["osb_res"] = "fsb00"
        else:
            T["osb"] = A.carve([128], F32)
            T["osb_res"] = "osb"

    carve_p2(128)
    XH = A.carve([D], F32)
    scvr4 = [A.carve([1408], F32) for _ in range(2)]
    osb11 = scvr4[0].rearrange("p (a c) -> p a c", a=11)
    scvT = A.carve([44, 32], F32)
    ncs = A.carve([1408], F32)

    dma("sp", gpf, g_pffn.partition_broadcast(128), w=["gpf"], stream="d_c3")
    dma("sp", gqf, g_qffn.partition_broadcast(128), w=["gqf"], stream="d_c4")
    w_dn_r = w_down.rearrange("(k p) e -> p k e", p=128)
    for c4, (f0, f1) in enumerate(((0, 4), (4, 10), (10, 16), (16, 22))):
        dma("pool", WDN[:, f0:f1, :], w_dn_r[:, f0:f1, :], w=["wdn%d" % c4], stream="d_WB%d" % c4)

    for cg_ in range(4):
        scvr = scvr4[cg_ % 2]
        dma("sp", scvr[0:32, :], scv[:, cg_ * 1408:(cg_ + 1) * 1408], w=["scvr%d" % (cg_ % 2)], stream="d_c%d" % (7 + cg_ % 2))
        b = nb()

        def f(e, b=b, scvr=scvr):
            ins = None
            for i in range(11):
                ins = e.transpose(out=bank(b)[:, i * 32:(i + 1) * 32], in_=scvr[0:32, i * 128:(i + 1) * 128], identity=idf[0:32, 0:32])
            return ins
        S.op("pe", f, r=["scvr%d" % (cg_ % 2), "idf"], w=["b%d" % b])
        S.op("act", lambda e, b=b, cg_=cg_: e.copy(out=scvT[:, cg_ * 11:(cg_ + 1) * 11, :], in_=bank(b)[:, 0:352].rearrange("p (a b) -> p a b", a=11)), r=["b%d" % b], w=["scvT"])

    def ffn_front_a(nrows, xt_ap, xres, j):
        h2, st2 = T["h2"][j], T["st2"]
        c = 4 * j
        S.op("act", lambda e: e.activation(out=h2[0:nrows], in_=xt_ap[0:nrows], func=AF.Square, scale=1.0 / 32.0, accum_out=st2[0:nrows, c:c + 1]), r=[xres], w=["h2_%d" % j, "fms%d" % j])
        S.op("pool", lambda e: e.tensor_scalar(out=st2[0:nrows, c + 1:c + 2], in0=st2[0:nrows, c:c + 1], scalar1=EPS, scalar2=None, op0=ALU.add), r=["fms%d" % j], w=["frs%d" % j])
        S.op("pool", lambda e: e.tensor_tensor(out=st2[0:nrows, c + 1:c + 2], in0=st2[0:nrows, c + 1:c + 2], in1=negh[0:nrows], op=ALU.pow), r=["frs%d" % j, "negh"], w=["frs%d" % j])
        S.op("dve", lambda e: e.scalar_tensor_tensor(out=h2[0:nrows], in0=xt_ap[0:nrows], scalar=st2[0:nrows, c + 1:c + 2], in1=gpf[0:nrows], op0=ALU.mult, op1=ALU.mult),
             r=[xres, "frs%d" % j, "gpf"], w=["h2_%d" % j])

    def ffn_front_b(nrows, col0, j, hsel=0):
        h2, h2T = T["h2"][j], T["h2T"][hsel]
        hres = "h2T%d" % hsel
        b = nb()
        pv = bankb(b)

        def f(e):
            ins = None
            for k in range(8):
                ins = e.transpose(out=pv[:, k * 128:k * 128 + nrows], in_=h2[0:nrows, k * 128:(k + 1) * 128], identity=idb[0:nrows, 0:nrows])
            return ins
        S.op("pe", f, r=["h2_%d" % j, "idb"], w=["b%d" % b])
        src = pv[:, 0:1024].rearrange("p (k t) -> p k t", k=8)[:, :, 0:nrows]
        S.op("act", lambda e: e.copy(out=h2T[:, :, col0:col0 + nrows], in_=src), r=["b%d" % b], w=[hres])

    def nb2():
        if rot["i"] % 2:
            nb()
        b = nb()
        nb()
        return b

    def ffn_tile(N, sample, x1_tiles, yout, hooks=None, hsel=0):
        actT, EXT, CG, CV, GG, fsb, ysb, st2, junk2 = (T[k] for k in ("actT", "EXT", "CG", "CV", "GG", "fsb", "ysb", "st2", "junk2"))
        h2T = T["h2T"][hsel]
        hres = "h2T%d" % hsel
        n_act = 128 if sample else N

        NTB = n_act // 128

        def down_slice(fc):
            def f(e):
                ins = None
                for j in range(NTB):
                    for half in range(2):
                        o = PS[:, (4 + 2 * j + half) * 512:(5 + 2 * j + half) * 512]
                        ins = e.matmul(o, lhsT=actT[:, fc, j * 128:(j + 1) * 128], rhs=WDN[:, fc, half * 512:(half + 1) * 512],
                                       start=(fc == 0), stop=(fc == NFC - 1))
                return ins
            S.op("pe", f, r=["actT%d" % fc, "wdn%d" % (0 if fc < 4 else 1 if fc < 10 else 2 if fc < 16 else 3)], w=["b%d" % (4 + i) for i in range(2 * NTB)])

        def tail_ops(fc):
            bi_ = fc % 2
            gg = GG[bi_]
            S.op("act", lambda e: e.activation(out=gg[:, 0:n_act], in_=CG[bi_][:, 0:n_act], func=AF.Gelu_apprx_tanh), r=["cg%d" % bi_], w=["gg%d" % bi_])
            S.op("pool", lambda e: e.tensor_tensor(out=actT[:, fc, 0:n_act], in0=gg[:, 0:n_act], in1=CV[bi_][:, 0:n_act], op=ALU.mult),
                 r=["gg%d" % bi_, "cv%d" % bi_], w=["actT%d" % fc])

        for fc in range(NFC):
            bi_ = fc % 2
            stage = []
            for gv in range(2):
                fcc = fc + gv * NFC
                b = nb()

                def f(e, fcc=fcc, b=b):
                    ins = None
                    for k in range(8):
                        ins = e.matmul(bank(b)[:, 0:N], lhsT=WUP[:, k, fcc * 128:(fcc + 1) * 128], rhs=h2T[:, k, 0:N], start=(k == 0), stop=(k == 7))
                    return ins
                S.op("pe", f, r=[hres, "wup%d" % (fcc // 4)], w=["b%d" % b])
                ext = EXT[bi_][gv]
                er = "ext%d%d" % (bi_, gv)
                erc = er + "c"
                cdst = (CG if gv == 0 else CV)[bi_]
                cres = ("cg%d" if gv == 0 else "cv%d") % bi_
                w0 = cwt[:, 0, fcc:fcc + 1]
                w1 = cwt[:, 1, fcc:fcc + 1]
                w2 = cwt[:, 2, fcc:fcc + 1]
                bb = cwt[:, 3, fcc:fcc + 1]
                if not sample:
                    S.op("pool", lambda e, ext=ext, fcc=fcc: e.tensor_copy(out=ext[:, 0:2], in_=carry[:, :, fcc]), r=["carry%d" % fcc], w=[erc])
                    S.op("act", lambda e, ext=ext, b=b: e.copy(out=ext[:, 2:2 + N], in_=bank(b)[:, 0:N]), r=["b%d" % b], w=[er])
                    S.op("pool", lambda e, ext=ext, fcc=fcc: e.tensor_copy(out=carry[:, :, fcc], in_=ext[:, N:N + 2]), r=[er], w=["carry%d" % fcc])
                    v2, v1, v0 = ext[:, 2:N + 2], ext[:, 1:N + 1], ext[:, 0:N]
                    cd = cdst[:, 0:N]
                else:
                    S.op("dve", lambda e, fcc=fcc, b=b: e.tensor_scalar(out=carry[:, :, fcc], in0=bank(b)[:, 0:2], scalar1=hvt, scalar2=None, op0=ALU.mult),
                         r=["b%d" % b, "hvt"], w=["carry%d" % fcc])
                    e3 = ext[:, 0:160].rearrange("p (q t) -> p q t", q=16)
                    S.op("pool", lambda e, e3=e3, fcc=fcc: e.tensor_copy(out=e3[:, :, 0:2], in_=scvT[:, fcc, :].rearrange("p (q i) -> p q i", q=16)), r=["scvT"], w=[erc])
                    S.op("act", lambda e, e3=e3, b=b: e.copy(out=e3[:, :, 2:10], in_=bank(b)[:, 2:130].rearrange("p (q t) -> p q t", q=16)), r=["b%d" % b], w=[er])
                    nview = bass.AP(ncs.tensor, ncs.offset + fcc, [list(ncs.ap[0]), [88, 16], [44, 2]])
                    S.op("pool", lambda e, e3=e3, nview=nview: e.tensor_copy(out=nview, in_=e3[:, :, 8:10]), r=[er], w=["ncs"])
                    v2, v1, v0 = e3[:, :, 2:10], e3[:, :, 1:9], e3[:, :, 0:8]
                    cd = cdst[:, 0:128].rearrange("p (q t) -> p q t", q=16)
                stage.append((cd, v2, v1, v0, w2, w1, w0, bb, er, erc, cres))
            for (cd, v2, v1, v0, w2, w1, w0, bb, er, erc, cres) in stage:
                S.op("act", lambda e, cd=cd, v2=v2, w2=w2, bb=bb: e.activation(out=cd, in_=v2, func=AF.Identity, scale=w2, bias=bb), r=[er, "cwt"], w=[cres])
            for (cd, v2, v1, v0, w2, w1, w0, bb, er, erc, cres) in stage:
                S.op("dve", lambda e, cd=cd, v1=v1, w1=w1: e.scalar_tensor_tensor(out=cd, in0=v1, scalar=w1, in1=cd, op0=ALU.mult, op1=ALU.add), r=[er, erc, cres, "cwt"], w=[cres])
                S.op("dve", lambda e, cd=cd, v0=v0, w0=w0: e.scalar_tensor_tensor(out=cd, in0=v0, scalar=w0, in1=cd, op0=ALU.mult, op1=ALU.add), r=[er, erc, cres, "cwt"], w=[cres])
            if fc > 0:
                tail_ops(fc - 1)
            if fc >= 2:
                down_slice(fc - 2)
            for hk in (hooks or {}).get(fc, ()):
                hk()
        tail_ops(NFC - 1)
        down_slice(NFC - 2)
        down_slice(NFC - 1)
        for j in range(NTB):
            for half in range(2):
                bk = 4 + 2 * j + half
                eng = "act" if half == 0 else "dve"
                if eng == "act":
                    S.op("act", lambda e, j=j, half=half, bk=bk: e.copy(out=fsb[:, j, half * 512:(half + 1) * 512], in_=bank(bk)), r=["b%d" % bk], w=["fsb%d%d" % (j, half)])
                else:
                    S.op("dve", lambda e, j=j, half=half, bk=bk: e.tensor_copy(out=fsb[:, j, half * 512:(half + 1) * 512], in_=bank(bk)), r=["b%d" % bk], w=["fsb%d%d" % (j, half)])

        def make_tail(j):
            def run():
                fres = ["fsb%d0" % j, "fsb%d1" % j]
                ftok = fsb[:, j, :]
                junk_ = T["h2"][j]
                c = 8 + 2 * j
                S.op("act", lambda e: e.activation(out=junk_, in_=ftok, func=AF.Square, scale=1.0 / 32.0, accum_out=st2[:, c:c + 1]), r=fres, w=["h2_%d" % j, "gms%d" % j])
                S.op("pool", lambda e: e.tensor_scalar(out=st2[:, c + 1:c + 2], in0=st2[:, c:c + 1], scalar1=EPS, scalar2=None, op0=ALU.add), r=["gms%d" % j], w=["grs%d" % j])
                S.op("pool", lambda e: e.tensor_tensor(out=st2[:, c + 1:c + 2], in0=st2[:, c + 1:c + 2], in1=negh, op=ALU.pow), r=["grs%d" % j, "negh"], w=["grs%d" % j])
                x1t, x1r = x1_tiles[j]
                S.op("dve", lambda e: e.scalar_tensor_tensor(out=ftok, in0=ftok, scalar=st2[:, c + 1:c + 2], in1=gqf, op0=ALU.mult, op1=ALU.mult), r=fres + ["grs%d" % j, "gqf"], w=fres)
                S.op("dve", lambda e: e.tensor_tensor(out=ftok, in0=ftok, in1=x1t, op=ALU.add), r=fres + [x1r], w=fres)
                dma("sp", yout[j], ftok, r=fres, w=["o_y"], stream="d_yoh%d" % (j % 2))
            return run
        return [make_tail(j) for j in range(n_act // 128)]

    X1a = T["X1"][0][:, 0, :]
    dma("sp", XH[0:2, :], x1s[126:128, :], w=["XH"], stream="d_x2h")
    dma("sp", X1a, x1s[(NOWN + 1) * 128:(NOWN + 2) * 128, :], w=["X10a"], stream="d_x2a")
    ffn_front_a(2, XH, "XH", 0)
    ffn_front_b(2, 0, 0)
    ffn_front_a(128, X1a, "X10a", 1)
    ffn_front_b(128, 2, 1)
    for tl in ffn_tile(130, True, [(X1a, "X10a")], [ys]):
        tl()
    for g0 in range(0, 11, 4):
        n_ = min(4, 11 - g0)
        b = nb()

        def f(e, b=b, g0=g0, n_=n_):
            ins = None
            for i in range(n_):
                ins = e.transpose(out=bank(b)[:, i * 128:(i + 1) * 128], in_=ncs[:, (g0 + i) * 128:(g0 + i + 1) * 128], identity=idf)
            return ins
        S.op("pe", f, r=["ncs", "idf"], w=["b%d" % b])
        S.op("act", lambda e, b=b, g0=g0, n_=n_: e.copy(out=osb11[:, g0:g0 + n_, :], in_=bank(b)[:, 0:n_ * 128].rearrange("p (a c) -> p a c", a=n_)), r=["b%d" % b], w=["scvr0"])
    dma("pool", cvs.rearrange("(c p) f -> p c f", p=128), osb11, r=["scvr0"], w=["o_cvs"], stream="d_o11")
    if KSTOP == 7:
        return finish()
    S.barrier()
    carve_p2(256)
    osb = T["osb"]
    def p2_load(ti):
        xt = T["X1"][ti % 2]
        for j in range(2):
            row = 1 + ti * 2 + j
            dma("sp", xt[:, j, :], x1s[row * 128:(row + 1) * 128, :], w=["X1%d%s" % (ti % 2, "ab"[j])], stream="d_x2%d%s" % (ti % 2, "ab"[j]))

    def p2_fa(ti, j):
        return lambda: ffn_front_a(128, T["X1"][ti % 2][:, j, :], "X1%d%s" % (ti % 2, "ab"[j]), j)

    def p2_fb(ti, j):
        return lambda: ffn_front_b(128, j * 128, j, hsel=ti % 2)

    NT = NOWN // 2
    p2_load(0)
    for j in range(2):
        p2_fa(0, j)()
        p2_fb(0, j)()
    pending = []
    for ti in range(NT):
        xt = T["X1"][ti % 2]
        res = ["X1%d%s" % (ti % 2, "ab"[j]) for j in range(2)]
        hooks = {}
        if pending:
            hooks[1] = [pending[0]]
            hooks[3] = [pending[1]]
        if ti + 1 < NT:
            hooks[5] = [lambda ti=ti: p2_load(ti + 1)]
            hooks[7] = [p2_fa(ti + 1, 0)]
            hooks[9] = [p2_fa(ti + 1, 1)]
            hooks[13] = [p2_fb(ti + 1, 0)]
            hooks[15] = [p2_fb(ti + 1, 1)]
        pending = ffn_tile(256, False, [(xt[:, j, :], res[j]) for j in range(2)], [yp[(ti * 2 + j) * 128:(ti * 2 + j + 1) * 128, :] for j in range(2)],
                           hooks=hooks, hsel=ti % 2)
    for tl in pending:
        tl()
    b = nb()
    S.op("pe", lambda e, b=b: e.transpose(out=bank(b)[0:88, 0:128], in_=flat(carry), identity=idf), r=["carry%d" % f_ for f_ in range(44)] + ["idf"], w=["b%d" % b])
    S.op("act", lambda e, b=b, osb=osb: e.copy(out=osb[0:88, :], in_=bank(b)[0:88, 0:128]), r=["b%d" % b], w=[T["osb_res"]])
    dma("pool", cvp, osb[0:88, :], r=[T["osb_res"]], w=["o_cvp"], stream="d_o12")

    return finish()


_CACHE = {}


def _rope_tables():
    half = 32
    inv = (10000.0 ** (-np.arange(half, dtype=np.float32) / half)).astype(np.float32)
    return inv


def _host_consts(core):
    inv = _rope_tables()
    pos = np.zeros((NPB + 1, 128), np.float32)
    for b in range(NPB):
        pos[b] = core * TOK - 256 + b * 128 + np.arange(128)
    pos[NPB] = PAST + (np.arange(128) % 8)
    ang = (pos[:, :, None].astype(np.float32) * inv[None, None, :]).astype(np.float32)
    c = np.cos(ang).astype(np.float32)
    s = np.sin(ang).astype(np.float32)
    ropec = np.concatenate([c, c], axis=-1).astype(np.float32)
    ropes = np.concatenate([-s, s], axis=-1).astype(np.float32)
    i = np.arange(128)[:, None]
    j = np.arange(128)[None, :]
    maskp = np.concatenate([np.where(j > i, 0.0, NEG), np.where(j <= i, 0.0, NEG)], axis=1).astype(np.float32)
    maskf = maskp.copy()
    if core == 0:
        maskf[:, 0:128] = NEG
    NEG8 = 8.0 * NEG
    maskp8 = np.concatenate([np.where(j > i, 0.0, NEG8), np.where(j <= i, 0.0, NEG8)], axis=1).astype(np.float32)
    maskf8 = maskp8.copy()
    if core == 0:
        maskf8[:, 0:128] = NEG8
    t = (np.arange(128) % 8)[:, None]
    q = (np.arange(128) // 8)[:, None]
    tt = (np.arange(128) % 8)[None, :]
    qq = (np.arange(128) // 8)[None, :]
    masks = np.concatenate([np.where(j >= t + 1, 0.0, NEG), np.where((qq == q) & (tt <= t), 0.0, NEG)], axis=1).astype(np.float32)
    tril = (j <= i).astype(np.float32)
    bdm = ((qq == q) & (tt <= t)).astype(np.float32)
    e8 = np.tile(np.eye(8, dtype=np.float32), (1, 16))
    hv = np.full((128, 1), 0.0 if core == 0 else 1.0, np.float32)
    return dict(ropec=ropec, ropes=ropes, maskp=maskp, maskf=maskf, masks=masks, maskp8=maskp8, maskf8=maskf8, tril=tril, bdm=bdm, e8=e8, hv=hv,
                ident=np.eye(128, dtype=np.float32))


def kernel(x_prompt, x_sample, cache_win_k, cache_win_v, cache_mem_k, cache_mem_v, state_conv, mem_prompt,
           pre_mix_g, w_in, attn_sinks, sg_ln_g, sg_ln_b, sg_w, sg_b, mem_norm_g, w_mem_kv, w_o,
           post_mix_g, pre_ffn_g, w_up, conv_w, conv_b, w_down, post_ffn_g):
    f = lambda a: np.ascontiguousarray(np.asarray(a, dtype=np.float32))
    if "nc" not in _CACHE:
        _CACHE["nc"] = build_program()
    nc = _CACHE["nc"]
    xpr = f(x_prompt)[0]
    xpad = np.concatenate([np.zeros((256, D), np.float32), xpr], axis=0)
    shared = dict(
        memp=f(mem_prompt)[0], w_in=f(w_in)[0], w_mkv=f(w_mem_kv)[0], w_o=f(w_o)[0].reshape(1536, D), w_up=f(w_up)[0], w_down=f(w_down)[0],
        g_pre=f(pre_mix_g), g_post=f(post_mix_g), g_pffn=f(pre_ffn_g), g_qffn=f(post_ffn_g), g_mem=f(mem_norm_g),
        ln_g=f(sg_ln_g), ln_b=f(sg_ln_b), sinks=f(attn_sinks), sg_w=f(sg_w)[0], sg_b=f(sg_b)[0], conv_w=f(conv_w)[0], conv_b=f(conv_b),
    )
    in_maps = []
    for c in range(NCORES):
        m = dict(shared)
        m.update(_host_consts(c))
        m["xp"] = np.ascontiguousarray(xpad[c * TOK:c * TOK + NPB * 128])
        sl = slice(c * SEQS, (c + 1) * SEQS)
        m["xs"] = f(x_sample)[sl].reshape(128, D)
        m["cwk"] = f(cache_win_k)[0, sl].reshape(SEQS, 128, 128)
        m["cwv"] = f(cache_win_v)[0, sl].reshape(SEQS, 128, 128)
        m["cmk"] = f(cache_mem_k)[0, sl].reshape(SEQS, 256, 512)
        m["cmv"] = f(cache_mem_v)[0, sl].reshape(SEQS, 256, 512)
        m["scv"] = f(state_conv)[0, sl].reshape(32, 2 * DFF)
        in_maps.append(m)
    res = run_bass_kernel_spmd(nc, in_maps, core_ids=list(range(NCORES))).results
    y_p = np.concatenate([r["yp"] for r in res], axis=0)[None]
    y_s = np.concatenate([r["ys"].reshape(SEQS, 8, D) for r in res], axis=0)
    last = res[NCORES - 1]
    wk_p = last["wkp"].reshape(1, 1, 128, 2, 64)
    wv_p = last["wvp"].reshape(1, 1, 128, 2, 64)
    mk_p = res[0]["mkp"].reshape(1, 1, 256, 4, 128)
    mv_p = res[0]["mvp"].reshape(1, 1, 256, 4, 128)
    cv_p = last["cvp"].reshape(1, 1, 2, 2 * DFF)
    wk_s = np.concatenate([r["wks"] for r in res], axis=0).reshape(1, 128, 128, 2, 64)
    wv_s = np.concatenate([r["wvs"] for r in res], axis=0).reshape(1, 128, 128, 2, 64)
    sgv_s = np.concatenate([r["sgv"].reshape(SEQS, 8, 512) for r in res], axis=0)[None]
    cv_s = np.concatenate([r["cvs"].reshape(SEQS, 2, 2 * DFF) for r in res], axis=0)[None]
    outs = (y_p, y_s, wk_p, wv_p, mk_p, mv_p, cv_p, wk_s, wv_s, sgv_s, cv_s)
    return tuple(np.ascontiguousarray(o, dtype=np.float32) for o in outs)
```

```python
import numpy as np
import concourse.bass as bass
import concourse.mybir as mybir
from concourse.bass_utils import run_bass_kernel_spmd

F32 = mybir.dt.float32
BF16 = mybir.dt.bfloat16
AF = mybir.ActivationFunctionType
ALU = mybir.AluOpType
AX = mybir.AxisListType

NCORES = 8
D = 1024
SEQ = 16384
TOK = SEQ // NCORES
NOWN = TOK // 128
NPB = NOWN + 2
SEQS = 16
INW = 5376
DFF = 2816
NFC = 22
EPS = 1e-6
NEG = -30000.0
PAST = 16384
ZG = [(0, 512), (512, 256), (768, 512), (1280, 512), (1792, 512)] + [(2304 + 512 * i, 512) for i in range(6)]


class Sched:
    ENGS = ("pe", "act", "dve", "pool", "sp")

    def __init__(self, nc):
        self.nc = nc
        self.q = {e: [] for e in self.ENGS}
        self.sem = {}
        self.cnt = {}
        self.waited = {e: {} for e in self.ENGS}
        self.lastw = {}
        self.readers = {}
        self.snap = {}
        self.seq = 0
        for e in ("pe", "act", "dve", "pool"):
            self._mk(e)

    def _mk(self, name):
        self.sem[name] = self.nc.alloc_semaphore("s_" + name)
        self.cnt[name] = 0

    def op(self, eng, fn, r=(), w=(), dma=None, after=()):
        deps = {}

        def need(tok):
            if tok is None:
                return
            s, v = tok
            if deps.get(s, 0) < v:
                deps[s] = v

        w = list(w) + [x for x in r if len(x) == 2 and x[0] == "b" and x[1].isdigit()]
        r = [x for x in r if not (len(x) == 2 and x[0] == "b" and x[1].isdigit())]
        for x in r:
            need(self.lastw.get(x))
        for x in w:
            need(self.lastw.get(x))
            for t in self.readers.get(x, ()):
                need(t)
        for t in after:
            need(t)
        waits = []
        wd = self.waited[eng]
        for s, v in sorted(deps.items(), key=lambda kv: -self.snap.get(kv, (0, None))[0]):
            if s == "pe" and eng == "pe":
                continue
            if wd.get(s, 0) >= v:
                continue
            wd[s] = v
            waits.append((s, v))
            sn = self.snap.get((s, v))
            if sn is not None and sn[1] is not None:
                for k2, v2 in sn[1].items():
                    if wd.get(k2, 0) < v2:
                        wd[k2] = v2
        if dma is not None:
            if dma not in self.sem:
                self._mk(dma)
            stream, inc = dma, 16
        else:
            stream, inc = eng, 1
        self.cnt[stream] += inc
        tok = (stream, self.cnt[stream])
        self.seq += 1
        self.snap[tok] = (self.seq, dict(wd))
        self.q[eng].append((waits, fn, stream, inc))
        for x in w:
            self.lastw[x] = tok
            self.readers[x] = []
        for x in r:
            self.readers.setdefault(x, []).append(tok)
        return tok

    def barrier(self, keep_streams=(), keep_res=()):
        for e in self.ENGS:
            waits = []
            for s, v in self.cnt.items():
                if s.startswith(tuple(keep_streams)) if keep_streams else False:
                    continue
                if v > self.waited[e].get(s, 0):
                    self.waited[e][s] = v
                    waits.append((s, v))
            self.q[e].append((waits, None, None, 0))
        self.lastw = {k: v for k, v in self.lastw.items() if keep_res and k.startswith(tuple(keep_res))}
        self.readers = {}

    def emit(self):
        nc = self.nc
        handles = {"pe": "tensor", "act": "scalar", "dve": "vector", "pool": "gpsimd", "sp": "sync"}
        with nc.Block() as block:
            for en in self.ENGS:
                def make(en):
                    def f(eng):
                        for waits, fn, stream, inc in self.q[en]:
                            for s, v in waits:
                                eng.wait_ge(self.sem[s], v)
                            if fn is None:
                                continue
                            ins = fn(eng)
                            ins.then_inc(self.sem[stream], inc)
                    return f
                getattr(block, handles[en])(make(en))


class Arena:
    def __init__(self, nc, nbytes):
        self.t = nc.alloc_sbuf_tensor("arena", [128, nbytes // 4], F32)
        self.cap = nbytes
        self.off = 0

    def carve(self, shape_free, dtype):
        n = int(np.prod(shape_free))
        nb = n * (2 if dtype == BF16 else 4)
        nb = (nb + 31) // 32 * 32
        assert self.off + nb <= self.cap, ("SBUF arena overflow", self.off, nb, self.cap)
        ap = self.t[:, self.off // 4:(self.off + nb) // 4]
        self.off += nb
        if dtype == BF16:
            ap = ap.bitcast(BF16)
        ap = ap[:, 0:n]
        if len(shape_free) == 2:
            ap = ap.rearrange("p (a b) -> p a b", a=shape_free[0])
        elif len(shape_free) == 3:
            ap = ap.rearrange("p (a b c) -> p a b c", a=shape_free[0], b=shape_free[1])
        elif len(shape_free) == 4:
            ap = ap.rearrange("p (a b c d) -> p a b c d", a=shape_free[0], b=shape_free[1], c=shape_free[2])
        return ap


def flat(ap):
    n = len(ap.shape)
    if n == 2:
        return ap
    if n == 3:
        return ap.rearrange("p a b -> p (a b)")
    if n == 4:
        return ap.rearrange("p a b c -> p (a b c)")
    return ap.rearrange("p a b c d -> p (a b c d)")


def build_program():
    nc = bass.Bass("TRN2", target_bir_lowering=False)
    S = Sched(nc)

    def din(name, shape):
        return nc.dram_tensor(name, list(shape), F32, kind="ExternalInput").ap()

    def dout(name, shape):
        return nc.dram_tensor(name, list(shape), F32, kind="ExternalOutput").ap()

    xp = din("xp", [NPB * 128, D])
    xs = din("xs", [128, D])
    cwk = din("cwk", [SEQS, 128, 128])
    cwv = din("cwv", [SEQS, 128, 128])
    cmk = din("cmk", [SEQS, 256, 512])
    cmv = din("cmv", [SEQS, 256, 512])
    scv = din("scv", [32, 2 * DFF])
    memp = din("memp", [256, D])
    w_in = din("w_in", [D, INW])
    w_mkv = din("w_mkv", [D, 1024])
    w_o = din("w_o", [1536, D])
    w_up = din("w_up", [D, 2 * DFF])
    w_down = din("w_down", [DFF, D])
    g_pre = din("g_pre", [1, D])
    g_post = din("g_post", [1, D])
    g_pffn = din("g_pffn", [1, D])
    g_qffn = din("g_qffn", [1, D])
    g_mem = din("g_mem", [1, D])
    ln_g = din("ln_g", [1, 512])
    ln_b = din("ln_b", [1, 512])
    sinks = din("sinks", [1, 8])
    sg_w = din("sg_w", [4, 128, 128])
    sg_b = din("sg_b", [4, 128])
    conv_w = din("conv_w", [3, 2 * DFF])
    conv_b = din("conv_b", [1, 2 * DFF])
    ident = din("ident", [128, 128])
    ropec = din("ropec", [NPB + 1, 128, 64])
    ropes = din("ropes", [NPB + 1, 128, 64])
    maskp_d = din("maskp", [128, 256])
    maskf_d = din("maskf", [128, 256])
    maskp8_d = din("maskp8", [128, 256])
    maskf8_d = din("maskf8", [128, 256])
    masks_d = din("masks", [128, 256])
    tril_d = din("tril", [128, 128])
    bdm_d = din("bdm", [128, 128])
    e8_d = din("e8", [8, 128])
    hv_d = din("hv", [128, 1])

    yp = dout("yp", [TOK, D])
    ys = dout("ys", [128, D])
    wkp = dout("wkp", [128, 128])
    wvp = dout("wvp", [128, 128])
    mkp = dout("mkp", [256, 512])
    mvp = dout("mvp", [256, 512])
    cvp = dout("cvp", [88, 128])
    wks = dout("wks", [SEQS, 128, 128])
    wvs = dout("wvs", [SEQS, 128, 128])
    sgv = dout("sgv", [128, 512])
    cvs = dout("cvs", [1408, 128])
    x1s = nc.dram_tensor("x1s", [(NOWN + 2) * 128, D], F32).ap()

    A = Arena(nc, 212480)
    PS = nc.alloc_psum_tensor("PS", [128, 4096], F32)

    def bank(i):
        return PS[:, i * 512:(i + 1) * 512]

    def bankb(i):
        return bank(i).bitcast(BF16)

    rot = {"i": 0, "n": 6}

    def nb():
        i = rot["i"] % rot["n"]
        rot["i"] = (i + 1) % rot["n"]
        return i

    out_toks = []

    def dma(eng, out, in_, r=(), w=(), stream=None, after=()):
        return S.op(eng, lambda e: e.dma_start(out=out, in_=in_), r=r, w=w, dma=stream, after=after)

    import os
    KSTOP = int(os.environ.get("KSTOP", "99"))

    def finish():
        fin = {}
        for s_, v in S.cnt.items():
            if s_.startswith("d_"):
                fin[s_] = v
        S.q["sp"].append(([(s_, v) for s_, v in fin.items() if v > S.waited["sp"].get(s_, 0)], None, None, 0))
        S.emit()
        return nc

    idf = A.carve([128], F32)
    idb = A.carve([128], BF16)
    negh = A.carve([1], F32)
    hvt = A.carve([1], F32)
    carry = A.carve([2, 44], F32)
    cwt = A.carve([4, 44], F32)
    maskp8 = A.carve([256], BF16)
    maskf8 = A.carve([256], BF16)
    WA_OFF = A.off
    WA = A.carve([8, 5632], BF16)
    WB_OFF = A.off
    WB = A.carve([22, 1024], BF16)
    WIN = flat(WA)[:, 0:8 * INW].rearrange("p (k e) -> p k e", k=8)
    WO = WB[:, 0:12, :]
    WMKV = WB[:, 12:20, :]
    WUP = WA
    WDN = WB
    P_MARK = A.off

    dma("sp", idf, ident, w=["idf"], stream="d_c0")
    dma("pool", idb, ident, w=["idb"], stream="d_c1")
    dma("sp", hvt, hv_d, w=["hvt"], stream="d_c2")
    S.op("dve", lambda e: e.memset(negh, -0.5), w=["negh"])

    dma("pool", maskp8, maskp8_d, w=["maskp8"], stream="d_c24")
    dma("pool", maskf8, maskf8_d, w=["maskf8"], stream="d_c25")
    w_in_r = w_in.rearrange("(k p) e -> p k e", p=128)
    w_mkv_r = w_mkv.rearrange("(k p) e -> p k e", p=128)
    w_o_r = w_o.rearrange("(k p) e -> p k e", p=128)
    for half in range(2):
        dma("pool", WMKV[:, :, half * 512:(half + 1) * 512], w_mkv_r[:, :, half * 512:(half + 1) * 512],
            w=["wmkv%d" % half], stream="d_WB%d" % (12 + half))
    for ci, (c0, cw) in enumerate(ZG):
        dma("pool", WIN[:, :, c0:c0 + cw], w_in_r[:, :, c0:c0 + cw], w=["win%d" % ci], stream="d_WA%d" % ci)
    for br in range(3):
        dma("pool", WO[:, br * 4:(br + 1) * 4, :], w_o_r[:, br * 4:(br + 1) * 4, :], w=["wo%d" % br], stream="d_WB%d" % br)

    def rstd_from(ms, out, eps, tag):
        S.op("pool", lambda e: e.tensor_scalar(out=out, in0=ms, scalar1=eps, scalar2=None, op0=ALU.add), r=[tag + "ms"], w=[tag + "rs"])
        S.op("pool", lambda e: e.tensor_tensor(out=out, in0=out, in1=negh, op=ALU.pow), r=[tag + "rs", "negh"], w=[tag + "rs"])

    def transposes(srcs, src_res, dst, dst_res, dt, evac_eng="act"):
        b = nb()
        n = len(srcs)
        pv = bankb(b) if dt == BF16 else bank(b)
        idt = idb if dt == BF16 else idf

        def f(e):
            ins = None
            for i, s in enumerate(srcs):
                ins = e.transpose(out=pv[:, i * 128:(i + 1) * 128], in_=s, identity=idt)
            return ins
        S.op("pe", f, r=list(src_res) + ["idb", "idf"], w=["b%d" % b])
        src = pv[:, 0:n * 128]
        if len(dst.shape) == 3:
            src = src.rearrange("p (a b) -> p a b", a=dst.shape[1])
        if evac_eng == "act":
            S.op("act", lambda e: e.copy(out=dst, in_=src), r=["b%d" % b], w=list(dst_res))
        else:
            S.op(evac_eng, lambda e: e.tensor_copy(out=dst, in_=src), r=["b%d" % b], w=list(dst_res))

    gpre = A.carve([D], F32)
    gpost = A.carve([D], F32)
    lng = A.carve([512], F32)
    lnb = A.carve([512], F32)
    snk = A.carve([8], F32)
    maskp = A.carve([256], F32)
    maskf = A.carve([256], F32)
    masks = A.carve([256], F32)
    nsnk = A.carve([8], F32)
    WT = A.carve([4, 128], BF16)
    WTS = A.carve([4, 128], BF16)
    sgbT = A.carve([4], F32)
    sgbS = A.carve([4], F32)
    mkT = A.carve([4, 256], BF16)
    mvb = A.carve([2, 512], BF16)
    P1_MARK = A.off

    for t, src, nm, st in [(gpre, g_pre, "gpre", 3), (gpost, g_post, "gpost", 4), (lng, ln_g, "lng", 5), (lnb, ln_b, "lnb", 6), (snk, sinks, "snk", 7)]:
        dma("sp", t, src.partition_broadcast(128), w=[nm], stream="d_c%d" % st)
    dma("sp", maskp, maskp_d, w=["maskp"], stream="d_c8")
    dma("sp", maskf, maskf_d, w=["maskf"], stream="d_c9")
    dma("sp", masks, masks_d, w=["masks"], stream="d_c10")
    S.op("dve", lambda e: e.tensor_scalar(out=nsnk, in0=snk, scalar1=-1.0, scalar2=None, op0=ALU.mult), r=["snk"], w=["nsnk"])

    gmem = A.carve([D], F32)
    mx0 = A.carve([D], F32)
    mx1 = A.carve([D], F32)
    mnb = A.carve([2, D], BF16)
    mnT = A.carve([8, 256], BF16)
    junk0 = A.carve([D], BF16)
    st0 = A.carve([8], F32)
    trilt = A.carve([128], F32)
    bdmt = A.carve([128], F32)
    e8t = A.carve([128], F32)
    wraw = A.carve([4, 128], F32)
    wmsk = A.carve([4, 128], BF16)
    w8 = A.carve([4, 8], F32)
    r8 = A.carve([4, 16, 8], F32)
    wrep = A.carve([4, 128], BF16)
    sgbr = A.carve([128], F32)
    mo = A.carve([2, 512], F32)
    mo2 = A.carve([2, 512], F32)
    cinp = [A.carve([128], F32) for _ in range(2)]

    dma("sp", gmem, g_mem.partition_broadcast(128), w=["gmem"], stream="d_c11")
    dma("sp", mx0, memp[0:128, :], w=["mx0"], stream="d_c12")
    dma("sp", mx1, memp[128:256, :], w=["mx1"], stream="d_c13")
    dma("sp", trilt, tril_d, w=["trilt"], stream="d_c14")
    dma("sp", bdmt, bdm_d, w=["bdmt"], stream="d_c15")
    dma("sp", e8t[0:8, :], e8_d, w=["e8t"], stream="d_c16")
    dma("sp", wraw, sg_w.rearrange("g t s -> t g s"), w=["wraw"], stream="d_c17")
    dma("sp", w8[0:8, :, :], sg_w[:, 0:8, 0:8].rearrange("g t s -> t g s"), w=["w8"], stream="d_c18")
    dma("sp", sgbr[0:4, :], sg_b, w=["sgbr"], stream="d_c19")

    if KSTOP == -1:
        return finish()
    for mb, mx in enumerate((mx0, mx1)):
        S.op("act", lambda e, mx=mx, mb=mb: e.activation(out=junk0, in_=mx, func=AF.Square, scale=1.0 / 32.0, accum_out=st0[:, mb:mb + 1]),
             r=["mx%d" % mb], w=["junk0", "m%dms" % mb])
        rstd_from(st0[:, mb:mb + 1], st0[:, 2 + mb:3 + mb], EPS, "m%d" % mb)
        S.op("dve", lambda e, mx=mx, mb=mb: e.scalar_tensor_tensor(out=mnb[:, mb, :], in0=mx, scalar=st0[:, 2 + mb:3 + mb], in1=gmem, op0=ALU.mult, op1=ALU.mult),
             r=["mx%d" % mb, "m%drs" % mb, "gmem"], w=["mnb%d" % mb])
        transposes([mnb[:, mb, k * 128:(k + 1) * 128] for k in range(8)], ["mnb%d" % mb],
                   mnT[:, :, mb * 128:(mb + 1) * 128], ["mnT%d" % mb], BF16)
    for mb in range(2):
        for kv in range(2):
            b = nb()

            def f(e, mb=mb, kv=kv, b=b):
                ins = None
                for k in range(8):
                    ins = e.matmul(bank(b), lhsT=mnT[:, k, mb * 128:(mb + 1) * 128], rhs=WMKV[:, k, kv * 512:(kv + 1) * 512], start=(k == 0), stop=(k == 7))
                return ins
            S.op("pe", f, r=["mnT0", "mnT1", "wmkv%d" % kv], w=["b%d" % b])
            dst = (mo if kv == 0 else mo2)[:, mb, :]
            S.op("act", lambda e, dst=dst, b=b: e.copy(out=dst, in_=bank(b)), r=["b%d" % b], w=["mo%d%d" % (kv, mb)])
            if kv == 1:
                S.op("dve", lambda e, b=b, mb=mb: e.tensor_copy(out=mvb[:, mb, :], in_=bank(b)), r=["b%d" % b], w=["mvb"])
            out_toks.append(dma("sp", (mkp if kv == 0 else mvp)[mb * 128:(mb + 1) * 128, :], dst, r=["mo%d%d" % (kv, mb)], w=["o_m%d%d" % (kv, mb)], stream="d_o%d" % (mb * 2 + kv)))
    for h in range(4):
        b = nb()

        def f(e, h=h, b=b):
            ins = None
            for k in range(8):
                ins = e.matmul(bank(b)[:, 0:256], lhsT=WMKV[:, k, h * 128:(h + 1) * 128], rhs=mnT[:, k, :], start=(k == 0), stop=(k == 7))
            return ins
        S.op("pe", f, r=["mnT0", "mnT1", "wmkv0"], w=["b%d" % b])
        S.op("act", lambda e, h=h, b=b: e.copy(out=mkT[:, h, :], in_=bank(b)[:, 0:256]), r=["b%d" % b], w=["mkT"])

    if KSTOP == -2:
        return finish()
    for g in range(4):
        S.op("dve", lambda e, g=g: e.tensor_tensor(out=wmsk[:, g, :], in0=wraw[:, g, :], in1=trilt, op=ALU.mult), r=["wraw", "trilt"], w=["wmsk"])
    transposes([wmsk[:, g, :] for g in range(4)], ["wmsk"], flat(WT), ["WT"], BF16)
    if KSTOP == -3:
        return finish()
    S.op("dve", lambda e: e.tensor_copy(out=r8[0:8], in_=bass.AP(w8.tensor, w8[0:8].offset, [list(w8[0:8].ap[0]), [8, 4], [0, 16], [1, 8]])), r=["w8"], w=["r8"])
    for g in range(4):
        b = nb()
        S.op("pe", lambda e, g=g, b=b: e.matmul(bank(b)[:, 0:128], lhsT=e8t[0:8, :], rhs=r8[0:8, g].rearrange("p a b -> p (a b)"), start=True, stop=True),
             r=["e8t", "r8"], w=["b%d" % b])
        S.op("dve", lambda e, g=g, b=b: e.tensor_tensor(out=wrep[:, g, :], in0=bank(b)[:, 0:128], in1=bdmt, op=ALU.mult), r=["b%d" % b, "bdmt"], w=["wrep"])
    transposes([wrep[:, g, :] for g in range(4)], ["wrep"], flat(WTS), ["WTS"], BF16)
    if KSTOP == -4:
        return finish()
    b = nb()
    S.op("pe", lambda e, b=b: e.transpose(out=bank(b)[:, 0:4], in_=sgbr[0:4, :], identity=idf[0:4, 0:4]), r=["sgbr", "idf"], w=["b%d" % b])
    S.op("act", lambda e, b=b: e.copy(out=sgbT, in_=bank(b)[:, 0:4]), r=["b%d" % b], w=["sgbT"])
    b = nb()
    S.op("pe", lambda e, b=b: e.matmul(bank(b)[:, 0:4], lhsT=e8t[0:8, :], rhs=sgbT[0:8, :], start=True, stop=True), r=["e8t", "sgbT"], w=["b%d" % b])
    S.op("act", lambda e, b=b: e.copy(out=sgbS, in_=bank(b)[:, 0:4]), r=["b%d" % b], w=["sgbS"])

    for part in range(2):
        cin = cinp[part]
        for r_ in range(2):
            rr = part * 2 + r_
            src = (conv_w[rr] if rr < 3 else conv_b[0]).rearrange("(c p) -> c p", p=128)
            dma("sp", cin[r_ * 44:(r_ + 1) * 44, :], src, w=["cin%d" % part], stream="d_c%d" % (20 + rr))
        b = nb()
        S.op("pe", lambda e, b=b, cin=cin: e.transpose(out=bank(b)[:, 0:88], in_=cin[0:88, :], identity=idf[0:88, 0:88]), r=["cin%d" % part, "idf"], w=["b%d" % b])
        S.op("act", lambda e, b=b, part=part: e.copy(out=flat(cwt)[:, part * 88:(part + 1) * 88], in_=bank(b)[:, 0:88]), r=["b%d" % b], w=["cwt"])
    if KSTOP == 1:
        return finish()
    S.barrier(keep_streams=("d_WA", "d_WB0", "d_WB1", "d_WB2"), keep_res=("win", "wo"))
    A.off = P1_MARK

    TB = Arena.__new__(Arena)
    TB.t = A.t
    TB.off = WB_OFF + 12 * 2048
    TB.cap = WB_OFF + 22 * 2048
    tg = TB.carve([3072], F32)
    acc = TB.carve([D], F32)
    tmp2 = TB.carve([D], F32)
    XIN = [A.carve([D], F32) for _ in range(2)]
    RC = [A.carve([64], F32) for _ in range(2)]
    RS = [A.carve([64], F32) for _ in range(2)]
    hb = A.carve([D], BF16)
    junk = A.carve([D], BF16)
    hT = A.carve([8, 128], BF16)
    stt = A.carve([128], F32)
    tq = A.carve([512], F32)
    uq = A.carve([512], F32)
    qr = A.carve([512], BF16)
    kr = A.carve([128], F32)
    kbb = A.carve([128], BF16)
    vf = A.carve([128], F32)
    vring = A.carve([2, 128], BF16)
    kTring = A.carve([2, 128], BF16)
    qT = A.carve([4, 128], BF16)
    usb = A.carve([512], F32)
    gvs = A.carve([512], F32)
    vn = gvs
    vnb = A.carve([512], BF16)
    sgo = A.carve([512], BF16)
    mqb = A.carve([512], BF16)
    mqT = A.carve([4, 128], BF16)
    ssb8 = A.carve([8, 256], F32)
    pb8 = A.carve([8, 256], BF16)
    pT8 = A.carve([8, 2, 128], BF16)
    ssb = ssb8[:, 0:4]
    pb = pb8[:, 0:4]
    pT = pT8[:, 0:4]
    em, pmb, pmT = ssb, pb, pT
    brT = [A.carve([4, 128], BF16) for _ in range(3)]
    P1_END = A.off

    def stat(i, n=1):
        return stt[:, i:i + n]

    def head(xsrc, slot, ridx):
        xin = XIN[slot]
        xr = "xin%d" % slot
        dma("sp", xin, xsrc, w=[xr], stream="d_xin%d" % slot)
        dma("sp", RC[slot], ropec[ridx], w=["rc%d" % slot], stream="d_rc%d" % slot)
        dma("sp", RS[slot], ropes[ridx], w=["rs%d" % slot], stream="d_rs%d" % slot)
        S.op("act", lambda e: e.activation(out=junk, in_=xin, func=AF.Square, scale=1.0 / 32.0, accum_out=stat(0)), r=[xr], w=["junk", "ams"])
        rstd_from(stat(0), stat(1), EPS, "a")
        S.op("dve", lambda e: e.scalar_tensor_tensor(out=hb, in0=xin, scalar=stat(1), in1=gpre, op0=ALU.mult, op1=ALU.mult), r=[xr, "ars", "gpre"], w=["hb"])
        transposes([hb[:, k * 128:(k + 1) * 128] for k in range(8)], ["hb"], flat(hT), ["hT"], BF16)

    def zmm(gi, b):
        c0, cw = ZG[gi]

        def f(e):
            ins = None
            for k in range(8):
                ins = e.matmul(bank(b)[:, 0:cw], lhsT=hT[:, k, :], rhs=WIN[:, k, c0:c0 + cw], start=(k == 0), stop=(k == 7))
            return ins
        S.op("pe", f, r=["hT", "win%d" % gi], w=["b%d" % b])

    def rope_ops(src, nh, slot, b, t_, u_):
        src4 = src.rearrange("p (h a i) -> p h a i", h=nh, a=2)
        swp = bass.AP(src.tensor, src.offset + 32, [list(src.ap[0]), [64, nh], [-32, 2], [1, 32]])
        rc, rs_ = RC[slot], RS[slot]
        cb = bass.AP(rc.tensor, rc.offset, [list(rc.ap[0]), [0, nh], [32, 2], [1, 32]])
        sb = bass.AP(rs_.tensor, rs_.offset, [list(rs_.ap[0]), [0, nh], [32, 2], [1, 32]])
        t4 = t_.rearrange("p (h a i) -> p h a i", h=nh, a=2)
        u4 = u_.rearrange("p (h a i) -> p h a i", h=nh, a=2)
        S.op("dve", lambda e: e.tensor_tensor(out=t4, in0=src4, in1=cb, op=ALU.mult), r=["b%d" % b, "rc%d" % slot], w=["tq"])
        S.op("dve", lambda e: e.tensor_tensor(out=u4, in0=swp, in1=sb, op=ALU.mult), r=["b%d" % b, "rs%d" % slot], w=["uq"])

    def proj_kv(slot):
        b1 = nb()
        zmm(1, b1)
        rope_ops(bank(b1)[:, 0:128], 2, slot, b1, tq[:, 0:128], uq[:, 0:128])
        S.op("dve", lambda e: e.tensor_tensor(out=kr, in0=tq[:, 0:128], in1=uq[:, 0:128], op=ALU.add), r=["tq", "uq"], w=["kr"])
        S.op("act", lambda e: e.copy(out=kbb, in_=kr), r=["kr"], w=["kbb"])
        S.op("act", lambda e: e.copy(out=vf, in_=bank(b1)[:, 128:256]), r=["b%d" % b1], w=["vf"])

    def proj_q(slot):
        b0 = nb()
        zmm(0, b0)
        rope_ops(bank(b0), 8, slot, b0, tq, uq)
        qr_perm = qr.rearrange("p (h e d) -> p e h d", h=4, e=2)
        S.op("dve", lambda e: e.tensor_tensor(out=qr_perm, in0=tq.rearrange("p (e h d) -> p e h d", e=2, h=4), in1=uq.rearrange("p (e h d) -> p e h d", e=2, h=4), op=ALU.add),
             r=["tq", "uq"], w=["qr"])

    def proj_sgu():
        b2 = nb()
        zmm(2, b2)
        S.op("act", lambda e: e.activation(out=usb, in_=bank(b2), func=AF.Gelu), r=["b%d" % b2], w=["usb"])

    def proj_sgv():
        b3 = nb()
        zmm(3, b3)
        S.op("act", lambda e: e.activation(out=gvs, in_=bank(b3), func=AF.Gelu, accum_out=stat(2)), r=["b%d" % b3], w=["gvs", "lsum"])
        S.op("dve", lambda e: e.tensor_scalar(out=stat(3), in0=stat(2), scalar1=-1.0 / 512.0, scalar2=None, op0=ALU.mult), r=["lsum"], w=["lnm"])
        S.op("dve", lambda e: e.tensor_scalar(out=gvs, in0=gvs, scalar1=stat(3), scalar2=None, op0=ALU.add), r=["gvs", "lnm"], w=["gvs"])
        S.op("act", lambda e: e.activation(out=junk[:, 0:512], in_=gvs, func=AF.Square, scale=float(512 ** -0.5), accum_out=stat(4)), r=["gvs"], w=["junk", "lms"])
        rstd_from(stat(4), stat(5), EPS, "l")
        S.op("dve", lambda e: e.scalar_tensor_tensor(out=gvs, in0=gvs, scalar=stat(5), in1=lng, op0=ALU.mult, op1=ALU.mult), r=["gvs", "lrs", "lng"], w=["gvs"])
        S.op("dve", lambda e: e.tensor_tensor(out=gvs, in0=gvs, in1=lnb, op=ALU.add), r=["gvs", "lnb"], w=["gvs"])
        S.op("act", lambda e: e.copy(out=vnb, in_=gvs), r=["gvs"], w=["vnb"])

    def proj_mq():
        b4 = nb()
        zmm(4, b4)
        S.op("act", lambda e: e.copy(out=mqb, in_=bank(b4)), r=["b%d" % b4], w=["mqb"])

    def proj_gates(lo, hi):
        for gi in range(lo, hi):
            bg = nb()
            zmm(5 + gi, bg)
            S.op("act", lambda e, gi=gi, bg=bg: e.activation(out=tg[:, gi * 512:(gi + 1) * 512], in_=bank(bg), func=AF.Tanh, scale=0.5), r=["b%d" % bg], w=["tg%d" % gi])

    def kv_publish(slot_kv):
        transposes([kbb], ["kbb"], kTring[:, slot_kv, :], ["kT%d" % slot_kv], BF16)
        S.op("act", lambda e: e.copy(out=vring[:, slot_kv, :], in_=vf), r=["vf"], w=["v%d" % slot_kv])

    def softmax_direct(src_res, src_ap, scale, sink_cols, dst32, dst32_res, dstb, dstb_res, so, sx, nh):
        M, NM, SM, DF, RI = so + 8, so + 8 + nh, so + 8 + 2 * nh, so + 8 + 3 * nh, so + 8 + 4 * nh
        src_res = list(src_res)
        S.op("dve", lambda e: e.tensor_reduce(out=stat(M, nh), in_=src_ap, axis=AX.X, op=ALU.max), r=src_res, w=["smx" + sx])
        if sink_cols is not None:
            S.op("dve", lambda e: e.scalar_tensor_tensor(out=stat(NM, nh), in0=stat(M, nh), scalar=-scale, in1=nsnk[:, sink_cols:sink_cols + nh], op0=ALU.mult, op1=ALU.min),
                 r=["smx" + sx, "nsnk"], w=["snm" + sx])
        else:
            S.op("dve", lambda e: e.tensor_scalar(out=stat(NM, nh), in0=stat(M, nh), scalar1=-scale, scalar2=None, op0=ALU.mult), r=["smx" + sx], w=["snm" + sx])

        def fexp(e):
            ins = None
            for h in range(nh):
                ins = e.activation(out=dst32[:, h, :], in_=src_ap[:, h, :], func=AF.Exp, bias=stat(NM + h), scale=scale, accum_out=stat(SM + h))
            return ins
        S.op("act", fexp, r=src_res + ["snm" + sx], w=[dst32_res, "ssum" + sx])
        if sink_cols is not None:
            S.op("dve", lambda e: e.tensor_tensor(out=stat(DF, nh), in0=snk[:, sink_cols:sink_cols + nh], in1=stat(NM, nh), op=ALU.add), r=["snk", "snm" + sx], w=["sdf" + sx])
            S.op("act", lambda e: e.activation(out=stat(DF, nh), in_=stat(DF, nh), func=AF.Exp), r=["sdf" + sx], w=["sdf" + sx])
            S.op("dve", lambda e: e.tensor_tensor(out=stat(SM, nh), in0=stat(SM, nh), in1=stat(DF, nh), op=ALU.add), r=["ssum" + sx, "sdf" + sx], w=["ssum" + sx])
        S.op("dve", lambda e: e.reciprocal(out=stat(RI, nh), in_=stat(SM, nh)), r=["ssum" + sx], w=["srin" + sx])
        rin = stat(RI, nh)
        rb = bass.AP(rin.tensor, rin.offset, [list(rin.ap[0]), [1, nh], [0, 256]])
        S.op("dve", lambda e: e.tensor_tensor(out=dstb, in0=dst32, in1=rb, op=ALU.mult), r=[dst32_res, "srin" + sx], w=[dstb_res])

    def softmax4(src_res, src_ap, mask_ap, mask_res, scale, sink_cols, dst32, dst32_res, dstb, dstb_res, so=44, sx="", nh=4):
        M, NM, SM, DF, RI = so + 8, so + 8 + nh, so + 8 + 2 * nh, so + 8 + 3 * nh, so + 8 + 4 * nh
        if mask_ap is not None:
            mk = bass.AP(mask_ap.tensor, mask_ap.offset, [list(mask_ap.ap[0]), [0, nh], [1, 256]])
            S.op("dve", lambda e: e.scalar_tensor_tensor(out=dst32, in0=src_ap, scalar=scale, in1=mk, op0=ALU.mult, op1=ALU.add),
                 r=list(src_res) + [mask_res], w=[dst32_res])
        else:
            S.op("act", lambda e: e.activation(out=dst32, in_=src_ap, func=AF.Identity, scale=scale), r=list(src_res), w=[dst32_res])
        S.op("dve", lambda e: e.tensor_reduce(out=stat(M, nh), in_=dst32, axis=AX.X, op=ALU.max), r=[dst32_res], w=["smx" + sx])
        if sink_cols is not None:
            S.op("dve", lambda e: e.tensor_tensor(out=stat(M, nh), in0=stat(M, nh), in1=snk[:, sink_cols:sink_cols + nh], op=ALU.max), r=["smx" + sx, "snk"], w=["smx" + sx])
        S.op("dve", lambda e: e.tensor_scalar(out=stat(NM, nh), in0=stat(M, nh), scalar1=-1.0, scalar2=None, op0=ALU.mult), r=["smx" + sx], w=["snm" + sx])

        def fexp(e):
            ins = None
            for h in range(nh):
                ins = e.activation(out=dst32[:, h, :], in_=dst32[:, h, :], func=AF.Exp, bias=stat(NM + h), scale=1.0, accum_out=stat(SM + h))
            return ins
        S.op("act", fexp, r=[dst32_res, "snm" + sx], w=[dst32_res, "ssum" + sx])
        if sink_cols is not None:
            S.op("dve", lambda e: e.tensor_tensor(out=stat(DF, nh), in0=snk[:, sink_cols:sink_cols + nh], in1=stat(NM, nh), op=ALU.add), r=["snk", "snm" + sx], w=["sdf" + sx])
            S.op("act", lambda e: e.activation(out=stat(DF, nh), in_=stat(DF, nh), func=AF.Exp), r=["sdf" + sx], w=["sdf" + sx])
            S.op("dve", lambda e: e.tensor_tensor(out=stat(SM, nh), in0=stat(SM, nh), in1=stat(DF, nh), op=ALU.add), r=["ssum" + sx, "sdf" + sx], w=["ssum" + sx])
        S.op("dve", lambda e: e.reciprocal(out=stat(RI, nh), in_=stat(SM, nh)), r=["ssum" + sx], w=["srin" + sx])
        rin = stat(RI, nh)
        rb = bass.AP(rin.tensor, rin.offset, [list(rin.ap[0]), [1, nh], [0, 256]])
        S.op("dve", lambda e: e.tensor_tensor(out=dstb, in0=dst32, in1=rb, op=ALU.mult), r=[dst32_res, "srin" + sx], w=[dstb_res])

    OB = 3

    def attn_q_transposes():
        transposes([qr[:, h * 128:(h + 1) * 128] for h in range(4)], ["qr"], flat(qT), ["qT"], BF16)

    def attn_scores8(cur, mask8, mask8_res):
        prev = 1 - cur

        def fs(e):
            ins = None
            for e_ in range(2):
                for h in range(4):
                    for kbi, sl in enumerate((prev, cur)):
                        c0 = 4 * 512 + (e_ * 4 + h) * 256 + kbi * 128
                        e.matmul(PS[:, c0:c0 + 128], lhsT=qT[e_ * 64:(e_ + 1) * 64, h, :], rhs=kTring[e_ * 64:(e_ + 1) * 64, sl, :],
                                 start=(h % 2 == 0 and kbi == 0), stop=False, skip_group_check=True)
            for hh in range(8):
                c0 = 4 * 512 + hh * 256
                ins = e.matmul(PS[:, c0:c0 + 256], lhsT=idb, rhs=mask8, start=False, stop=True, skip_group_check=True)
            return ins
        S.op("pe", fs, r=["qT", "kT0", "kT1", "idb", mask8_res], w=["b4", "b5", "b6", "b7"])

    def attn_softmax8():
        sc = PS[:, 4 * 512:8 * 512].rearrange("p (h k) -> p h k", h=8)
        softmax_direct(["b4", "b5", "b6", "b7"], sc, 0.125, 0, ssb8, "ssb", pb8, "pb", so=0, sx="8", nh=8)

    def attn_pv8(cur):
        prev = 1 - cur
        transposes([pb8[:, hh, kbi * 128:(kbi + 1) * 128] for hh in range(4) for kbi in range(2)], ["pb"], flat(pT8[:, 0:4]), ["pT"], BF16)
        transposes([pb8[:, hh, kbi * 128:(kbi + 1) * 128] for hh in range(4, 8) for kbi in range(2)], ["pb"], flat(pT8[:, 4:8]), ["pT2"], BF16)

        def fpv(e):
            ins = None
            for hh in range(8):
                e_ = hh // 4
                cc, par = hh // 2, hh % 2
                for kbi, sl in enumerate((prev, cur)):
                    ins = e.matmul(bank(OB)[par * 64:(par + 1) * 64, cc * 128:(cc + 1) * 128], lhsT=vring[:, sl, e_ * 64:(e_ + 1) * 64],
                                   rhs=pT8[:, hh, kbi, :], start=(kbi == 0), stop=(kbi == 1), skip_group_check=True)
            return ins
        S.op("pe", fpv, r=["pT", "pT2", "v0", "v1"], w=["b%d" % OB])

    def attn_evac():
        S.op("act", lambda e: e.copy(out=flat(brT[0]), in_=bank(OB)), r=["b%d" % OB], w=["brT0"])

    def mem_scores_softmax_only():
        sc = PS[:, 6 * 512:8 * 512].rearrange("p (h k) -> p h k", h=4)
        softmax_direct(["b6", "b7"], sc, float(128 ** -0.5), None, em, "ssb", pmb, "pb", so=44, sx="", nh=4)

    def mem_scores_softmax():
        mem_scores_softmax_only()
        transposes([pmb[:, h, mbi * 128:(mbi + 1) * 128] for h in range(4) for mbi in range(2)], ["pb"], flat(pmT), ["pT"], BF16)

    def mem_scores():
        transposes([mqb[:, h * 128:(h + 1) * 128] for h in range(4)], ["mqb"], flat(mqT), ["mqT"], BF16)

        def fs(e):
            ins = None
            for h in range(4):
                ins = e.matmul(PS[:, 6 * 512 + h * 256: 6 * 512 + (h + 1) * 256], lhsT=mqT[:, h, :], rhs=mkT[:, h, :], start=True, stop=True)
            return ins
        S.op("pe", fs, r=["mqT", "mkT"], w=["b6", "b7"])
        mem_scores_softmax_only()

    def mem_pv():
        transposes([pmb[:, h, mbi * 128:(mbi + 1) * 128] for h in range(4) for mbi in range(2)], ["pb"], flat(pmT), ["pT"], BF16)

        def fpv(e):
            ins = None
            for h in range(4):
                for mbi in range(2):
                    ins = e.matmul(bank(OB)[:, h * 128:(h + 1) * 128], lhsT=mvb[:, mbi, h * 128:(h + 1) * 128], rhs=pmT[:, h, mbi, :],
                                   start=(mbi == 0), stop=(mbi == 1), skip_group_check=True)
            return ins
        S.op("pe", fpv, r=["pT", "mvb"], w=["b%d" % OB])
        S.op("act", lambda e: e.copy(out=flat(brT[2]), in_=bank(OB)), r=["b%d" % OB], w=["brT2"])

    def sg_mix(wt, wt_res, bias, bias_res):
        b = nb()

        def f(e):
            ins = None
            for g in range(4):
                ins = e.matmul(bank(b)[:, g * 128:(g + 1) * 128], lhsT=wt[:, g, :], rhs=vnb[:, g * 128:(g + 1) * 128], start=True, stop=True)
            return ins
        S.op("pe", f, r=["vnb", wt_res], w=["b%d" % b])

        def fo(e):
            ins = None
            for g in range(4):
                ins = e.scalar_tensor_tensor(out=sgo[:, g * 128:(g + 1) * 128], in0=bank(b)[:, g * 128:(g + 1) * 128], scalar=bias[:, g:g + 1],
                                             in1=usb[:, g * 128:(g + 1) * 128], op0=ALU.add, op1=ALU.mult)
            return ins
        S.op("dve", fo, r=["b%d" % b, "usb", bias_res], w=["sgo"])
        transposes([sgo[:, c * 128:(c + 1) * 128] for c in range(4)], ["sgo"], flat(brT[1]), ["brT1"], BF16)

    def proj_branch(br):
        for half in range(2):
            b = nb()

            def f(e, half=half, b=b):
                ins = None
                for cc in range(4):
                    ins = e.matmul(bank(b), lhsT=brT[br][:, cc, :], rhs=WO[:, br * 4 + cc, half * 512:(half + 1) * 512], start=(cc == 0), stop=(cc == 3))
                return ins
            S.op("pe", f, r=["brT%d" % br, "wo%d" % br], w=["b%d" % b])
            gi = br * 2 + half
            ah = acc[:, half * 512:(half + 1) * 512]
            th = tmp2[:, half * 512:(half + 1) * 512]
            if br == 0:
                S.op("dve", lambda e, gi=gi, b=b, ah=ah: e.scalar_tensor_tensor(out=ah, in0=tg[:, gi * 512:(gi + 1) * 512], scalar=1.0, in1=bank(b), op0=ALU.add, op1=ALU.mult),
                     r=["tg%d" % gi, "b%d" % b], w=["acc%d" % half])
            else:
                S.op("dve", lambda e, gi=gi, b=b, th=th: e.scalar_tensor_tensor(out=th, in0=tg[:, gi * 512:(gi + 1) * 512], scalar=1.0, in1=bank(b), op0=ALU.add, op1=ALU.mult),
                     r=["tg%d" % gi, "b%d" % b], w=["tmp%d" % half])
                S.op("dve", lambda e, ah=ah, th=th: e.tensor_tensor(out=ah, in0=ah, in1=th, op=ALU.add), r=["acc%d" % half, "tmp%d" % half], w=["acc%d" % half])

    def back2(slot, x1row, store_eng="sp"):
        xin = XIN[slot]
        xr = "xin%d" % slot
        S.op("act", lambda e: e.activation(out=junk, in_=acc, func=AF.Square, scale=1.0 / 32.0, accum_out=stat(6)), r=["acc0", "acc1"], w=["junk", "pms"])
        rstd_from(stat(6), stat(7), 4.0 * EPS, "p")
        S.op("dve", lambda e: e.scalar_tensor_tensor(out=acc, in0=acc, scalar=stat(7), in1=gpost, op0=ALU.mult, op1=ALU.mult), r=["acc0", "acc1", "prs", "gpost"], w=["acc0", "acc1"])
        S.op("dve", lambda e: e.tensor_tensor(out=xin, in0=xin, in1=acc, op=ALU.add), r=[xr, "acc0", "acc1"], w=[xr])
        dma(store_eng, x1s[x1row * 128:(x1row + 1) * 128, :], xin, r=[xr], w=["x1s%d" % x1row],
            stream="d_x1o%d%s" % (slot, "" if store_eng == "pool" else "h"))

    rot["n"] = 3
    rot["i"] = 0
    head(xp[0:128, :], 0, 0)
    proj_kv(0)
    kv_publish(0)
    head(xp[128:256, :], 1, 1)
    proj_kv(1)
    proj_q(1)
    for bi in range(1, NPB):
        slot = bi % 2
        kv_publish(slot)
        if bi == NPB - 1:
            out_toks.append(dma("sp", wkp, kr, r=["kr"], w=["o_wkp"], stream="d_o4"))
            out_toks.append(dma("sp", wvp, vf, r=["vf"], w=["o_wvp"], stream="d_o5"))
        mk_ap, mk_res = (maskf8, "maskf8") if bi == 2 else (maskp8, "maskp8")
        attn_q_transposes()
        attn_scores8(slot, mk_ap, mk_res)
        attn_softmax8()
        proj_sgu()
        proj_sgv()
        proj_mq()
        proj_gates(0, 4)
        attn_pv8(slot)
        sg_mix(WT, "WT", sgbT, "sgbT")
        attn_evac()
        mem_scores()
        proj_gates(4, 6)
        proj_branch(0)
        proj_branch(1)
        mem_pv()
        proj_branch(2)
        nslot = 1 - slot
        if bi + 1 < NPB:
            head(xp[(bi + 1) * 128:(bi + 2) * 128, :], nslot, bi + 1)
        else:
            head(xs, nslot, NPB)
        proj_kv(nslot)
        proj_q(nslot)
        back2(slot, bi - 1)
        if KSTOP == 3 and bi == 2:
            return finish()
    if KSTOP == 4:
        return finish()

    proj_sgu()
    proj_sgv()
    proj_mq()
    proj_gates(0, 6)
    out_toks.append(dma("sp", sgv, gvs, r=["gvs"], w=["o_sgv"], stream="d_o6"))
    out_toks.append(dma("sp", wks[:, 0:120, :], cwk[:, 8:128, :], w=["o_wks_a"], stream="d_o7"))
    out_toks.append(dma("sp", wvs[:, 0:120, :], cwv[:, 8:128, :], w=["o_wvs_a"], stream="d_o8"))
    for q in range(SEQS):
        out_toks.append(dma("sp", wks[q, 120:128, :], kr[q * 8:(q + 1) * 8, :], r=["kr"], w=["o_wks_b%d" % q], stream="d_o9"))
        out_toks.append(dma("sp", wvs[q, 120:128, :], vf[q * 8:(q + 1) * 8, :], r=["vf"], w=["o_wvs_b%d" % q], stream="d_o10"))
    if KSTOP == 5:
        return finish()
    S.barrier()
    SB = Arena.__new__(Arena)
    SB.t = A.t
    SB.off = WA_OFF
    SB.cap = WA_OFF + 8 * 5632 * 2
    kqT = SB.carve([SEQS, 4, 256], BF16)
    Zq = [SB.carve([SEQS, 128], BF16) for _ in range(2)]
    Zs = [SB.carve([SEQS, 128], BF16) for _ in range(2)]
    cwkb = SB.carve([SEQS, 128], BF16)
    cwkT = SB.carve([SEQS, 128], BF16)
    cwvb = SB.carve([SEQS, 128], BF16)
    NST = 4
    kst = [SB.carve([2, 512], BF16) for _ in range(NST)]
    vst = [SB.carve([2, 512], BF16) for _ in range(NST)]

    for z in range(2):
        S.op("dve", lambda e, z=z: e.memset(flat(Zq[z]), 0.0), w=["Zq%d" % z])
        S.op("dve", lambda e, z=z: e.memset(flat(Zs[z]), 0.0), w=["Zs%d" % z])
    def k_load(q):
        dma("pool", kst[q % NST], cmk[q].rearrange("(mb m) c -> m mb c", mb=2), w=["kst%d" % (q % NST)], stream="d_sk%d" % (q % NST))

    def v_load(q):
        dma("pool", vst[q % NST], cmv[q].rearrange("(mb m) c -> m mb c", mb=2), w=["vst%d" % (q % NST)], stream="d_sv%d" % (q % NST))

    for q in range(NST):
        k_load(q)
    dma("pool", cwkb, cwk.rearrange("q j c -> j q c"), w=["cwkb"], stream="d_s0")
    dma("pool", cwvb, cwv.rearrange("q j c -> j q c"), w=["cwvb"], stream="d_s1")
    for half in range(2):
        transposes([cwkb[:, half * 8 + i, :] for i in range(8)], ["cwkb"], cwkT[:, half * 8:(half + 1) * 8, :], ["cwkT%d" % half], BF16)
    for q in range(SEQS):
        st = kst[q % NST]
        transposes([st[:, mbi, h * 128:(h + 1) * 128] for h in range(4) for mbi in range(2)], ["kst%d" % (q % NST)],
                   kqT[:, q], ["kqT%d" % q], BF16)
        if q + NST < SEQS:
            k_load(q + NST)
    for q in range(NST):
        v_load(q)

    kv_publish(1)
    transposes([qr[:, h * 128:(h + 1) * 128] for h in range(4)], ["qr"], flat(qT), ["qT"], BF16)
    transposes([mqb[:, h * 128:(h + 1) * 128] for h in range(4)], ["mqb"], flat(mqT), ["mqT"], BF16)

    def diag_fill(dst, src):
        d = bass.AP(dst.tensor, dst.offset, [list(dst.ap[0]), [136, 16], [1, 8]])
        s = src.rearrange("p (q t) -> p q t", q=16)
        return d, s

    ob = OB
    for grp in range(2):
        e_ = grp
        for h in range(4):
            z = Zs[h % 2]
            zr = "Zs%d" % (h % 2)
            d_, s_ = diag_fill(z, qT[:, h, :])
            S.op("dve", lambda e, d_=d_, s_=s_: e.tensor_copy(out=d_, in_=s_), r=["qT"], w=[zr])

            def fs(e, h=h, e_=e_, z=z):
                ins = None
                base = 6 * 512 + h * 256
                for q in range(SEQS):
                    ins = e.matmul(PS[:, base:base + 128], lhsT=z[e_ * 64:(e_ + 1) * 64, q, :], rhs=cwkT[e_ * 64:(e_ + 1) * 64, q, :],
                                   start=(q == 0 and h % 2 == 0), stop=(q == SEQS - 1), skip_group_check=True)
                ins = e.matmul(PS[:, base + 128:base + 256], lhsT=qT[e_ * 64:(e_ + 1) * 64, h, :], rhs=kTring[e_ * 64:(e_ + 1) * 64, 1, :],
                               start=False, stop=True, skip_group_check=True)
                return ins
            S.op("pe", fs, r=[zr, "cwkT0", "cwkT1", "qT", "kT1"], w=["b6", "b7"])
        sc = PS[:, 6 * 512:8 * 512].rearrange("p (h k) -> p h k", h=4)
        softmax4(["b6", "b7"], sc, masks, "masks", 0.125, grp * 4, ssb, "ssb", pb, "pb")
        transposes([pb[:, h, kbi * 128:(kbi + 1) * 128] for h in range(4) for kbi in range(2)], ["pb"], flat(pT), ["pT"], BF16)

        def fpv(e, e_=e_):
            ins = None
            for h in range(4):
                hh = e_ * 4 + h
                cc, par = hh // 2, hh % 2
                o = bank(ob)[par * 64:(par + 1) * 64, cc * 128:(cc + 1) * 128]
                ins = e.matmul(o, lhsT=vring[:, 1, e_ * 64:(e_ + 1) * 64], rhs=pT[:, h, 1, :], start=True, stop=False, skip_group_check=True)
                for q in range(SEQS):
                    ins = e.matmul(o[:, q * 8:(q + 1) * 8], lhsT=cwvb[:, q, e_ * 64:(e_ + 1) * 64], rhs=pT[:, h, 0, q * 8:(q + 1) * 8],
                                   start=False, stop=(q == SEQS - 1), skip_group_check=True)
            return ins
        S.op("pe", fpv, r=["pT", "v1", "cwvb"], w=["b%d" % ob])
    S.op("act", lambda e: e.copy(out=flat(brT[0]), in_=bank(ob)), r=["b%d" % ob], w=["brT0"])

    sg_mix(WTS, "WTS", sgbS, "sgbS")

    for h in range(4):
        z = Zq[h % 2]
        zr = "Zq%d" % (h % 2)
        d_, s_ = diag_fill(z, mqT[:, h, :])
        S.op("dve", lambda e, d_=d_, s_=s_: e.tensor_copy(out=d_, in_=s_), r=["mqT"], w=[zr])

        def fs(e, h=h, z=z):
            ins = None
            base = 6 * 512 + h * 256
            for q in range(SEQS):
                ins = e.matmul(PS[:, base:base + 256], lhsT=z[:, q, :], rhs=kqT[:, q, h, :], start=(q == 0 and h % 2 == 0), stop=(q == SEQS - 1), skip_group_check=True)
            return ins
        S.op("pe", fs, r=[zr] + ["kqT%d" % q for q in range(SEQS)], w=["b6", "b7"])
    mem_scores_softmax()
    ob2 = OB
    for q in range(SEQS):
        st = vst[q % NST]

        def fpv(e, q=q, st=st):
            ins = None
            for h in range(4):
                for mbi in range(2):
                    ins = e.matmul(bank(ob2)[:, h * 128 + q * 8: h * 128 + (q + 1) * 8], lhsT=st[:, mbi, h * 128:(h + 1) * 128], rhs=pmT[:, h, mbi, q * 8:(q + 1) * 8],
                                   start=(q == 0 and h == 0 and mbi == 0), stop=(mbi == 1), skip_group_check=True)
            return ins
        S.op("pe", fpv, r=["pT", "vst%d" % (q % NST)], w=["b%d" % ob2])
        if q + NST < SEQS:
            v_load(q + NST)
    S.op("act", lambda e: e.copy(out=flat(brT[2]), in_=bank(ob2)), r=["b%d" % ob2], w=["brT2"])
    fence = [(st_, v_) for st_, v_ in S.cnt.items() if v_ > 0 and (st_ in ("pe", "act", "dve", "pool") or st_.startswith("d_s"))]
    w_up_r = w_up.rearrange("(k p) e -> p k e", p=128)
    for c in (0, 5, 6, 1, 7, 2, 8, 3, 9, 4, 10):
        dma("pool", WUP[:, :, c * 512:(c + 1) * 512], w_up_r[:, :, c * 512:(c + 1) * 512], w=["wup%d" % c], stream="d_WA%d" % c, after=fence)
    proj_branch(0)
    proj_branch(1)
    proj_branch(2)
    back2(0, NOWN + 1, store_eng="sp")
    if KSTOP == 6:
        return finish()

    S.barrier(keep_streams=("d_WA",), keep_res=("wup",))

    rot["n"] = 4
    rot["i"] = 0
    A.off = P_MARK
    gpf = A.carve([D], F32)
    gqf = A.carve([D], F32)
    P2_MARK = A.off
    T = {}

    def carve_p2(ntok):
        A.off = P2_MARK
        nblk = ntok // 128
        T["X1"] = [A.carve([nblk, D], F32) for _ in range(2)]
        T["h2"] = [A.carve([D], BF16) for _ in range(2)]
        T["junk2"] = None
        T["st2"] = A.carve([16], F32)
        T["h2T"] = [A.carve([8, ntok + 2], BF16) for _ in range(2 if ntok > 128 else 1)]
        T["actT"] = A.carve([NFC, ntok], BF16)
        T["EXT"] = [[A.carve([ntok + 4 if ntok > 128 else 160], F32) for _ in range(2)] for _ in range(2)]
        T["CG"] = [A.carve([ntok], F32) for _ in range(2)]
        T["CV"] = [A.carve([ntok], F32) for _ in range(2)]
        T["GG"] = [A.carve([ntok], F32) for _ in range(2)]
        T["fsb"] = A.carve([ntok // 128, D], F32)
        T["ysb"] = None
        if ntok > 128:
            T["osb"] = T["fsb"][:, 0, 0:128]
            T["osb_res"] = "fsb00"
        else:
            T["osb"] = A.carve([128], F32)
            T["osb_res"] = "osb"

    carve_p2(128)
    XH = A.carve([D], F32)
    scvr4 = [A.carve([1408], F32) for _ in range(2)]
    osb11 = scvr4[0].rearrange("p (a c) -> p a c", a=11)
    scvT = A.carve([44, 32], F32)
    ncs = A.carve([1408], F32)

    dma("sp", gpf, g_pffn.partition_broadcast(128), w=["gpf"], stream="d_c3")
    dma("sp", gqf, g_qffn.partition_broadcast(128), w=["gqf"], stream="d_c4")
    w_dn_r = w_down.rearrange("(k p) e -> p k e", p=128)
    for c4, (f0, f1) in enumerate(((0, 4), (4, 10), (10, 16), (16, 22))):
        dma("pool", WDN[:, f0:f1, :], w_dn_r[:, f0:f1, :], w=["wdn%d" % c4], stream="d_WB%d" % c4)

    for cg_ in range(4):
        scvr = scvr4[cg_ % 2]
        dma("sp", scvr[0:32, :], scv[:, cg_ * 1408:(cg_ + 1) * 1408], w=["scvr%d" % (cg_ % 2)], stream="d_c%d" % (7 + cg_ % 2))
        b = nb()

        def f(e, b=b, scvr=scvr):
            ins = None
            for i in range(11):
                ins = e.transpose(out=bank(b)[:, i * 32:(i + 1) * 32], in_=scvr[0:32, i * 128:(i + 1) * 128], identity=idf[0:32, 0:32])
            return ins
        S.op("pe", f, r=["scvr%d" % (cg_ % 2), "idf"], w=["b%d" % b])
        S.op("act", lambda e, b=b, cg_=cg_: e.copy(out=scvT[:, cg_ * 11:(cg_ + 1) * 11, :], in_=bank(b)[:, 0:352].rearrange("p (a b) -> p a b", a=11)), r=["b%d" % b], w=["scvT"])

    def ffn_front_a(nrows, xt_ap, xres, j):
        h2, st2 = T["h2"][j], T["st2"]
        c = 4 * j
        S.op("act", lambda e: e.activation(out=h2[0:nrows], in_=xt_ap[0:nrows], func=AF.Square, scale=1.0 / 32.0, accum_out=st2[0:nrows, c:c + 1]), r=[xres], w=["h2_%d" % j, "fms%d" % j])
        S.op("pool", lambda e: e.tensor_scalar(out=st2[0:nrows, c + 1:c + 2], in0=st2[0:nrows, c:c + 1], scalar1=EPS, scalar2=None, op0=ALU.add), r=["fms%d" % j], w=["frs%d" % j])
        S.op("pool", lambda e: e.tensor_tensor(out=st2[0:nrows, c + 1:c + 2], in0=st2[0:nrows, c + 1:c + 2], in1=negh[0:nrows], op=ALU.pow), r=["frs%d" % j, "negh"], w=["frs%d" % j])
        S.op("dve", lambda e: e.scalar_tensor_tensor(out=h2[0:nrows], in0=xt_ap[0:nrows], scalar=st2[0:nrows, c + 1:c + 2], in1=gpf[0:nrows], op0=ALU.mult, op1=ALU.mult),
             r=[xres, "frs%d" % j, "gpf"], w=["h2_%d" % j])

    def ffn_front_b(nrows, col0, j, hsel=0):
        h2, h2T = T["h2"][j], T["h2T"][hsel]
        hres = "h2T%d" % hsel
        b = nb()
        pv = bankb(b)

        def f(e):
            ins = None
            for k in range(8):
                ins = e.transpose(out=pv[:, k * 128:k * 128 + nrows], in_=h2[0:nrows, k * 128:(k + 1) * 128], identity=idb[0:nrows, 0:nrows])
            return ins
        S.op("pe", f, r=["h2_%d" % j, "idb"], w=["b%d" % b])
        src = pv[:, 0:1024].rearrange("p (k t) -> p k t", k=8)[:, :, 0:nrows]
        S.op("act", lambda e: e.copy(out=h2T[:, :, col0:col0 + nrows], in_=src), r=["b%d" % b], w=[hres])

    def nb2():
        if rot["i"] % 2:
            nb()
        b = nb()
        nb()
        return b

    def ffn_tile(N, sample, x1_tiles, yout, hooks=None, hsel=0):
        actT, EXT, CG, CV, GG, fsb, ysb, st2, junk2 = (T[k] for k in ("actT", "EXT", "CG", "CV", "GG", "fsb", "ysb", "st2", "junk2"))
        h2T = T["h2T"][hsel]
        hres = "h2T%d" % hsel
        n_act = 128 if sample else N

        NTB = n_act // 128

        def down_slice(fc):
            def f(e):
                ins = None
                for j in range(NTB):
                    for half in range(2):
                        o = PS[:, (4 + 2 * j + half) * 512:(5 + 2 * j + half) * 512]
                        ins = e.matmul(o, lhsT=actT[:, fc, j * 128:(j + 1) * 128], rhs=WDN[:, fc, half * 512:(half + 1) * 512],
                                       start=(fc == 0), stop=(fc == NFC - 1))
                return ins
            S.op("pe", f, r=["actT%d" % fc, "wdn%d" % (0 if fc < 4 else 1 if fc < 10 else 2 if fc < 16 else 3)], w=["b%d" % (4 + i) for i in range(2 * NTB)])

        def tail_ops(fc):
            bi_ = fc % 2
            gg = GG[bi_]
            S.op("act", lambda e: e.activation(out=gg[:, 0:n_act], in_=CG[bi_][:, 0:n_act], func=AF.Gelu_apprx_tanh), r=["cg%d" % bi_], w=["gg%d" % bi_])
            S.op("pool", lambda e: e.tensor_tensor(out=actT[:, fc, 0:n_act], in0=gg[:, 0:n_act], in1=CV[bi_][:, 0:n_act], op=ALU.mult),
                 r=["gg%d" % bi_, "cv%d" % bi_], w=["actT%d" % fc])

        for fc in range(NFC):
            bi_ = fc % 2
            stage = []
            for gv in range(2):
                fcc = fc + gv * NFC
                b = nb()

                def f(e, fcc=fcc, b=b):
                    ins = None
                    for k in range(8):
                        ins = e.matmul(bank(b)[:, 0:N], lhsT=WUP[:, k, fcc * 128:(fcc + 1) * 128], rhs=h2T[:, k, 0:N], start=(k == 0), stop=(k == 7))
                    return ins
                S.op("pe", f, r=[hres, "wup%d" % (fcc // 4)], w=["b%d" % b])
                ext = EXT[bi_][gv]
                er = "ext%d%d" % (bi_, gv)
                erc = er + "c"
                cdst = (CG if gv == 0 else CV)[bi_]
                cres = ("cg%d" if gv == 0 else "cv%d") % bi_
                w0 = cwt[:, 0, fcc:fcc + 1]
                w1 = cwt[:, 1, fcc:fcc + 1]
                w2 = cwt[:, 2, fcc:fcc + 1]
                bb = cwt[:, 3, fcc:fcc + 1]
                if not sample:
                    S.op("pool", lambda e, ext=ext, fcc=fcc: e.tensor_copy(out=ext[:, 0:2], in_=carry[:, :, fcc]), r=["carry%d" % fcc], w=[erc])
                    S.op("act", lambda e, ext=ext, b=b: e.copy(out=ext[:, 2:2 + N], in_=bank(b)[:, 0:N]), r=["b%d" % b], w=[er])
                    S.op("pool", lambda e, ext=ext, fcc=fcc: e.tensor_copy(out=carry[:, :, fcc], in_=ext[:, N:N + 2]), r=[er], w=["carry%d" % fcc])
                    v2, v1, v0 = ext[:, 2:N + 2], ext[:, 1:N + 1], ext[:, 0:N]
                    cd = cdst[:, 0:N]
                else:
                    S.op("dve", lambda e, fcc=fcc, b=b: e.tensor_scalar(out=carry[:, :, fcc], in0=bank(b)[:, 0:2], scalar1=hvt, scalar2=None, op0=ALU.mult),
                         r=["b%d" % b, "hvt"], w=["carry%d" % fcc])
                    e3 = ext[:, 0:160].rearrange("p (q t) -> p q t", q=16)
                    S.op("pool", lambda e, e3=e3, fcc=fcc: e.tensor_copy(out=e3[:, :, 0:2], in_=scvT[:, fcc, :].rearrange("p (q i) -> p q i", q=16)), r=["scvT"], w=[erc])
                    S.op("act", lambda e, e3=e3, b=b: e.copy(out=e3[:, :, 2:10], in_=bank(b)[:, 2:130].rearrange("p (q t) -> p q t", q=16)), r=["b%d" % b], w=[er])
                    nview = bass.AP(ncs.tensor, ncs.offset + fcc, [list(ncs.ap[0]), [88, 16], [44, 2]])
                    S.op("pool", lambda e, e3=e3, nview=nview: e.tensor_copy(out=nview, in_=e3[:, :, 8:10]), r=[er], w=["ncs"])
                    v2, v1, v0 = e3[:, :, 2:10], e3[:, :, 1:9], e3[:, :, 0:8]
                    cd = cdst[:, 0:128].rearrange("p (q t) -> p q t", q=16)
                stage.append((cd, v2, v1, v0, w2, w1, w0, bb, er, erc, cres))
            for (cd, v2, v1, v0, w2, w1, w0, bb, er, erc, cres) in stage:
                S.op("act", lambda e, cd=cd, v2=v2, w2=w2, bb=bb: e.activation(out=cd, in_=v2, func=AF.Identity, scale=w2, bias=bb), r=[er, "cwt"], w=[cres])
            for (cd, v2, v1, v0, w2, w1, w0, bb, er, erc, cres) in stage:
                S.op("dve", lambda e, cd=cd, v1=v1, w1=w1: e.scalar_tensor_tensor(out=cd, in0=v1, scalar=w1, in1=cd, op0=ALU.mult, op1=ALU.add), r=[er, erc, cres, "cwt"], w=[cres])
                S.op("dve", lambda e, cd=cd, v0=v0, w0=w0: e.scalar_tensor_tensor(out=cd, in0=v0, scalar=w0, in1=cd, op0=ALU.mult, op1=ALU.add), r=[er, erc, cres, "cwt"], w=[cres])
            if fc > 0:
                tail_ops(fc - 1)
            if fc >= 2:
                down_slice(fc - 2)
            for hk in (hooks or {}).get(fc, ()):
                hk()
        tail_ops(NFC - 1)
        down_slice(NFC - 2)
        down_slice(NFC - 1)
        for j in range(NTB):
            for half in range(2):
                bk = 4 + 2 * j + half
                eng = "act" if half == 0 else "dve"
                if eng == "act":
                    S.op("act", lambda e, j=j, half=half, bk=bk: e.copy(out=fsb[:, j, half * 512:(half + 1) * 512], in_=bank(bk)), r=["b%d" % bk], w=["fsb%d%d" % (j, half)])
                else:
                    S.op("dve", lambda e, j=j, half=half, bk=bk: e.tensor_copy(out=fsb[:, j, half * 512:(half + 1) * 512], in_=bank(bk)), r=["b%d" % bk], w=["fsb%d%d" % (j, half)])

        def make_tail(j):
            def run():
                fres = ["fsb%d0" % j, "fsb%d1" % j]
                ftok = fsb[:, j, :]
                junk_ = T["h2"][j]
                c = 8 + 2 * j
                S.op("act", lambda e: e.activation(out=junk_, in_=ftok, func=AF.Square, scale=1.0 / 32.0, accum_out=st2[:, c:c + 1]), r=fres, w=["h2_%d" % j, "gms%d" % j])
                S.op("pool", lambda e: e.tensor_scalar(out=st2[:, c + 1:c + 2], in0=st2[:, c:c + 1], scalar1=EPS, scalar2=None, op0=ALU.add), r=["gms%d" % j], w=["grs%d" % j])
                S.op("pool", lambda e: e.tensor_tensor(out=st2[:, c + 1:c + 2], in0=st2[:, c + 1:c + 2], in1=negh, op=ALU.pow), r=["grs%d" % j, "negh"], w=["grs%d" % j])
                x1t, x1r = x1_tiles[j]
                S.op("dve", lambda e: e.scalar_tensor_tensor(out=ftok, in0=ftok, scalar=st2[:, c + 1:c + 2], in1=gqf, op0=ALU.mult, op1=ALU.mult), r=fres + ["grs%d" % j, "gqf"], w=fres)
                S.op("dve", lambda e: e.tensor_tensor(out=ftok, in0=ftok, in1=x1t, op=ALU.add), r=fres + [x1r], w=fres)
                dma("sp", yout[j], ftok, r=fres, w=["o_y"], stream="d_yoh%d" % (j % 2))
            return run
        return [make_tail(j) for j in range(n_act // 128)]

    X1a = T["X1"][0][:, 0, :]
    dma("sp", XH[0:2, :], x1s[126:128, :], w=["XH"], stream="d_x2h")
    dma("sp", X1a, x1s[(NOWN + 1) * 128:(NOWN + 2) * 128, :], w=["X10a"], stream="d_x2a")
    ffn_front_a(2, XH, "XH", 0)
    ffn_front_b(2, 0, 0)
    ffn_front_a(128, X1a, "X10a", 1)
    ffn_front_b(128, 2, 1)
    for tl in ffn_tile(130, True, [(X1a, "X10a")], [ys]):
        tl()
    for g0 in range(0, 11, 4):
        n_ = min(4, 11 - g0)
        b = nb()

        def f(e, b=b, g0=g0, n_=n_):
            ins = None
            for i in range(n_):
                ins = e.transpose(out=bank(b)[:, i * 128:(i + 1) * 128], in_=ncs[:, (g0 + i) * 128:(g0 + i + 1) * 128], identity=idf)
            return ins
        S.op("pe", f, r=["ncs", "idf"], w=["b%d" % b])
        S.op("act", lambda e, b=b, g0=g0, n_=n_: e.copy(out=osb11[:, g0:g0 + n_, :], in_=bank(b)[:, 0:n_ * 128].rearrange("p (a c) -> p a c", a=n_)), r=["b%d" % b], w=["scvr0"])
    dma("sp", cvs.rearrange("(c p) f -> p c f", p=128), osb11, r=["scvr0"], w=["o_cvs"], stream="d_o11")
    if KSTOP == 7:
        return finish()
    S.barrier()
    carve_p2(256)
    osb = T["osb"]
    def p2_load(ti):
        xt = T["X1"][ti % 2]
        for j in range(2):
            row = 1 + ti * 2 + j
            dma("sp", xt[:, j, :], x1s[row * 128:(row + 1) * 128, :], w=["X1%d%s" % (ti % 2, "ab"[j])], stream="d_x2%d%s" % (ti % 2, "ab"[j]))

    def p2_fa(ti, j):
        return lambda: ffn_front_a(128, T["X1"][ti % 2][:, j, :], "X1%d%s" % (ti % 2, "ab"[j]), j)

    def p2_fb(ti, j):
        return lambda: ffn_front_b(128, j * 128, j, hsel=ti % 2)

    NT = NOWN // 2
    p2_load(0)
    for j in range(2):
        p2_fa(0, j)()
        p2_fb(0, j)()
    pending = []
    for ti in range(NT):
        xt = T["X1"][ti % 2]
        res = ["X1%d%s" % (ti % 2, "ab"[j]) for j in range(2)]
        hooks = {}
        if pending:
            hooks[1] = [pending[0]]
            hooks[3] = [pending[1]]
        if ti + 1 < NT:
            hooks[5] = [lambda ti=ti: p2_load(ti + 1)]
            hooks[7] = [p2_fa(ti + 1, 0)]
            hooks[9] = [p2_fa(ti + 1, 1)]
            hooks[13] = [p2_fb(ti + 1, 0)]
            hooks[15] = [p2_fb(ti + 1, 1)]
        pending = ffn_tile(256, False, [(xt[:, j, :], res[j]) for j in range(2)], [yp[(ti * 2 + j) * 128:(ti * 2 + j + 1) * 128, :] for j in range(2)],
                           hooks=hooks, hsel=ti % 2)
    for tl in pending:
        tl()
    b = nb()
    S.op("pe", lambda e, b=b: e.transpose(out=bank(b)[0:88, 0:128], in_=flat(carry), identity=idf), r=["carry%d" % f_ for f_ in range(44)] + ["idf"], w=["b%d" % b])
    S.op("act", lambda e, b=b, osb=osb: e.copy(out=osb[0:88, :], in_=bank(b)[0:88, 0:128]), r=["b%d" % b], w=[T["osb_res"]])
    dma("sp", cvp, osb[0:88, :], r=[T["osb_res"]], w=["o_cvp"], stream="d_o12")

    return finish()


_CACHE = {}


def _rope_tables():
    half = 32
    inv = (10000.0 ** (-np.arange(half, dtype=np.float32) / half)).astype(np.float32)
    return inv


def _host_consts(core):
    inv = _rope_tables()
    pos = np.zeros((NPB + 1, 128), np.float32)
    for b in range(NPB):
        pos[b] = core * TOK - 256 + b * 128 + np.arange(128)
    pos[NPB] = PAST + (np.arange(128) % 8)
    ang = (pos[:, :, None].astype(np.float32) * inv[None, None, :]).astype(np.float32)
    c = np.cos(ang).astype(np.float32)
    s = np.sin(ang).astype(np.float32)
    ropec = np.concatenate([c, c], axis=-1).astype(np.float32)
    ropes = np.concatenate([-s, s], axis=-1).astype(np.float32)
    i = np.arange(128)[:, None]
    j = np.arange(128)[None, :]
    maskp = np.concatenate([np.where(j > i, 0.0, NEG), np.where(j <= i, 0.0, NEG)], axis=1).astype(np.float32)
    maskf = maskp.copy()
    if core == 0:
        maskf[:, 0:128] = NEG
    NEG8 = 8.0 * NEG
    maskp8 = np.concatenate([np.where(j > i, 0.0, NEG8), np.where(j <= i, 0.0, NEG8)], axis=1).astype(np.float32)
    maskf8 = maskp8.copy()
    if core == 0:
        maskf8[:, 0:128] = NEG8
    t = (np.arange(128) % 8)[:, None]
    q = (np.arange(128) // 8)[:, None]
    tt = (np.arange(128) % 8)[None, :]
    qq = (np.arange(128) // 8)[None, :]
    masks = np.concatenate([np.where(j >= t + 1, 0.0, NEG), np.where((qq == q) & (tt <= t), 0.0, NEG)], axis=1).astype(np.float32)
    tril = (j <= i).astype(np.float32)
    bdm = ((qq == q) & (tt <= t)).astype(np.float32)
    e8 = np.tile(np.eye(8, dtype=np.float32), (1, 16))
    hv = np.full((128, 1), 0.0 if core == 0 else 1.0, np.float32)
    return dict(ropec=ropec, ropes=ropes, maskp=maskp, maskf=maskf, masks=masks, maskp8=maskp8, maskf8=maskf8, tril=tril, bdm=bdm, e8=e8, hv=hv,
                ident=np.eye(128, dtype=np.float32))


def kernel(x_prompt, x_sample, cache_win_k, cache_win_v, cache_mem_k, cache_mem_v, state_conv, mem_prompt,
           pre_mix_g, w_in, attn_sinks, sg_ln_g, sg_ln_b, sg_w, sg_b, mem_norm_g, w_mem_kv, w_o,
           post_mix_g, pre_ffn_g, w_up, conv_w, conv_b, w_down, post_ffn_g):
    f = lambda a: np.ascontiguousarray(np.asarray(a, dtype=np.float32))
    if "nc" not in _CACHE:
        _CACHE["nc"] = build_program()
    nc = _CACHE["nc"]
    xpr = f(x_prompt)[0]
    xpad = np.concatenate([np.zeros((256, D), np.float32), xpr], axis=0)
    shared = dict(
        memp=f(mem_prompt)[0], w_in=f(w_in)[0], w_mkv=f(w_mem_kv)[0], w_o=f(w_o)[0].reshape(1536, D), w_up=f(w_up)[0], w_down=f(w_down)[0],
        g_pre=f(pre_mix_g), g_post=f(post_mix_g), g_pffn=f(pre_ffn_g), g_qffn=f(post_ffn_g), g_mem=f(mem_norm_g),
        ln_g=f(sg_ln_g), ln_b=f(sg_ln_b), sinks=f(attn_sinks), sg_w=f(sg_w)[0], sg_b=f(sg_b)[0], conv_w=f(conv_w)[0], conv_b=f(conv_b),
    )
    in_maps = []
    for c in range(NCORES):
        m = dict(shared)
        m.update(_host_consts(c))
        m["xp"] = np.ascontiguousarray(xpad[c * TOK:c * TOK + NPB * 128])
        sl = slice(c * SEQS, (c + 1) * SEQS)
        m["xs"] = f(x_sample)[sl].reshape(128, D)
        m["cwk"] = f(cache_win_k)[0, sl].reshape(SEQS, 128, 128)
        m["cwv"] = f(cache_win_v)[0, sl].reshape(SEQS, 128, 128)
        m["cmk"] = f(cache_mem_k)[0, sl].reshape(SEQS, 256, 512)
        m["cmv"] = f(cache_mem_v)[0, sl].reshape(SEQS, 256, 512)
        m["scv"] = f(state_conv)[0, sl].reshape(32, 2 * DFF)
        in_maps.append(m)
    res = run_bass_kernel_spmd(nc, in_maps, core_ids=list(range(NCORES))).results
    y_p = np.concatenate([r["yp"] for r in res], axis=0)[None]
    y_s = np.concatenate([r["ys"].reshape(SEQS, 8, D) for r in res], axis=0)
    last = res[NCORES - 1]
    wk_p = last["wkp"].reshape(1, 1, 128, 2, 64)
    wv_p = last["wvp"].reshape(1, 1, 128, 2, 64)
    mk_p = res[0]["mkp"].reshape(1, 1, 256, 4, 128)
    mv_p = res[0]["mvp"].reshape(1, 1, 256, 4, 128)
    cv_p = last["cvp"].reshape(1, 1, 2, 2 * DFF)
    wk_s = np.concatenate([r["wks"] for r in res], axis=0).reshape(1, 128, 128, 2, 64)
    wv_s = np.concatenate([r["wvs"] for r in res], axis=0).reshape(1, 128, 128, 2, 64)
    sgv_s = np.concatenate([r["sgv"].reshape(SEQS, 8, 512) for r in res], axis=0)[None]
    cv_s = np.concatenate([r["cvs"].reshape(SEQS, 2, 2 * DFF) for r in res], axis=0)[None]
    outs = (y_p, y_s, wk_p, wv_p, mk_p, mv_p, cv_p, wk_s, wv_s, sgv_s, cv_s)
    return tuple(np.ascontiguousarray(o, dtype=np.float32) for o in outs)
```
